# Optimizing a Trainium2 kernel written in Bass

```python
import jax, jax.numpy as jnp
from jax import lax
import numpy as np

D_MODEL = 1024
BATCH = 32
SEQ = 256
DEPTH = 4
DEC_BATCH = 8
DEC_SEQ = 1024
PAST_LEN = 512

GRID_W = 64
D_MIX = D_MODEL
H_A = 4
D_A = D_MIX // 2
DH_A = D_A // H_A
H_B = 4
D_B = D_MIX // 4
DH_B = D_B // H_B
H_C = 4
D_C = D_MIX // 4
DV_C = D_C // H_C
DK_C = DV_C // 2
GLA_RANK = 16
GLA_TAU = 16.0
MLSTM_CHUNK = 64
RET_CHUNK = 64
GLA_CHUNK = 32
ROPE_BASE = 10000.0
EPS = 1e-6
IN_WIDTHS = (D_A, D_A, D_A, D_A, D_A, 4 * H_A, D_B, D_B, D_B, D_B, H_C * DK_C, H_C * DK_C, D_C, D_C, 2 * GLA_RANK)
D_IN = 5 * D_A + 4 * H_A + 4 * D_B + 2 * H_C * DK_C + 2 * D_C + 2 * GLA_RANK

kernel_name = 'hybrid_mlstm_retention_gla_diffusion_step'


def _rmsnorm(x, g):
    x = x.astype(jnp.float32)
    return x * lax.rsqrt(jnp.mean(x * x, axis=-1, keepdims=True) + EPS) * g.astype(jnp.float32)


def _head_rms(y):
    return y * lax.rsqrt(jnp.mean(y * y, axis=-1, keepdims=True) + EPS)


def _chunk(t, c):
    return t.reshape((t.shape[0], t.shape[1] // c, c) + t.shape[2:])


def _grid_rotary(L):
    rows = L // GRID_W
    r = jnp.repeat(jnp.arange(rows, dtype=jnp.float32), GRID_W)
    col = jnp.tile(jnp.arange(GRID_W, dtype=jnp.float32), rows)
    n_f = DH_B // 4
    freqs = ROPE_BASE ** (-jnp.arange(n_f, dtype=jnp.float32) / n_f)
    ang = jnp.concatenate([r[:, None] * freqs, col[:, None] * freqs], axis=-1)
    return jnp.cos(ang)[None, :, None, :], jnp.sin(ang)[None, :, None, :]


def _rotate(t, cos, sin):
    t1, t2 = jnp.split(t, 2, axis=-1)
    return jnp.concatenate([t1 * cos - t2 * sin, t1 * sin + t2 * cos], axis=-1)


def _mlstm(q, k, v, i_pre, f_pre, state0):
    f32 = jnp.float32
    C0, n0, m0 = (s.astype(f32) for s in state0)
    Cs = MLSTM_CHUNK
    qc, kc, vc = _chunk(q, Cs), _chunk(k, Cs), _chunk(v, Cs)
    ic = _chunk(i_pre, Cs)
    F = jnp.cumsum(jax.nn.log_sigmoid(_chunk(f_pre, Cs)), axis=2)
    F_last = F[:, :, -1]
    g = F_last[:, :, None] - F + ic
    gm = jnp.max(g, axis=2)
    wg = jnp.exp(g - gm[:, :, None])
    U_C = jnp.einsum('bnjh,bnjhd,bnjhe->bnhde', wg, kc, vc)
    U_n = jnp.einsum('bnjh,bnjhd->bnhd', wg, kc)

    def step(carry, inp):
        Cp, npv, mp = carry
        fl, gmx, uc, un = inp
        m_new = jnp.maximum(fl + mp, gmx)
        a = jnp.exp(fl + mp - m_new)
        b = jnp.exp(gmx - m_new)
        new = (a[..., None, None] * Cp + b[..., None, None] * uc, a[..., None] * npv + b[..., None] * un, m_new)
        return new, carry

    mv = lambda t: jnp.moveaxis(t, 1, 0)
    (C_last, n_last, m_last), (C_prev, n_prev, m_prev) = lax.scan(
        step, (C0, n0, m0), (mv(F_last), mv(gm), mv(U_C), mv(U_n)))
    C_prev, n_prev, m_prev = mv(C_prev), mv(n_prev), mv(m_prev)
    a_in = F + m_prev[:, :, None]
    causal = jnp.tril(jnp.ones((Cs, Cs), dtype=bool))
    D = F[:, :, :, None] - F[:, :, None, :] + ic[:, :, None, :]
    D = jnp.where(causal[:, :, None], D, -jnp.inf)
    m_i = jnp.maximum(a_in, jnp.max(D, axis=3))
    wD = jnp.exp(D - m_i[:, :, :, None])
    wa = jnp.exp(a_in - m_i)
    qk = jnp.einsum('bnihd,bnjhd->bnijh', qc, kc) * wD
    num = jnp.einsum('bnijh,bnjhe->bnihe', qk, vc) + wa[..., None] * jnp.einsum('bnihd,bnhde->bnihe', qc, C_prev)
    den = jnp.sum(qk, axis=3) + wa * jnp.einsum('bnihd,bnhd->bnih', qc, n_prev)
    h = num / jnp.maximum(jnp.abs(den), jnp.exp(-m_i))[..., None]
    return h.reshape(v.shape), (C_last, n_last, m_last)


def _retention(q, k, v, log_gamma, S0):
    f32 = jnp.float32
    S0 = S0.astype(f32)
    Cs = RET_CHUNK
    qc, kc, vc = _chunk(q, Cs), _chunk(k, Cs), _chunk(v, Cs)
    pos = jnp.arange(Cs, dtype=f32)
    rel = pos[:, None] - pos[None, :]
    decay = jnp.where((rel >= 0)[..., None], jnp.exp(jnp.maximum(rel, 0.0)[..., None] * log_gamma), 0.0)
    scores = jnp.einsum('bnihd,bnjhd->bnhij', qc, kc) * jnp.transpose(decay, (2, 0, 1))
    intra = jnp.einsum('bnhij,bnjhe->bnihe', scores, vc)
    w_end = jnp.exp((Cs - 1 - pos)[:, None] * log_gamma)
    U = jnp.einsum('bnjhd,jh,bnjhe->bnhde', kc, w_end, vc)
    g_chunk = jnp.exp(Cs * log_gamma)[:, None, None]

    def step(S, U_n):
        return g_chunk * S + U_n, S

    S_last, S_prev = lax.scan(step, S0, jnp.moveaxis(U, 1, 0))
    S_prev = jnp.moveaxis(S_prev, 0, 1)
    w_q = jnp.exp((pos + 1.0)[:, None] * log_gamma)
    inter = jnp.einsum('bnihd,ih,bnhde->bnihe', qc, w_q, S_prev)
    return (intra + inter).reshape(v.shape), S_last


def _gla(q, k, v, log_a, S0):
    f32 = jnp.float32
    S0 = S0.astype(f32)
    Cs = GLA_CHUNK
    qc, kc, vc = _chunk(q, Cs), _chunk(k, Cs), _chunk(v, Cs)
    b = jnp.cumsum(_chunk(log_a, Cs), axis=2)
    causal = jnp.tril(jnp.ones((Cs, Cs), dtype=bool))
    diff = b[:, :, :, None] - b[:, :, None, :]
    diff = jnp.where(causal[None, None, :, :, None, None], diff, -jnp.inf)
    scores = jnp.einsum('bnihd,bnjhd,bnijhd->bnhij', qc, kc, jnp.exp(diff))
    intra = jnp.einsum('bnhij,bnjhe->bnihe', scores, vc)
    b_last = b[:, :, -1]
    U = jnp.einsum('bnjhd,bnjhe->bnhde', kc * jnp.exp(b_last[:, :, None] - b), vc)

    def step(S, inp):
        gl, U_n = inp
        return jnp.exp(gl)[..., None] * S + U_n, S

    S_last, S_prev = lax.scan(step, S0, (jnp.moveaxis(b_last, 1, 0), jnp.moveaxis(U, 1, 0)))
    S_prev = jnp.moveaxis(S_prev, 0, 1)
    inter = jnp.einsum('bnihd,bnhde->bnihe', qc * jnp.exp(b), S_prev)
    return (intra + inter).reshape(v.shape), S_last


def _mixer(h, st_f, st_b, rot, w_in, gate_b, ret_logit, gla_w2, gla_b2, hn_g, w_out):
    f32 = jnp.float32
    B, L, _ = h.shape
    proj = jnp.einsum('bld,de->ble', h, w_in.astype(f32))
    (aq, ak, av, ao, az, ag, bq, bk, bv, bz, cq, ck, cv, cz, clr) = jnp.split(
        proj, np.cumsum(IN_WIDTHS)[:-1].tolist(), axis=-1)
    flip = lambda t: jnp.flip(t, axis=1)

    aq = aq.reshape(B, L, H_A, DH_A)
    ak = ak.reshape(B, L, H_A, DH_A) * DH_A ** -0.5
    av = av.reshape(B, L, H_A, DH_A)
    ag = (ag + gate_b.astype(f32)).reshape(B, L, 4, H_A)
    ha_f, sa_f = _mlstm(aq, ak, av, ag[:, :, 0], ag[:, :, 1], st_f[0:3])
    ha_b, sa_b = _mlstm(flip(aq), flip(ak), flip(av), flip(ag[:, :, 2]), flip(ag[:, :, 3]), st_b[0:3])
    ya = jax.nn.sigmoid(ao).reshape(B, L, H_A, DH_A) * (ha_f + flip(ha_b))

    bq = bq.reshape(B, L, H_B, DH_B)
    bk = bk.reshape(B, L, H_B, DH_B)
    bv = bv.reshape(B, L, H_B, DH_B)
    if rot is not None:
        bq = _rotate(bq, rot[0], rot[1])
        bk = _rotate(bk, rot[0], rot[1])
    bk = bk * DH_B ** -0.5
    log_gamma = jax.nn.log_sigmoid(ret_logit.astype(f32))
    hb_f, sb_f = _retention(bq, bk, bv, log_gamma[0], st_f[3])
    hb_b, sb_b = _retention(flip(bq), flip(bk), flip(bv), log_gamma[1], st_b[3])
    yb = hb_f + flip(hb_b)

    cq = cq.reshape(B, L, H_C, DK_C)
    ck = ck.reshape(B, L, H_C, DK_C) * DK_C ** -0.5
    cv = cv.reshape(B, L, H_C, DV_C)
    clr = clr.reshape(B, L, 2, GLA_RANK)
    log_a = jax.nn.log_sigmoid(jnp.einsum('bldr,drk->bldk', clr, gla_w2.astype(f32)) + gla_b2.astype(f32)) / GLA_TAU
    log_a = log_a.reshape(B, L, 2, H_C, DK_C)
    hc_f, sc_f = _gla(cq, ck, cv, log_a[:, :, 0], st_f[4])
    hc_b, sc_b = _gla(flip(cq), flip(ck), flip(cv), flip(log_a[:, :, 1]), st_b[4])
    yc = hc_f + flip(hc_b)

    y = jnp.concatenate([_head_rms(ya).reshape(B, L, D_A), _head_rms(yb).reshape(B, L, D_B),
                         _head_rms(yc).reshape(B, L, D_C)], axis=-1)
    y = y * hn_g.astype(f32) * jax.nn.silu(jnp.concatenate([az, bz, cz], axis=-1))
    out = jnp.einsum('ble,ed->bld', y, w_out.astype(f32))
    return out, (sa_f[0], sa_f[1], sa_f[2], sb_f, sc_f), (sa_b[0], sa_b[1], sa_b[2], sb_b, sc_b)


def _layer(x, mod, st_f, st_b, rot, g, w_in, gate_b, ret_logit, gla_w2, gla_b2, hn_g, w_out):
    shift, scale, gate = jnp.split(mod[:, None, :], 3, axis=-1)
    h = _rmsnorm(x, g) * (1.0 + scale) + shift
    out, sf, sb = _mixer(h, st_f, st_b, rot, w_in, gate_b, ret_logit, gla_w2, gla_b2, hn_g, w_out)
    return (x.astype(jnp.float32) + gate * out).astype(x.dtype), sf, sb


def _modulation(cvec, w, b):
    return jnp.einsum('bd,de->be', jax.nn.silu(cvec.astype(jnp.float32)), w.astype(jnp.float32)) + b.astype(jnp.float32)


def _zero_state(B):
    z = lambda *s: jnp.zeros((B,) + s, jnp.float32)
    return (z(H_A, DH_A, DH_A), z(H_A, DH_A), z(H_A), z(H_B, DH_B, DH_B), z(H_C, DK_C, DV_C))


def setup_inputs(seed: int = 0) -> dict:
    key = jax.random.key(seed)
    ks = jax.random.split(key, 24)
    f32 = jnp.float32
    nrm = lambda k, shape, s: s * jax.random.normal(k, shape, f32)
    x_prompt = nrm(ks[0], (BATCH, SEQ, D_MODEL), 1.0)
    x_sample = nrm(ks[1], (DEC_BATCH, DEC_SEQ, D_MODEL), 1.0)
    state_mlstm_C = nrm(ks[2], (DEC_BATCH, DEPTH, 2, H_A, DH_A, DH_A), 0.5)
    state_mlstm_n = nrm(ks[3], (DEC_BATCH, DEPTH, 2, H_A, DH_A), 0.5)
    state_mlstm_m = nrm(ks[4], (DEC_BATCH, DEPTH, 2, H_A), 1.0)
    state_ret = nrm(ks[5], (DEC_BATCH, DEPTH, 2, H_B, DH_B, DH_B), 1.0)
    state_gla = nrm(ks[6], (DEC_BATCH, DEPTH, 2, H_C, DK_C, DV_C), 1.0)
    c = nrm(ks[7], (DEC_BATCH, D_MODEL), 1.0)
    c_ctx = nrm(ks[8], (D_MODEL,), 1.0)
    norm_g = 1.0 + nrm(ks[9], (DEPTH, D_MODEL), 0.02)
    w_ada = nrm(ks[10], (DEPTH, D_MODEL, 3 * D_MODEL), 0.5 * D_MODEL ** -0.5)
    b_ada = nrm(ks[11], (DEPTH, 3 * D_MODEL), 0.02)
    w_in = nrm(ks[12], (DEPTH, D_MODEL, D_IN), D_MODEL ** -0.5)
    i_bias = nrm(ks[13], (DEPTH, 2, 1, H_A), 0.1)
    f_bias = jnp.linspace(3.0, 6.0, H_A, dtype=f32) + nrm(ks[14], (DEPTH, 2, 1, H_A), 0.1)
    mlstm_gate_b = jnp.concatenate([i_bias, f_bias], axis=2).reshape(DEPTH, 4 * H_A)
    e = 5.0 + jnp.arange(H_B, dtype=f32)
    ret_decay_logit = jnp.log(2.0 ** e - 1.0) + nrm(ks[15], (DEPTH, 2, H_B), 0.05)
    gla_w2 = nrm(ks[16], (DEPTH, 2, GLA_RANK, H_C * DK_C), GLA_RANK ** -0.5)
    gla_b2 = 1.0 + nrm(ks[17], (DEPTH, 2, H_C * DK_C), 0.1)
    headnorm_g = 1.0 + nrm(ks[18], (DEPTH, D_MIX), 0.02)
    w_out = nrm(ks[19], (DEPTH, D_MIX, D_MODEL), D_MIX ** -0.5)
    final_g = 1.0 + nrm(ks[20], (D_MODEL,), 0.02)
    return {'x_prompt': x_prompt, 'x_sample': x_sample, 'state_mlstm_C': state_mlstm_C,
            'state_mlstm_n': state_mlstm_n, 'state_mlstm_m': state_mlstm_m, 'state_ret': state_ret,
            'state_gla': state_gla, 'c': c, 'c_ctx': c_ctx, 'norm_g': norm_g, 'w_ada': w_ada, 'b_ada': b_ada,
            'w_in': w_in, 'mlstm_gate_b': mlstm_gate_b, 'ret_decay_logit': ret_decay_logit, 'gla_w2': gla_w2,
            'gla_b2': gla_b2, 'headnorm_g': headnorm_g, 'w_out': w_out, 'final_g': final_g}


def reference(x_prompt, x_sample, state_mlstm_C, state_mlstm_n, state_mlstm_m, state_ret, state_gla, c, c_ctx,
              norm_g, w_ada, b_ada, w_in, mlstm_gate_b, ret_decay_logit, gla_w2, gla_b2, headnorm_g, w_out, final_g):
    Bp = x_prompt.shape[0]
    Ls = x_sample.shape[1]
    rot = _grid_rotary(Ls)
    caches = (state_mlstm_C, state_mlstm_n, state_mlstm_m, state_ret, state_gla)
    xp, xs = x_prompt, x_sample
    ctx_states = []
    for l in range(DEPTH):
        lp = (norm_g[l], w_in[l], mlstm_gate_b[l], ret_decay_logit[l], gla_w2[l], gla_b2[l], headnorm_g[l], w_out[l])
        mod_ctx = _modulation(c_ctx[None, :], w_ada[l], b_ada[l])
        mod_lat = _modulation(c, w_ada[l], b_ada[l])
        xp, sf, sb = _layer(xp, mod_ctx, _zero_state(Bp), _zero_state(Bp), None, *lp)
        ctx_states.append([jnp.stack([a, b], axis=1) for a, b in zip(sf, sb)])
        cache_f = [s[:, l, 0] for s in caches]
        cache_b = [s[:, l, 1] for s in caches]
        xs, _, _ = _layer(xs, mod_lat, cache_f, cache_b, rot, *lp)
    new_mlstm_C = jnp.stack([s[0] for s in ctx_states], axis=1)
    new_mlstm_n = jnp.stack([s[1] for s in ctx_states], axis=1)
    new_mlstm_m = jnp.stack([s[2] for s in ctx_states], axis=1)
    new_ret = jnp.stack([s[3] for s in ctx_states], axis=1)
    new_gla = jnp.stack([s[4] for s in ctx_states], axis=1)
    y_prompt = _rmsnorm(xp, final_g).astype(x_prompt.dtype)
    y_sample = _rmsnorm(xs, final_g).astype(x_sample.dtype)
    return (y_prompt, y_sample, new_mlstm_C, new_mlstm_n, new_mlstm_m, new_ret, new_gla)
```

```python
import numpy as np
from contextlib import ExitStack
import concourse.bass as bass
import concourse.mybir as mybir
from concourse.bass_utils import run_bass_kernel_spmd

F32 = mybir.dt.float32
BF16 = mybir.dt.bfloat16
AF = mybir.ActivationFunctionType
ALU = mybir.AluOpType
AX = mybir.AxisListType

DEPTH = 4
D = 1024
DIN = 4400
EPS = 1e-6
NCORES = 8
OQ, OK_, OV, OO, OZ, OG = 0, 512, 1024, 1536, 2048, 2560
BQ, BK, BV, BZ = 2576, 2832, 3088, 3344
CQ, CK, CV, CZ, CL = 3600, 3728, 3856, 4112, 4368

C_IDF, C_UINC, C_LINC, C_RIJ, C_RJI, C_POSF, C_POSB, C_ONES = [i * 128 for i in range(8)]
C_COLA, C_COLB = 1024, 1025
C_COS, C_SIN = 1026, 1026 + 256
C_BD2, C_BD4, C_MC2, C_MC4 = 1538, 1538 + 128, 1538 + 384, 1538 + 386
NCST = 1538 + 390


def make_consts():
    c = np.zeros((128, NCST), np.float32)
    p = np.arange(128)[:, None].astype(np.float32)
    f = np.arange(128)[None, :].astype(np.float32)
    c[:, C_IDF:C_IDF + 128] = (p == f)
    c[:, C_UINC:C_UINC + 128] = (p <= f)
    c[:, C_LINC:C_LINC + 128] = (p >= f)
    c[:, C_RIJ:C_RIJ + 128] = np.maximum(f - p, 0)
    c[:, C_RJI:C_RJI + 128] = np.maximum(p - f, 0)
    c[:, C_POSF:C_POSF + 128] = f + 1
    c[:, C_POSB:C_POSB + 128] = 128 - f
    c[:, C_ONES:C_ONES + 128] = 1.0
    c[:, C_COLA] = -(127 - p[:, 0])
    c[:, C_COLB] = -p[:, 0]
    L = 1024
    tok = np.arange(L)
    r = (tok // 64).astype(np.float32)
    col = (tok % 64).astype(np.float32)
    nf = 16
    freqs = (np.float32(10000.0) ** (-np.arange(nf, dtype=np.float32) / np.float32(nf))).astype(np.float32)
    ang = np.concatenate([r[:, None] * freqs, col[:, None] * freqs], axis=-1).astype(np.float32)
    cos = np.cos(ang).astype(np.float32).reshape(8, 128, 32).transpose(1, 0, 2).reshape(128, 256)
    sin = np.sin(ang).astype(np.float32).reshape(8, 128, 32).transpose(1, 0, 2).reshape(128, 256)
    pp = np.arange(128)[:, None]
    c[:, C_BD2:C_BD2 + 128] = (pp // 64 == (np.arange(128)[None, :] // 64))
    c[:, C_BD4:C_BD4 + 256] = (pp // 32 == (np.arange(256)[None, :] // 64))
    c[:, C_MC2:C_MC2 + 2] = (pp // 64 == np.arange(2)[None, :])
    c[:, C_MC4:C_MC4 + 4] = (pp // 32 == np.arange(4)[None, :])
    c[:, C_COS:C_COS + 256] = cos
    c[:, C_SIN:C_SIN + 256] = sin
    return c


import os as _os
ELIDE = int(_os.environ.get("MK_ELIDE", "5"))


class Buf:
    __slots__ = ("name", "w", "r", "excl")

    def __init__(self, name, excl=False):
        self.name = name
        self.w = None
        self.r = []
        self.excl = excl


class Sync:
    def __init__(self, nc, ctx):
        self.nc = nc
        self.eng = {'pe': nc.tensor, 'dve': nc.vector, 'act': nc.scalar, 'pool': nc.gpsimd, 'sp': nc.sync}
        self.sem, self.cnt, self.clk, self.hist, self.closed = {}, {}, {}, {}, {}
        for k in self.eng:
            self.sem[k] = ctx.enter_context(nc.semaphore("s_" + k))
            self.cnt[k] = 0
            self.clk[k] = {}
        self.ctx = ctx
        self.nwait = 0
        self.nins = 0

    def dma_sem(self, name):
        k = 'dma_' + name
        self.sem[k] = self.ctx.enter_context(self.nc.semaphore("s_" + k))
        self.cnt[k] = 0
        self.closed[k] = False
        return k

    def _deps(self, e, reads, writes):
        deps = {}
        if ELIDE in (3, 4, 5, 6):
            return self._deps2(e, reads, writes)
        if ELIDE == 0:
            e = '?'
        elif ELIDE == 1 and e != 'pe':
            e = '?'
        for b in reads:
            if b.w is not None:
                s, v = b.w
                if not (s == e and e == 'pe') and deps.get(s, 0) < v:
                    deps[s] = v
            if b.excl:
                for (s, v) in b.r:
                    if s != e and deps.get(s, 0) < v:
                        deps[s] = v
        for b in writes:
            if b.w is not None:
                s, v = b.w
                if s != e and deps.get(s, 0) < v:
                    deps[s] = v
            for (s, v) in b.r:
                if s != e and deps.get(s, 0) < v:
                    deps[s] = v
        return deps

    def _deps2(self, e, reads, writes):
        deps = {}
        cur = self.cnt.get(e, 0)

        def add(s, v, kind):
            if s == e:
                if v > cur:
                    assert e == 'pe', (e, s, v, cur)
                    return
                if e == 'pe' and ELIDE in (5, 6):
                    return
                if e != 'pe' and ELIDE in (4, 5) and kind != 'raw':
                    return
            if deps.get(s, 0) < v:
                deps[s] = v
        for b in reads:
            if b.w is not None:
                add(b.w[0], b.w[1], 'raw')
            if b.excl:
                for (s, v) in b.r:
                    if s != e:
                        add(s, v, 'rr')
        for b in writes:
            if b.w is not None:
                add(b.w[0], b.w[1], 'waw')
            for (s, v) in b.r:
                add(s, v, 'war')
        return deps

    def _wait1(self, e, s, v):
        clk = self.clk[e]
        if s.startswith('dma_'):
            v = self.cnt[s]
            self.closed[s] = True
        if clk.get(s, 0) >= v:
            return
        assert v <= self.cnt[s], ("wait on not-yet-signalled instruction", e, s, v, self.cnt[s])
        self.eng[e].wait_ge(self.sem[s], v)
        self.nwait += 1
        h = self.hist.get((s, v))
        if h:
            for a, b in h.items():
                if clk.get(a, 0) < b:
                    clk[a] = b
        clk[s] = v

    def _wait(self, e, deps):
        for s, v in deps.items():
            self._wait1(e, s, v)

    @staticmethod
    def _flat(xs):
        out = []
        for x in xs:
            if isinstance(x, (list, tuple)):
                out.extend(Sync._flat(x))
            else:
                out.append(x)
        return out

    def op(self, e, fn, reads=(), writes=(), inc=True):
        reads, writes = self._flat(reads), self._flat(writes)
        self._wait(e, self._deps(e, reads, writes))
        ins = fn(self.eng[e])
        self.nins += 1
        if not inc:
            v = self.cnt[e] + 1
            for b in reads:
                b.r.append((e, v))
            for b in writes:
                b.w = (e, v)
                b.r = []
            return ins
        self.cnt[e] += 1
        v = self.cnt[e]
        ins.then_inc(self.sem[e], 1)
        snap = dict(self.clk[e])
        snap[e] = v
        self.hist[(e, v)] = snap
        for b in reads:
            b.r.append((e, v))
        for b in writes:
            b.w = (e, v)
            b.r = []
        return ins

    def dma(self, e, dsem, out, in_, reads=(), writes=(), slow=False):
        reads, writes = self._flat(reads), self._flat(writes)
        self._wait(e, self._deps(e, reads, writes))
        if self.closed[dsem] and self.clk[e].get(dsem, 0) < self.cnt[dsem]:
            self._wait1(e, dsem, self.cnt[dsem])
        self.closed[dsem] = False
        if slow:
            ins = self.eng[e].dma_start(out=out, in_=in_, allow_slow_non_contiguous=True)
        else:
            ins = self.eng[e].dma_start(out=out, in_=in_)
        self.cnt[dsem] += 16
        v = self.cnt[dsem]
        ins.then_inc(self.sem[dsem], 16)
        self.nins += 1
        self.hist[(dsem, v)] = dict(self.clk[e])
        for b in reads:
            b.r.append((dsem, v))
        for b in writes:
            b.w = (dsem, v)
            b.r = []
        return ins

    def final_wait(self, e):
        for s in self.sem:
            if s.startswith('dma_') and self.cnt[s] > 0 and self.clk[e].get(s, 0) < self.cnt[s]:
                self.eng[e].wait_ge(self.sem[s], self.cnt[s])


import os
STAGE = int(os.environ.get("MK_STAGE", "99"))
SUB = int(os.environ.get("MK_SUB", "99"))
MKX = int(os.environ.get("MK_X", "0"))
ELIDE = int(os.environ.get("MK_ELIDE", "5"))


PHASES = []


def build(depth=DEPTH):
    nc = bass.Bass("TRN2", target_bir_lowering=False)
    din = lambda n, s: nc.dram_tensor(n, list(s), F32, kind="ExternalInput").ap()
    dout = lambda n, s: nc.dram_tensor(n, list(s), F32, kind="ExternalOutput").ap()
    x_d = din("x", (2048, D))
    cvec_d = din("cvec", (2, D))
    sC_d = din("sC", (DEPTH, 2, 4, 128, 128))
    sn_d = din("sn", (DEPTH, 2, 4, 128))
    sm_d = din("sm", (DEPTH * 8,))
    sret_d = din("sret", (DEPTH, 2, 4, 64, 64))
    sgla_d = din("sgla", (DEPTH, 2, 4, 32, 64))
    normg_d = din("norm_g", (DEPTH, D))
    wada_d = din("w_ada", (DEPTH, D, 3 * D))
    bada_d = din("b_ada", (DEPTH, 3 * D))
    win_d = din("w_in", (DEPTH, D, DIN))
    gateb_d = din("gate_b", (DEPTH * 16,))
    retl_d = din("ret_logit", (DEPTH * 8,))
    w2_d = din("gla_w2", (DEPTH, 2, 16, 128))
    b2_d = din("gla_b2", (DEPTH, 256))
    hng_d = din("hn_g", (DEPTH, D))
    wout_d = din("w_out", (DEPTH, D, D))
    fing_d = din("final_g", (D,))
    cst_d = din("cst", (128, NCST))
    y_d = dout("y", (2048, D))
    oC_d = dout("oC", (4, DEPTH, 2, 4, 128, 128))
    on_d = dout("on", (128, 128))
    om_d = dout("om", (1, 128))
    oret_d = dout("oret", (4, DEPTH, 2, 4, 64, 64))
    ogla_d = dout("ogla", (4, DEPTH, 2, 4, 32, 64))

    with ExitStack() as ctx:
        S = Sync(nc, ctx)

        def T(name, shape, dt=F32):
            return ctx.enter_context(nc.sbuf_tensor("sb_" + name, list(shape), dt)), Buf(name)

        def PS(name, shape, dt=F32):
            return ctx.enter_context(nc.psum_tensor("ps_" + name, list(shape), dt))

        x_sb, _ = T("x_sb", (128, 16, D))
        bx = [Buf("x%d" % i) for i in range(16)]
        hT, _ = T("hT", (128, 8, 1024), BF16)
        b_hT = [Buf("hT0"), Buf("hT1")]
        yT, _ = T("yT", (128, 8, 1024), BF16)
        b_yT = [Buf("yT%d" % i) for i in range(8)]
        NSLOT = 4
        Wr = []
        for s in range(NSLOT):
            t, b = T("wr%d" % s, (128, 8, 528), BF16)
            Wr.append((t, b, S.dma_sem("wr%d" % s)))
        ada = []
        for s in range(2):
            t, b = T("ada%d" % s, (128, 8, 128), BF16)
            ada.append((t, b, S.dma_sem("ada%d" % s)))
        cst, b_cst = T("cst", (128, NCST))
        idb, b_idb = T("idb", (128, 128), BF16)
        uincb, b_uincb = T("uincb", (128, 128), BF16)
        lincb, b_lincb = T("lincb", (128, 128), BF16)
        gate_bc, b_gate = T("gate_bc", (128, 2, D))
        bg_bc, b_bg = T("bg_bc", (128, D))
        ngT, b_ngT = T("ngT", (128, DEPTH, 8))
        hngT, b_hngT = T("hngT", (128, DEPTH, 8))
        bshT, b_bshT = T("bshT", (128, DEPTH, 8))
        bscT, b_bscT = T("bscT", (128, DEPTH, 8))
        cT, b_cT = T("cT", (128, 8, 2))
        scT, b_scT = T("scT", (128, 8, 2), BF16)
        screp, b_screp = T("screp", (128, 2, 8, 128), BF16)
        gsT2, _ = T("gsT", (128, 2, 8, 2))
        shT2, _ = T("shT", (128, 2, 8, 2))
        b_gsTp = [Buf("gsT0"), Buf("gsT1")]
        b_shTp = [Buf("shT0"), Buf("shT1")]
        modraw, b_modraw = T("modraw", (128, 16, 2))
        gb_bc, b_gb = T("gb_bc", (128, DEPTH * 16))
        m0_bc, b_m0 = T("m0_bc", (128, DEPTH * 8))
        rl_bc, b_rl = T("rl_bc", (128, DEPTH * 8))
        ss, b_ss = T("ss", (128, 16))
        rstd, b_rstd = T("rstd", (128, 16))
        Gs, b_Gs = T("Gs", (128, 8, 16))
        L1g, b_L1g = T("L1g", (128, 2, 8, 4))
        Fp, b_Fp = T("Fp", (128, 8, 16))
        ug, b_ug = T("ug", (128, 8, 8))
        umax, b_umax = T("umax", (64, 1))
        Abc, b_Abc = T("Abc", (128, 8, 8))
        MP, b_MP = T("MP", (128, 8, 8))
        Sg, b_Sg = T("Sg", (128, 8, 8))
        Mfin, b_Mfin = T("Mfin", (128, 8))
        rg, b_rg = T("rg", (128, 8, 8))
        wg, b_wg = T("wg", (128, 8, 8))
        flo, b_flo = T("flo", (128, 8, 8))
        tmpg, b_tmpg = T("tmpg", (128, 8, 8))
        Mst, b_Mst = T("Mst", (128, 128))
        Nst, b_Nst = T("Nst", (128, 128))
        ktok, _ = T("ktok", (128, 8, 256), BF16)
        b_ktok = [Buf("ktok%d" % i) for i in range(8)]
        vt_f, _ = T("vt_f", (128, 8, 2, 130), BF16)
        b_vtf = [Buf("vtf%d" % i) for i in range(8)]
        b_vtb = [Buf("vtb%d" % i) for i in range(8)]
        b_vtok = [Buf("vtok%d" % i) for i in range(8)]
        vt_b, _ = T("vt_b", (128, 8, 2, 130), BF16)
        vtok, _ = T("vtok", (128, 8, 256), BF16)
        Sb16, b_Sb16 = T("Sb16", (128, 8, 2, 130), BF16)
        Cst = []
        for i in range(4):
            t, b = T("Cst%d" % i, (128, 260))
            Cst.append((t, b, S.dma_sem("cst%d" % i)))
        C16, b_C16 = T("C16", (128, 2, 260), BF16)
        qtok, b_qtok = T("qtok", (128, 256), BF16)
        Esb, _ = T("Esb", (128, 2, 512))
        b_Esb = [[Buf("Esb0lo"), Buf("Esb0hi")], [Buf("Esb1lo"), Buf("Esb1hi")]]
        zsb, _ = T("zsb", (128, 2, 256))
        b_zsb = [Buf("zsb0"), Buf("zsb1")]
        qkT, b_qkT = T("qkT", (128, 4, 128), BF16)
        qhT, b_qhT = T("qhT", (128, 2, 2, 128), BF16)
        Pf, b_Pf = T("Pf", (128, 4, 128), BF16)
        Pb, b_Pb = T("Pb", (128, 4, 128), BF16)
        dn, _ = T("dn", (128, 8))
        b_dn = [Buf("dn_f"), Buf("dn_b")]
        ya, _ = T("ya", (128, 256))
        b_ya = [Buf("ya0"), Buf("ya1")]
        ssy, _ = T("ssy", (128, 8))
        b_ssy = [Buf("ssy%d" % i) for i in range(8)]
        ybf, b_ybf = T("ybf", (128, 256), BF16)
        rot1, b_rot1 = T("rot1", (128, 4, 32))
        rot2, b_rot2 = T("rot2", (128, 4, 32))
        L1r, b_L1r = T("L1r", (128, 8))
        nL1r, b_nL1r = T("nL1r", (128, 8))
        wend, b_wend = T("wend", (128, 2, 4))
        gC, b_gC = T("gC", (128, 8))
        gblk, b_gblk = T("gblk", (128, 2, 2))
        nLblk, b_nLblk = T("nLblk", (128, 2, 2))
        Wq, b_Wq = T("Wq", (128, 2, 2, 128))
        Mret, b_Mret = T("Mret", (128, 4, 128))
        w2s, b_w2s = T("w2s", (33, 256))
        w2x, b_w2x = T("w2x", (33, 256), BF16)
        clrT, b_clrT = T("clrT", (33, 128), BF16)
        Gc, b_Gc = T("Gc", (128, 8, 2))
        xn_parts = [(ktok[:].rearrange("p t c -> p (t c)").rearrange("p (a f) -> p a f", a=2), b_ktok),
                    (vtok[:].rearrange("p t c -> p (t c)").rearrange("p (a f) -> p a f", a=2), b_vtok)]
        EBd = [(vt_f[:].rearrange("p t j n -> p (t j n)").bitcast(F32)[:, 0:1024].rearrange("p (t n) -> p t n", t=8), b_vtf),
               (vt_b[:].rearrange("p t j n -> p (t j n)").bitcast(F32)[:, 0:1024].rearrange("p (t n) -> p t n", t=8), b_vtb)]
        junk, b_junk = ybf, b_ybf
        udiag, b_udiag = ya[0:64, 0:64], b_ya[0]
        tmpS, b_tmpS = zsb[:, 0, :], b_zsb[0]
        tm1, b_tm1 = ya[:, 0:128], b_ya[0]
        tm2, b_tm2 = rot1[:].rearrange("p a b -> p (a b)"), b_rot1
        EBn, b_EBn = ya, b_ya
        L1c, b_L1c = Esb[:, 0, 0:256], b_Esb[0][0]
        fin_bc, b_fin = bg_bc, b_bg
        d_cst = S.dma_sem("cst")
        d_x = S.dma_sem("x")
        d_misc = S.dma_sem("misc")
        d_out = S.dma_sem("out")
        d_st = S.dma_sem("stout")

        PA = PS("PA", (128, 512)); b_PA = Buf("PA", True)
        PB = PS("PB", (128, 512)); b_PB = Buf("PB", True)
        PT0 = PS("PT0", (128, 1024), BF16); b_PT0 = Buf("PT0", True)
        PT1 = PS("PT1", (128, 1024), BF16); b_PT1 = Buf("PT1", True)
        PSc = PS("PSc", (128, 512)); b_PSc = Buf("PSc", True)
        PO1 = PS("PO1", (128, 512)); b_PO1 = Buf("PO1", True)
        PO2 = PS("PO2", (128, 512)); b_PO2 = Buf("PO2", True)
        PX = PS("PX", (128, 512)); b_PU = Buf("PX", True); b_PM = b_PU
        PU = PX[:, 0:260]
        PM = PX[:, 260:512]
        PTs = [(PT0, b_PT0), (PT1, b_PT1)]
        pt_i = [0]

        def nextPT():
            pt_i[0] ^= 1
            return PTs[pt_i[0]]

        ctx.enter_context(nc.Block())

        def ACT(fn, R, W): return S.op('act', fn, R, W)
        def DVE(fn, R, W): return S.op('dve', fn, R, W)
        def PE(fn, R, W, inc=True): return S.op('pe', fn, R, W, inc)

        def mm(out, lhsT, rhs, start, stop, R, W, tp=None, inc=True):
            if tp is not None:
                return PE(lambda e: e.matmul(out, lhsT=lhsT, rhs=rhs, start=start, stop=stop, tile_position=tp), R, W, inc)
            return PE(lambda e: e.matmul(out, lhsT=lhsT, rhs=rhs, start=start, stop=stop), R, W, inc)

        def tr(out, in_, ident, R, W, inc=True):
            return PE(lambda e: e.transpose(out, in_, ident), R, W, inc)

        def act(out, in_, func, R, W, scale=1.0, bias=0.0, accum=None):
            if accum is None:
                return ACT(lambda e: e.activation(out=out, in_=in_, func=func, scale=scale, bias=bias), R, W)
            return ACT(lambda e: e.activation(out=out, in_=in_, func=func, scale=scale, bias=bias, accum_out=accum), R, W)

        def tt(out, in0, in1, op, R, W):
            return DVE(lambda e: e.tensor_tensor(out=out, in0=in0, in1=in1, op=op), R, W)

        def ts(out, in0, s1, s2, op0, R, W, op1=None):
            if op1 is None:
                return DVE(lambda e: e.tensor_scalar(out, in0, s1, None, op0=op0), R, W)
            return DVE(lambda e: e.tensor_scalar(out, in0, s1, s2, op0=op0, op1=op1), R, W)

        def stt(out, in0, scalar, in1, op0, op1, R, W):
            return DVE(lambda e: e.scalar_tensor_tensor(out=out, in0=in0, scalar=scalar, in1=in1, op0=op0, op1=op1), R, W)

        def cp(out, in_, R, W):
            return DVE(lambda e: e.tensor_copy(out, in_), R, W)

        def rsq(out, in_, n, R, W, tmp):
            act(tmp, in_, AF.Ln, R, [W[0]], scale=1.0 / n, bias=EPS)
            act(out, tmp, AF.Exp, [W[0]], W, scale=-0.5)

        cI = lambda c0, n=128: cst[:, c0:c0 + n]
        idf = cI(C_IDF)
        ones = cI(C_ONES)

        S.dma('sp', d_cst, cst[:], cst_d, writes=[b_cst])
        for g in range(4):
            S.dma('sp', d_x, x_sb[:, g * 4:(g + 1) * 4, :],
                  x_d[g * 512:(g + 1) * 512, :].rearrange("(t p) f -> p t f", p=128), writes=bx[g * 4:(g + 1) * 4])
        S.dma('sp', d_misc, gb_bc[:], gateb_d.partition_broadcast(128), writes=[b_gb])
        S.dma('sp', d_misc, m0_bc[:], sm_d.partition_broadcast(128), writes=[b_m0])
        S.dma('sp', d_misc, rl_bc[:], retl_d.partition_broadcast(128), writes=[b_rl])
        for v in range(2):
            S.dma('sp', d_misc, cT[:, :, v], cvec_d[v].rearrange("(kc p) -> p kc", p=128), writes=[b_cT], slow=True)
        for (t_, b_, src, off) in ((ngT, b_ngT, normg_d, 0), (hngT, b_hngT, hng_d, 0),
                                   (bshT, b_bshT, bada_d, 0), (bscT, b_bscT, bada_d, D)):
            for l in range(DEPTH):
                S.dma('sp', d_misc, t_[:, l, :], src[l, off:off + D].rearrange("(kc p) -> p kc", p=128), writes=[b_], slow=True)
        cp(idb[:], idf, [b_cst], [b_idb])
        cp(uincb[:], cI(C_UINC), [b_cst], [b_uincb])
        cp(lincb[:], cI(C_LINC), [b_cst], [b_lincb])
        DVE(lambda e: e.memset(Mst[:], 0.0), [], [b_Mst])
        DVE(lambda e: e.memset(Nst[:], 0.0), [], [b_Nst])
        DVE(lambda e: e.memset(w2s[:], 0.0), [], [b_w2s])
        DVE(lambda e: e.memset(clrT[:], 1.0), [], [b_clrT])
        for i in range(4):
            DVE(lambda e: e.memset(Cst[i][0][:], 0.0), [], [Cst[i][1]])
        act(tmpS[:, 0:16], cT[:].rearrange("p k v -> p (k v)"), AF.Exp, [b_cT], [b_tmpS], scale=-1.0)
        ts(tmpS[:, 0:16], tmpS[:, 0:16], 1.0, None, ALU.add, [b_tmpS], [b_tmpS])
        DVE(lambda e: e.reciprocal(tmpS[:, 0:16], tmpS[:, 0:16]), [b_tmpS], [b_tmpS])
        tt(scT[:].rearrange("p k v -> p (k v)"), tmpS[:, 0:16], cT[:].rearrange("p k v -> p (k v)"), ALU.mult, [b_tmpS, b_cT], [b_scT])
        for v in range(2):
            cp(screp[:, v, :, :], scT[:, :, v:v + 1].to_broadcast([128, 8, 128]), [b_scT], [b_screp])

        ring_state = {'n': 0}

        def load_block(l, pieces, wsrc):
            s = ring_state['n'] % NSLOT
            ring_state['n'] += 1
            wt, wb_, ws = Wr[s]
            c = 0
            for (c0, n) in pieces:
                S.dma('pool', ws, wt[:, :, c:c + n], wsrc[l, :, c0:c0 + n].rearrange("(kc p) n -> p kc n", p=128), writes=[wb_])
                c += n
            return wt, wb_

        def proj(ps, psb, tloc, wt, wb_, c0, n, hb):
            toks = slice(tloc * 128, (tloc + 1) * 128)
            for kc in range(8):
                mm(ps, hT[:, kc, toks], wt[:, kc, c0:c0 + n], kc == 0, kc == 7, [hb, wb_], [psb], inc=(kc == 7))

        ada_state = {'n': 0}

        ada_q = []

        def ada_prefetch(l, blk):
            sl = ada_state['n'] % 2
            ada_state['n'] += 1
            at, ab, asem = ada[sl]
            S.dma('pool', asem, at[:], wada_d[l, :, blk * 128:(blk + 1) * 128].rearrange("(kc p) n -> p kc n", p=128), writes=[ab])
            ada_q.append((l, blk, at, ab))

        def ada_consume(l, blk):
            l_, blk_, at, ab = ada_q.pop(0)
            assert (l_, blk_) == (l, blk), (l_, blk_, l, blk)
            if blk < 16:
                for kc in range(8):
                    mm(PM[:, 0:2], at[:, kc, :], scT[:, kc, :], kc == 0, kc == 7, [ab, b_scT], [b_PM], inc=(kc == 7))
                cp(modraw[:, blk, :], PM[:, 0:2], [b_PM], [b_modraw])
            else:
                cols = slice((blk - 16) * 128, (blk - 15) * 128)
                for v in range(2):
                    for kc in range(8):
                        mm(PO1[:, v * 128:(v + 1) * 128], screp[:, v, kc, :], at[:, kc, :], kc == 0, kc == 7, [ab, b_screp], [b_PO1], inc=(kc == 7))
                tt(gate_bc[:, :, cols], PO1[:, 0:256].rearrange("p (v n) -> p v n", v=2), bg_bc[:, cols].unsqueeze(1).to_broadcast([128, 2, 128]), ALU.add,
                   [b_PO1, b_bg], [b_gate])

        def mod_ss_finish(l):
            par = l % 2
            gsT, shT = gsT2[:, par], shT2[:, par]
            tt(shT, modraw[:, 0:8, :], bshT[:, l, :].unsqueeze(2).to_broadcast([128, 8, 2]), ALU.add, [b_modraw, b_bshT], [b_shTp[par]])
            tt(gsT, modraw[:, 8:16, :], bscT[:, l, :].unsqueeze(2).to_broadcast([128, 8, 2]), ALU.add, [b_modraw, b_bscT], [b_gsTp[par]])
            ts(gsT, gsT, 1.0, None, ALU.add, [b_gsTp[par]], [b_gsTp[par]])
            tt(gsT, gsT, ngT[:, l, :].unsqueeze(2).to_broadcast([128, 8, 2]), ALU.mult, [b_gsTp[par], b_ngT], [b_gsTp[par]])

        def mod_stream(l, blks):
            blks = list(blks)
            state = {'i': 0}
            for b_ in blks[0:2]:
                ada_prefetch(l, b_)

            def step():
                i = state['i']
                if i >= len(blks):
                    return
                ada_consume(l, blks[i])
                if i + 2 < len(blks):
                    ada_prefetch(l, blks[i + 2])
                state['i'] = i + 1
            return step

        def load_bg(l):
            S.dma('sp', d_misc, bg_bc[:], bada_d[l, 2 * D:3 * D].partition_broadcast(128), writes=[b_bg])

        def emit_ret_statics(l):
            lg = rl_bc[:, l * 8:(l + 1) * 8]
            act(L1r[:], lg, AF.Exp, [b_rl], [b_L1r], scale=-1.0)
            act(L1r[:], L1r[:], AF.Ln, [b_L1r], [b_L1r], bias=1.0)
            ts(nL1r[:], L1r[:], -1.0, None, ALU.mult, [b_L1r], [b_nL1r])
            act(wend[:, 0, :], L1r[:, 0:4], AF.Exp, [b_L1r, b_cst], [b_wend], scale=cst[:, C_COLA:C_COLA + 1])
            act(wend[:, 1, :], L1r[:, 4:8], AF.Exp, [b_L1r, b_cst], [b_wend], scale=cst[:, C_COLB:C_COLB + 1])
            ts(wend[:].rearrange("p d h -> p (d h)"), wend[:].rearrange("p d h -> p (d h)"), 0.125, None, ALU.mult, [b_wend], [b_wend])
            act(gC[:], L1r[:], AF.Exp, [b_L1r], [b_gC], scale=-128.0)
            for d in range(2):
                for blk in range(2):
                    for hh in range(2):
                        pr = slice(hh * 64, (hh + 1) * 64)
                        h = blk * 2 + hh
                        cp(gblk[pr, d, blk:blk + 1], gC[pr, d * 4 + h:d * 4 + h + 1], [b_gC], [b_gblk])
                        cp(nLblk[pr, d, blk:blk + 1], nL1r[pr, d * 4 + h:d * 4 + h + 1], [b_nL1r], [b_nLblk])
            for d in range(2):
                for blk in range(2):
                    act(Wq[:, d, blk, :], cI(C_POSF if d == 0 else C_POSB), AF.Exp, [b_cst, b_nLblk], [b_Wq], scale=nLblk[:, d, blk:blk + 1])
            for h in range(4):
                act(tm1, cI(C_RIJ), AF.Exp, [b_cst, b_nL1r], [b_tm1], scale=nL1r[:, h:h + 1])
                act(tm2, cI(C_RJI), AF.Exp, [b_cst, b_nL1r], [b_tm2], scale=nL1r[:, 4 + h:5 + h])
                tt(tm1, tm1, cI(C_UINC), ALU.mult, [b_tm1, b_cst], [b_tm1])
                tt(tm2, tm2, cI(C_LINC), ALU.mult, [b_tm2, b_cst], [b_tm2])
                tt(Mret[:, h, :], tm1, tm2, ALU.add, [b_tm1, b_tm2], [b_Mret])
                ts(Mret[:, h, :], Mret[:, h, :], 0.125, None, ALU.mult, [b_Mret], [b_Mret])

        def emit_gla_statics(l):
            S.dma('sp', d_misc, w2s[0:16, 0:128], w2_d[l, 0], writes=[b_w2s])
            S.dma('sp', d_misc, w2s[16:32, 128:256], w2_d[l, 1], writes=[b_w2s])
            S.dma('sp', d_misc, w2s[32:33, :], b2_d[l:l + 1, :], writes=[b_w2s])
            cp(w2x[:], w2s[:], [b_w2s], [b_w2x])

        def emit_norm(l, half):
            v = half
            gsT, shT = gsT2[:, l % 2], shT2[:, l % 2]
            b_gsT, b_shT = b_gsTp[l % 2], b_shTp[l % 2]
            for t in range(8):
                gt = half * 8 + t
                act(hT[:, t, :], x_sb[:, gt, :], AF.Square, [bx[gt]], [b_hT[0], b_hT[1], b_ss], accum=ss[:, t:t + 1])
            rsq(rstd[:, 0:8], ss[:, 0:8], float(D), [b_ss], [b_rstd], ss[:, 8:16])
            for g in range(2):
                for tl in range(4):
                    t = g * 4 + tl
                    gt = half * 8 + t
                    xp, xb_ = xn_parts[tl // 2]
                    ts(xp[:, tl % 2, :], x_sb[:, gt, :], rstd[:, t:t + 1], None, ALU.mult, [bx[gt], b_rstd], [xb_])
                for kc in range(8):
                    pt, ptb = nextPT()
                    for tl in range(4):
                        xp, xb_ = xn_parts[tl // 2]
                        tr(pt[:, tl * 128:(tl + 1) * 128], xp[:, tl % 2, kc * 128:(kc + 1) * 128], idb[:], [xb_, b_idb], [ptb], inc=(tl == 3))
                    dst = hT[:, kc, g * 512:(g + 1) * 512]
                    if kc % 2 == 0:
                        act(dst, pt[:, 0:512], AF.Identity, [ptb, b_gsT, b_shT], [b_hT[g]], scale=gsT[:, kc, v:v + 1], bias=shT[:, kc, v:v + 1])
                    else:
                        ts(dst, pt[:, 0:512], gsT[:, kc, v:v + 1], shT[:, kc, v:v + 1], ALU.mult, [ptb, b_gsT, b_shT], [b_hT[g]], op1=ALU.add)

        hb_of = lambda t: b_hT[t // 4]

        def sig_inplace(par, n, lo=0):
            bb = b_Esb[par][lo // 256:(lo + n + 255) // 256]
            act(Esb[:, par, lo:lo + n], Esb[:, par, lo:lo + n], AF.Ln, bb, bb, bias=1.0)
            act(Esb[:, par, lo:lo + n], Esb[:, par, lo:lo + n], AF.Exp, bb, bb, scale=-1.0)

        def finish_y(l, t, par, src, nh, dh, kc0, zoff):
            n = nh * dh
            for h in range(nh):
                act(junk[:, h * dh:(h + 1) * dh], src[:, h * dh:(h + 1) * dh], AF.Square, [b_ya[(h * dh) // 128]], [b_junk, b_ssy[h]], accum=ssy[:, h:h + 1])
            rsq(ssy[:, 0:nh], ssy[:, 0:nh], float(dh), b_ssy[0:nh], b_ssy[0:nh], ssy[:, 4:4 + nh])
            tt(zsb[:, par, 0:n], zsb[:, par, 0:n], Esb[:, par, zoff:zoff + n], ALU.mult, [b_zsb[par], b_Esb[par][zoff // 256]], [b_zsb[par]])
            tt(src, src, zsb[:, par, 0:n], ALU.mult, [b_ya, b_zsb[par]], [b_ya])
            tt(ybf[:, 0:n].rearrange("p (h e) -> p h e", h=nh), src.rearrange("p (h e) -> p h e", h=nh),
               ssy[:, 0:nh].unsqueeze(2).to_broadcast([128, nh, dh]), ALU.mult, [b_ya, b_ssy[0:nh]], [b_ybf])

        def y_transpose(l, t, kc0):
            pt, ptb = nextPT()
            for j in range(2):
                tr(pt[:, j * 128:(j + 1) * 128], ybf[:, j * 128:(j + 1) * 128], idb[:], [b_ybf, b_idb], [ptb], inc=(j == 1))
            for j in range(2):
                act(yT[:, kc0 + j, t * 128:(t + 1) * 128], pt[:, j * 128:(j + 1) * 128], AF.Identity, [ptb, b_hngT], [b_yT[t]],
                    scale=hngT[:, l, kc0 + j:kc0 + j + 1])

        seqs_of = lambda half: [(0, 2), (2, 4), (4, 6), (6, 8)] if half == 0 else [(0, 8)]

        pending_tail = []

        def flush_tail():
            while pending_tail:
                pending_tail.pop(0)()

        def run_pass2(half, Fp, Feq, Fer, M1, M2, B1, B2, Yt, seq_begin, seq_end):
            starts = {t0: si for si, (t0, t1) in enumerate(seqs_of(half))}
            ends = {t1 - 1: si for si, (t0, t1) in enumerate(seqs_of(half))}
            Fp(0)
            Feq(0)
            M1(0)
            Fer(0)
            Fp(1)
            for c in range(8):
                if c in starts:
                    seq_begin(starts[c])
                M2(c)
                if c in ends:
                    seq_end(ends[c])
                if c + 1 < 8:
                    Feq(c + 1)
                    M1(c + 1)
                B1(c)
                if c + 1 < 8:
                    Fer(c + 1)
                if c + 2 < 8:
                    Fp(c + 2)
                B2(c)
                if c < 7:
                    Yt(c)
                else:
                    pending_tail.append(lambda: Yt(7))

        def emit_gates(l, half):
            for d in range(2):
                act(L1g[:, d, :, :], Gs[:, :, d * 8 + 4:d * 8 + 8], AF.Exp, [b_Gs], [b_L1g], scale=-1.0)
                yield
            l1flat = L1g[:].rearrange("p d t h -> p (d t h)")
            act(l1flat, l1flat, AF.Ln, [b_L1g], [b_L1g], bias=1.0)
            yield
            mm(PM[:, 0:32], cI(C_UINC), l1flat[:, 0:32], True, True, [b_cst, b_L1g], [b_PM], inc=False)
            yield
            mm(PM[:, 32:64], cI(C_LINC), l1flat[:, 32:64], True, True, [b_cst, b_L1g], [b_PM], inc=False)
            yield
            mm(PM[:, 64:128], ones, l1flat, True, True, [b_cst, b_L1g], [b_PM])
            yield
            for d in range(2):
                cp(Fp[:, :, d * 4:(d + 1) * 4], PM[:, d * 32:(d + 1) * 32].rearrange("p (t h) -> p t h", t=8), [b_PM], [b_Fp])
                yield
                cp(Fp[:, :, 8 + d * 4:12 + d * 4], PM[:, 64 + d * 32:96 + d * 32].rearrange("p (t h) -> p t h", t=8), [b_PM], [b_Fp])
                yield
            for d in range(2):
                tt(ug[:, :, d * 4:(d + 1) * 4], Fp[:, :, d * 4:(d + 1) * 4], Gs[:, :, d * 8:d * 8 + 4], ALU.add, [b_Fp, b_Gs], [b_ug])
                yield
            PE(lambda e: e.transpose(PM[0:64, 0:128], ug[:].rearrange("p t g -> p (t g)"), idf), [b_ug, b_cst], [b_PM])
            yield
            DVE(lambda e: e.reduce_max(umax[:], PM[0:64, 0:128], axis=AX.X), [b_PM], [b_umax])
            yield
            ts(udiag, cst[0:64, C_IDF:C_IDF + 64], umax[:, 0:1], None, ALU.mult, [b_cst, b_umax], [b_udiag])
            yield
            mm(PM[:, 128:192], cst[0:64, C_ONES:C_ONES + 128], udiag, True, True, [b_cst, b_udiag], [b_PM])
            yield
            cp(Abc[:].rearrange("p t g -> p (t g)"), PM[:, 128:192], [b_PM], [b_Abc])
            yield
            if half == 0:
                v4 = lambda X: X[:].rearrange("p (s k) g -> p s k g", k=2)
                mst = Mst[:].rearrange("p (s r) -> p s r", s=4)
                for d in range(2):
                    sl = slice(d * 4, (d + 1) * 4)
                    tl = slice(8 + d * 4, 12 + d * 4)
                    k0, k1 = (0, 1) if d == 0 else (1, 0)
                    DVE(lambda e: e.memset(v4(MP)[:, :, k0, sl], 0.0), [], [b_MP])
                    yield
                    tt(v4(Sg)[:, :, k0, sl], v4(MP)[:, :, k0, sl], v4(Abc)[:, :, k0, sl], ALU.max, [b_MP, b_Abc], [b_Sg])
                    yield
                    tt(v4(MP)[:, :, k1, sl], v4(Sg)[:, :, k0, sl], v4(Fp)[:, :, k0, tl], ALU.subtract, [b_Sg, b_Fp], [b_MP])
                    yield
                    tt(v4(Sg)[:, :, k1, sl], v4(MP)[:, :, k1, sl], v4(Abc)[:, :, k1, sl], ALU.max, [b_MP, b_Abc], [b_Sg])
                    yield
                    tt(mst[:, :, l * 8 + d * 4:l * 8 + d * 4 + 4], v4(Sg)[:, :, k1, sl], v4(Fp)[:, :, k1, tl], ALU.subtract, [b_Sg, b_Fp], [b_Mst])
                    yield
            else:
                orders = [list(range(8)), list(range(7, -1, -1))]
                for d in range(2):
                    sl = slice(d * 4, (d + 1) * 4)
                    cp(MP[:, orders[d][0], sl], m0_bc[:, l * 8 + d * 4:l * 8 + d * 4 + 4], [b_m0], [b_MP])
                    yield
                for i in range(8):
                    for d in range(2):
                        sl = slice(d * 4, (d + 1) * 4)
                        tl = slice(8 + d * 4, 12 + d * 4)
                        c = orders[d][i]
                        tt(Sg[:, c, sl], MP[:, c, sl], Abc[:, c, sl], ALU.max, [b_MP, b_Abc], [b_Sg])
                        yield
                        if i + 1 < 8:
                            tt(MP[:, orders[d][i + 1], sl], Sg[:, c, sl], Fp[:, c, tl], ALU.subtract, [b_Sg, b_Fp], [b_MP])
                            yield
            tt(tmpg[:], MP[:], Sg[:], ALU.subtract, [b_MP, b_Sg], [b_tmpg])
            yield
            act(rg[:], tmpg[:], AF.Exp, [b_tmpg], [b_rg])
            yield
            tt(tmpg[:], ug[:], Sg[:], ALU.subtract, [b_ug, b_Sg], [b_tmpg])
            yield
            act(wg[:], tmpg[:], AF.Exp, [b_tmpg], [b_wg])
            yield
            tt(tmpg[:], Fp[:, :, 0:8], Sg[:], ALU.subtract, [b_Fp, b_Sg], [b_tmpg])
            yield
            act(flo[:], tmpg[:], AF.Exp, [b_tmpg], [b_flo])
            yield

        def state_init_A(l, half, j, d, h, slot=None):
            st, sb, ssem = Cst[j * 2 + (d if slot is None else slot)]
            if half == 0:
                DVE(lambda e: e.memset(st[:, 0:130], 0.0), [], [sb])
            else:
                S.dma('sp', ssem, st[:, 0:128], sC_d[l, d, h], writes=[sb])
                S.dma('sp', ssem, st[:, 128:129], sn_d[l, d, h].rearrange("(p o) -> p o", o=1), writes=[sb], slow=True)

        def state_out_A(l, si, j, d, h, slot=None):
            st, sb, ssem = Cst[j * 2 + (d if slot is None else slot)]
            S.dma('sp', ssem, oC_d[si, l, d, h], st[:, 0:128], reads=[sb])
            col = ((si * DEPTH + l) * 2 + d) * 4 + h
            cp(Nst[:, col:col + 1], st[:, 128:129], [sb], [b_Nst])

        def emit_A_pair(l, half, p, wKV, wQO, wZ, after_p1=lambda: None):
            (wkv, bkv), (wqo, bqo), (wz, bz) = wKV, wQO, wZ
            h0 = 2 * p
            ksc = 128.0 ** -0.5
            for t in range(8):
                ps, psb = (PA, b_PA) if t % 2 == 0 else (PB, b_PB)
                proj(ps[:], psb, t, wkv, bkv, 0, 512, hb_of(t))
                if t % 2 == 0:
                    act(ktok[:, t, :], ps[:, 0:256], AF.Identity, [psb], [b_ktok[t]], scale=ksc)
                    for j in range(2):
                        h = h0 + j
                        act(vt_f[:, t, j, 0:128], ps[:, 256 + j * 128:384 + j * 128], AF.Identity, [psb, b_wg], [b_vtf[t]], scale=wg[:, t, h:h + 1])
                        act(vt_b[:, t, j, 0:128], ps[:, 256 + j * 128:384 + j * 128], AF.Identity, [psb, b_wg], [b_vtb[t]], scale=wg[:, t, 4 + h:5 + h])
                else:
                    ts(ktok[:, t, :], ps[:, 0:256], ksc, None, ALU.mult, [psb], [b_ktok[t]])
                    for j in range(2):
                        h = h0 + j
                        ts(vt_f[:, t, j, 0:128], ps[:, 256 + j * 128:384 + j * 128], wg[:, t, h:h + 1], None, ALU.mult, [psb, b_wg], [b_vtf[t]])
                        ts(vt_b[:, t, j, 0:128], ps[:, 256 + j * 128:384 + j * 128], wg[:, t, 4 + h:5 + h], None, ALU.mult, [psb, b_wg], [b_vtb[t]])
                if t == 1:
                    flush_tail()
            for j in range(2):
                cp(vt_f[:, :, j, 128:129], wg[:, :, h0 + j:h0 + j + 1], [b_wg], [b_vtf])
                cp(vt_b[:, :, j, 128:129], wg[:, :, 4 + h0 + j:5 + h0 + j], [b_wg], [b_vtb])
            after_p1()
            for si, (t0, t1) in enumerate(seqs_of(half)):
                slot = 1 - (si % 2)
                for j in range(2):
                    state_init_A(l, half, j, 1, h0 + j, slot)
                cur = [slot, slot]
                for c in range(t1 - 1, t0 - 1, -1):
                    for j in range(2):
                        h = h0 + j
                        st, sb, _ = Cst[j * 2 + cur[j]]
                        if half == 1:
                            cur[j] = 1 - cur[j]
                        so, sob, _ = Cst[j * 2 + cur[j]]
                        pu, pub = (PU, b_PU) if j == 0 else (PSc, b_PSc)
                        act(Sb16[:, c, j, :], st[:, 0:130], AF.Identity, [sb, b_rg], [b_Sb16], scale=rg[:, c, 4 + h:5 + h])
                        mm(pu[:, 0:129], ktok[:, c, j * 128:(j + 1) * 128], vt_b[:, c, j, 0:129], True, True, [b_ktok[c], b_vtb[c]], [pub])
                        stt(so[:, 0:129], st[:, 0:129], rg[:, c, 4 + h:5 + h], pu[:, 0:129], ALU.mult, ALU.add, [sb, b_rg, pub], [sob])
                if half == 0:
                    for j in range(2):
                        state_out_A(l, si, j, 1, h0 + j, slot)

            def Fp(c):
                proj(PA[:], b_PA, c, wqo, bqo, 0, 512, hb_of(c))
                proj(PB[:, 0:256], b_PB, c, wz, bz, 0, 256, hb_of(c))

            def Feq(c):
                act(qtok[:], PA[:, 0:256], AF.Identity, [b_PA], [b_qtok])

            def Fer(c):
                par = c % 2
                act(Esb[:, par, 0:256], PA[:, 256:512], AF.Exp, [b_PA], [b_Esb[par][0]], scale=-1.0)
                act(Esb[:, par, 256:512], PB[:, 0:256], AF.Exp, [b_PB], [b_Esb[par][1]], scale=-1.0)
                act(zsb[:, par, :], PB[:, 0:256], AF.Identity, [b_PB], [b_zsb[par]])
                sig_inplace(par, 512)

            def M1(c):
                pt, ptb = nextPT()
                for j in range(2):
                    tr(pt[:, j * 128:(j + 1) * 128], qtok[:, j * 128:(j + 1) * 128], idb[:], [b_qtok, b_idb], [ptb], inc=False)
                    tr(pt[:, (2 + j) * 128:(3 + j) * 128], ktok[:, c, j * 128:(j + 1) * 128], idb[:], [b_ktok[c], b_idb], [ptb], inc=(j == 1))
                cp(qkT[:].rearrange("p a b -> p (a b)"), pt[:, 0:512], [ptb], [b_qkT])
                for j in range(2):
                    mm(PSc[:, j * 128:(j + 1) * 128], qkT[:, 2 + j, :], qkT[:, j, :], True, True, [b_qkT], [b_PSc], inc=(j == 1))
                psv = PSc[:, 0:256].rearrange("p (j i) -> p j i", j=2)
                tt(Pf[:, 0:2, :], psv, uincb[:].unsqueeze(1).to_broadcast([128, 2, 128]), ALU.mult, [b_PSc, b_uincb], [b_Pf])
                tt(Pb[:, 0:2, :], psv, lincb[:].unsqueeze(1).to_broadcast([128, 2, 128]), ALU.mult, [b_PSc, b_lincb], [b_Pb])

            def M2(c):
                for j in range(2):
                    h = h0 + j
                    st, sb, _ = Cst[j * 2 + 0]
                    act(C16[:, j, 0:130], st[:, 0:130], AF.Identity, [sb, b_rg], [b_C16], scale=rg[:, c, h:h + 1])
                for j in range(2):
                    mm(PO1[:, j * 256:j * 256 + 129], Pf[:, j, :], vt_f[:, c, j, 0:129], True, False, [b_Pf, b_vtf[c]], [b_PO1], inc=False)
                    mm(PO1[:, j * 256:j * 256 + 129], qkT[:, j, :], C16[:, j, 0:129], False, True, [b_qkT, b_C16], [b_PO1], inc=(j == 1))
                for j in range(2):
                    mm(PO2[:, j * 256:j * 256 + 129], Pb[:, j, :], vt_b[:, c, j, 0:129], True, False, [b_Pb, b_vtb[c]], [b_PO2], inc=False)
                    mm(PO2[:, j * 256:j * 256 + 129], qkT[:, j, :], Sb16[:, c, j, 0:129], False, True, [b_qkT, b_Sb16], [b_PO2], inc=(j == 1))
                for j in range(2):
                    st, sb, _ = Cst[j * 2 + 0]
                    mm(PU[:, 0:129], ktok[:, c, j * 128:(j + 1) * 128], vt_f[:, c, j, 0:129], True, True, [b_ktok[c], b_vtf[c]], [b_PU])
                    stt(st[:, 0:129], st[:, 0:129], rg[:, c, h0 + j:h0 + j + 1], PU[:, 0:129], ALU.mult, ALU.add, [sb, b_rg, b_PU], [sb])

            def B1(c):
                po1 = PO1[:].rearrange("p (j n) -> p j n", j=2)
                po2 = PO2[:].rearrange("p (j n) -> p j n", j=2)
                act(dn[:, 0:2].unsqueeze(2), po1[:, :, 128:129], AF.Abs, [b_PO1], [b_dn[0]])
                act(dn[:, 2:4].unsqueeze(2), po2[:, :, 128:129], AF.Abs, [b_PO2], [b_dn[1]])
                tt(dn[:, 0:2], dn[:, 0:2], flo[:, c, h0:h0 + 2], ALU.max, [b_dn[0], b_flo], [b_dn[0]])
                tt(dn[:, 2:4], dn[:, 2:4], flo[:, c, 4 + h0:6 + h0], ALU.max, [b_dn[1], b_flo], [b_dn[1]])
                DVE(lambda e: e.reciprocal(dn[:, 0:4], dn[:, 0:4]), [b_dn], [b_dn])
                tt(ya[:].rearrange("p (j e) -> p j e", j=2), po1[:, :, 0:128], dn[:, 0:2].unsqueeze(2).to_broadcast([128, 2, 128]), ALU.mult,
                   [b_PO1, b_dn], [b_ya])
                for j in range(2):
                    stt(ya[:, j * 128:(j + 1) * 128], po2[:, j, 0:128], dn[:, 2 + j:3 + j], ya[:, j * 128:(j + 1) * 128], ALU.mult, ALU.add,
                        [b_PO2, b_dn, b_ya[j]], [b_ya[j]])

            def B2(c):
                par = c % 2
                tt(ya[:], ya[:], Esb[:, par, 0:256], ALU.mult, [b_ya, b_Esb[par][0]], [b_ya])
                finish_y(l, c, par, ya[:], 2, 128, h0, 256)

            def Yt(c):
                y_transpose(l, c, h0)

            def seq_begin(si):
                for j in range(2):
                    state_init_A(l, half, j, 0, h0 + j)

            def seq_end(si):
                if half == 0:
                    for j in range(2):
                        state_out_A(l, si, j, 0, h0 + j)

            run_pass2(half, Fp, Feq, Fer, M1, M2, B1, B2, Yt, seq_begin, seq_end)

        def rotary(dst, ps, t, R, W):
            cs = cst[:, C_COS + t * 32:C_COS + (t + 1) * 32].unsqueeze(1).to_broadcast([128, 4, 32])
            sn = cst[:, C_SIN + t * 32:C_SIN + (t + 1) * 32].unsqueeze(1).to_broadcast([128, 4, 32])
            pv = ps.rearrange("p (h e) -> p h e", h=4)
            dv = dst.rearrange("p (h e) -> p h e", h=4)
            t1, t2 = pv[:, :, 0:32], pv[:, :, 32:64]
            tt(rot1[:], t1, cs, ALU.mult, R + [b_cst], [b_rot1])
            tt(rot2[:], t2, sn, ALU.mult, R + [b_cst], [b_rot2])
            tt(dv[:, :, 0:32], rot1[:], rot2[:], ALU.subtract, [b_rot1, b_rot2], W)
            tt(rot1[:], t1, sn, ALU.mult, R + [b_cst], [b_rot1])
            tt(rot2[:], t2, cs, ALU.mult, R + [b_cst], [b_rot2])
            tt(dv[:, :, 32:64], rot1[:], rot2[:], ALU.add, [b_rot1, b_rot2], W)

        def emit_B(l, half, wKV, wQZ):
            (wkv, bkv), (wqz, bqz) = wKV, wQZ
            vhf = vt_f[:].rearrange("p t j n -> p t (j n)")
            vhb = vt_b[:].rearrange("p t j n -> p t (j n)")
            sb16 = Sb16[:].rearrange("p t j n -> p t (j n)")
            for t in range(8):
                ps, psb = (PA, b_PA) if t % 2 == 0 else (PB, b_PB)
                proj(ps[:], psb, t, wkv, bkv, 0, 512, hb_of(t))
                pav = ps[:, 256:512].rearrange("p (h e) -> p h e", h=4)
                if half == 0:
                    act(ktok[:, t, :], ps[:, 0:256], AF.Identity, [psb], [b_ktok[t]])
                else:
                    rotary(ktok[:, t, :], ps[:, 0:256], t, [psb], [b_ktok[t]])
                act(vtok[:, t, :], ps[:, 256:512], AF.Identity, [psb], [b_vtok[t]])
                tt(vhf[:, t, 0:256].rearrange("p (h e) -> p h e", h=4), pav, wend[:, 0, :].unsqueeze(2).to_broadcast([128, 4, 64]), ALU.mult, [psb, b_wend], [b_vtf[t]])
                tt(vhb[:, t, 0:256].rearrange("p (h e) -> p h e", h=4), pav, wend[:, 1, :].unsqueeze(2).to_broadcast([128, 4, 64]), ALU.mult, [psb, b_wend], [b_vtb[t]])
                if t == 1:
                    flush_tail()

            if MKX == 21:
                return

            def st_init(blk, d, slot=None):
                st, sb, ssem = Cst[blk * 2 + (d if slot is None else slot)]
                if half == 0:
                    DVE(lambda e: e.memset(st[:, 0:128], 0.0), [], [sb])
                else:
                    for hh in range(2):
                        S.dma('sp', ssem, st[hh * 64:(hh + 1) * 64, hh * 64:(hh + 1) * 64], sret_d[l, d, blk * 2 + hh], writes=[sb])

            def st_out(si, blk, d, slot=None):
                st, sb, ssem = Cst[blk * 2 + (d if slot is None else slot)]
                for hh in range(2):
                    S.dma('sp', ssem, oret_d[si, l, d, blk * 2 + hh], st[hh * 64:(hh + 1) * 64, hh * 64:(hh + 1) * 64], reads=[sb])

            for si, (t0, t1) in enumerate(seqs_of(half)):
                slot = 1 - (si % 2)
                for blk in range(2):
                    st_init(blk, 1, slot)
                for c in range(t1 - 1, t0 - 1, -1):
                    for blk in range(2):
                        st, sb, _ = Cst[blk * 2 + slot]
                        bs = slice(blk * 128, (blk + 1) * 128)
                        pu, pub = (PU, b_PU) if blk == 0 else (PSc, b_PSc)
                        tt(sb16[:, c, bs], st[:, 0:128], cI(C_BD2), ALU.mult, [sb, b_cst], [b_Sb16])
                        mm(pu[:, 0:128], ktok[:, c, bs], vhb[:, c, bs], True, True, [b_ktok[c], b_vtb[c]], [pub])
                        stt(st[:, 0:128], st[:, 0:128], gblk[:, 1, blk:blk + 1], pu[:, 0:128], ALU.mult, ALU.add, [sb, b_gblk, pub], [sb])
                if half == 0:
                    for blk in range(2):
                        st_out(si, blk, 1, slot)
            if MKX == 22:
                return

            def Fp(c):
                proj(PA[:], b_PA, c, wqz, bqz, 0, 512, hb_of(c))

            def Feq(c):
                if half == 0:
                    act(qtok[:], PA[:, 0:256], AF.Identity, [b_PA], [b_qtok])
                else:
                    rotary(qtok[:], PA[:, 0:256], c, [b_PA], [b_qtok])

            def Fer(c):
                par = c % 2
                act(Esb[:, par, 256:512], PA[:, 256:512], AF.Exp, [b_PA], [b_Esb[par][1]], scale=-1.0)
                act(zsb[:, par, :], PA[:, 256:512], AF.Identity, [b_PA], [b_zsb[par]])
                sig_inplace(par, 256, 256)

            def M1(c):
                pt, ptb = nextPT()
                for blk in range(2):
                    tr(pt[:, blk * 128:(blk + 1) * 128], qtok[:, blk * 128:(blk + 1) * 128], idb[:], [b_qtok, b_idb], [ptb], inc=False)
                    tr(pt[:, (2 + blk) * 128:(3 + blk) * 128], ktok[:, c, blk * 128:(blk + 1) * 128], idb[:], [b_ktok[c], b_idb], [ptb], inc=(blk == 1))
                cp(qkT[:].rearrange("p a b -> p (a b)"), pt[:, 0:512], [ptb], [b_qkT])
                for d in range(2):
                    tt(qhT[:, d, :, :], qkT[:, 0:2, :], Wq[:, d, :, :], ALU.mult, [b_qkT, b_Wq], [b_qhT])
                qbd = Pb[:].rearrange("p a b -> p (a b)").rearrange("p (k a i) -> p k a i", k=2, a=2)
                for blk in range(2):
                    tt(qbd[:, blk, :, :], qkT[:, blk, :].unsqueeze(1).to_broadcast([128, 2, 128]),
                       cst[:, C_MC2:C_MC2 + 2].unsqueeze(2).to_broadcast([128, 2, 128]), ALU.mult, [b_qkT, b_cst], [b_Pb])
                for blk in range(2):
                    mm(PSc[:, blk * 256:(blk + 1) * 256], qkT[:, 2 + blk, :], qbd[:, blk, :, :].rearrange("p a i -> p (a i)"), True, True,
                       [b_qkT, b_Pb], [b_PSc], inc=(blk == 1))

            def M2(c):
                tt(Pf[:].rearrange("p a b -> p (a b)"), PSc[:], Mret[:].rearrange("p a b -> p (a b)"), ALU.mult, [b_PSc, b_Mret], [b_Pf])
                for blk in range(2):
                    st, sb, _ = Cst[blk * 2 + 0]
                    tt(C16[:, blk, 0:128], st[:, 0:128], cI(C_BD2), ALU.mult, [sb, b_cst], [b_C16])
                for blk in range(2):
                    bs = slice(blk * 128, (blk + 1) * 128)
                    mm(PO2[:, bs], qhT[:, 0, blk, :], C16[:, blk, 0:128], True, False, [b_qhT, b_C16], [b_PO2], inc=False)
                    mm(PO2[:, bs], qhT[:, 1, blk, :], sb16[:, c, bs], False, False, [b_qhT, b_Sb16], [b_PO2], inc=False)
                    for hh in range(2):
                        h = blk * 2 + hh
                        oc = slice(h * 64, (h + 1) * 64)
                        mm(PO2[:, oc], Pf[:, h, :], vtok[:, c, oc], False, hh == 1, [b_Pf, b_vtok[c]], [b_PO2], inc=(blk == 1 and hh == 1))
                for blk in range(2):
                    st, sb, _ = Cst[blk * 2 + 0]
                    bs = slice(blk * 128, (blk + 1) * 128)
                    mm(PU[:, 0:128], ktok[:, c, bs], vhf[:, c, bs], True, True, [b_ktok[c], b_vtf[c]], [b_PU])
                    stt(st[:, 0:128], st[:, 0:128], gblk[:, 0, blk:blk + 1], PU[:, 0:128], ALU.mult, ALU.add, [sb, b_gblk, b_PU], [sb])

            def B1(c):
                act(ya[:], PO2[:, 0:256], AF.Identity, [b_PO2], [b_ya])

            def B2(c):
                finish_y(l, c, c % 2, ya[:], 4, 64, 4, 256)

            def Yt(c):
                y_transpose(l, c, 4)

            def seq_begin(si):
                for blk in range(2):
                    st_init(blk, 0)

            def seq_end(si):
                if half == 0:
                    for blk in range(2):
                        st_out(si, blk, 0)

            run_pass2(half, Fp, Feq, Fer, M1, M2, B1, B2, Yt, seq_begin, seq_end)

        def emit_C(l, half, wKV, wQZ, after_p1, after_gates, per_tile=lambda: None):
            (wkv, bkv), (wqz, bqz) = wKV, wQZ
            sb16 = Sb16[:].rearrange("p t j n -> p t (j n)")
            for t in range(8):
                toks = slice(t * 128, (t + 1) * 128)
                for kc in range(8):
                    mm(PB[0:32, 0:128], wkv[:, kc, 400:432], hT[:, kc, toks], kc == 0, kc == 7, [bkv, hb_of(t)], [b_PB], inc=(kc == 7))
                act(clrT[0:32, :], PB[0:32, 0:128], AF.Identity, [b_PB], [b_clrT])
                mm(PB[:, 256:512], clrT[:], w2x[:], True, True, [b_clrT, b_w2x], [b_PB])
                act(L1c, PB[:, 256:512], AF.Exp, [b_PB], [b_L1c], scale=-1.0)
                act(L1c, L1c, AF.Ln, [b_L1c], [b_L1c], bias=1.0)
                proj(PA[:, 0:400], b_PA, t, wkv, bkv, 0, 400, hb_of(t))
                mm(PB[:, 0:128], cI(C_UINC), L1c[:, 0:128], True, True, [b_cst, b_L1c], [b_PB], inc=False)
                mm(PB[:, 128:256], cI(C_LINC), L1c[:, 128:256], True, True, [b_cst, b_L1c], [b_PB])
                for d in range(2):
                    mm(PM[:, 200 + d:201 + d], L1c[:, d * 128:(d + 1) * 128], cst[:, C_ONES:C_ONES + 1], True, True, [b_L1c, b_cst], [b_PM], inc=(d == 1))
                act(Gc[:, t, :], PM[:, 200:202], AF.Exp, [b_PM], [b_Gc], scale=-1.0 / 16)
                act(EBn[:], PB[:, 0:256], AF.Exp, [b_PB], [b_EBn], scale=1.0 / 16)
                for d in range(2):
                    act(EBd[d][0][:, t, :], PB[:, d * 128:(d + 1) * 128], AF.Exp, [b_PB], [EBd[d][1]], scale=-1.0 / 16)
                act(vtok[:, t, :], PA[:, 128:384], AF.Identity, [b_PA], [b_vtok[t]])
                stt(ktok[:, t, :].rearrange("p (d n) -> p d n", d=2), EBn[:].rearrange("p (d n) -> p d n", d=2), 32.0 ** -0.5,
                    PA[:, 0:128].unsqueeze(1).to_broadcast([128, 2, 128]), ALU.mult, ALU.mult, [b_EBn, b_PA], [b_ktok])
                tt(Gs[:, t, :], PA[:, 384:400], gb_bc[:, l * 16:(l + 1) * 16], ALU.add, [b_PA, b_gb], [b_Gs])
                per_tile()
                if t == 1:
                    flush_tail()
            after_p1()
            gates_gen = emit_gates(l, half)

            def gstep(n=1):
                for _ in range(n):
                    next(gates_gen, None)
            after_gates()

            def st_init(d, slot=None):
                st, sb, ssem = Cst[d if slot is None else slot]
                DVE(lambda e: e.memset(st[:, 0:256], 0.0), [], [sb])
                if half == 1:
                    for h in range(4):
                        S.dma('sp', ssem, st[h * 32:(h + 1) * 32, h * 64:(h + 1) * 64], sgla_d[l, d, h], writes=[sb])

            def st_out(si, d, slot=None):
                st, sb, ssem = Cst[d if slot is None else slot]
                for h in range(4):
                    S.dma('sp', ssem, ogla_d[si, l, d, h], st[h * 32:(h + 1) * 32, h * 64:(h + 1) * 64], reads=[sb])

            def st_update(d, c, slot=None):
                st, sb, _ = Cst[d if slot is None else slot]
                mm(PU[:, 0:256], ktok[:, c, d * 128:(d + 1) * 128], vtok[:, c, :], True, True, [b_ktok[c], b_vtok[c]], [b_PU])
                tt(st[:, 0:256], st[:, 0:256], PU[:, 0:256], ALU.add, [sb, b_PU], [sb])
                ts(st[:, 0:256], st[:, 0:256], Gc[:, c, d:d + 1], None, ALU.mult, [sb, b_Gc], [sb])

            for si, (t0, t1) in enumerate(seqs_of(half)):
                slot = 1 + (si % 2)
                st, sb, _ = Cst[slot]
                st_init(1, slot)
                for c in range(t1 - 1, t0 - 1, -1):
                    tt(sb16[:, c, 0:256], st[:, 0:256], cI(C_BD4, 256), ALU.mult, [sb, b_cst], [b_Sb16])
                    gstep(2)
                    st_update(1, c, slot)
                    gstep(2)
                if half == 0:
                    st_out(si, 1, slot)

            qt = qtok[:].rearrange("p (d n) -> p d n", d=2)

            def Fp(c):
                proj(PA[:, 0:384], b_PA, c, wqz, bqz, 0, 384, hb_of(c))

            def Feq(c):
                for d in range(2):
                    tt(qt[:, d, :], EBd[d][0][:, c, :], PA[:, 0:128], ALU.mult, [EBd[d][1], b_PA], [b_qtok])

            def Fer(c):
                par = c % 2
                act(Esb[:, par, 256:512], PA[:, 128:384], AF.Exp, [b_PA], [b_Esb[par][1]], scale=-1.0)
                act(zsb[:, par, :], PA[:, 128:384], AF.Identity, [b_PA], [b_zsb[par]])
                sig_inplace(par, 256, 256)

            def M1(c):
                pt, ptb = nextPT()
                for d in range(2):
                    tr(pt[:, d * 128:(d + 1) * 128], qt[:, d, :], idb[:], [b_qtok, b_idb], [ptb], inc=False)
                    tr(pt[:, (2 + d) * 128:(3 + d) * 128], ktok[:, c, d * 128:(d + 1) * 128], idb[:], [b_ktok[c], b_idb], [ptb], inc=(d == 1))
                cp(qkT[:].rearrange("p a b -> p (a b)"), pt[:, 0:512], [ptb], [b_qkT])
                mc4 = cst[:, C_MC4:C_MC4 + 4].unsqueeze(2).to_broadcast([128, 4, 128])
                tt(Pf[:], qkT[:, 0, :].unsqueeze(1).to_broadcast([128, 4, 128]), mc4, ALU.mult, [b_qkT, b_cst], [b_Pf])
                tt(Pb[:], qkT[:, 1, :].unsqueeze(1).to_broadcast([128, 4, 128]), mc4, ALU.mult, [b_qkT, b_cst], [b_Pb])
                mm(PSc[:], qkT[:, 2, :], Pf[:].rearrange("p a b -> p (a b)"), True, True, [b_qkT, b_Pf], [b_PSc])
                mm(PO1[:], qkT[:, 3, :], Pb[:].rearrange("p a b -> p (a b)"), True, True, [b_qkT, b_Pb], [b_PO1])

            def M2(c):
                st, sb, _ = Cst[0]
                tt(Pf[:], PSc[:].rearrange("p (h i) -> p h i", h=4), uincb[:].unsqueeze(1).to_broadcast([128, 4, 128]), ALU.mult, [b_PSc, b_uincb], [b_Pf])
                tt(Pb[:], PO1[:].rearrange("p (h i) -> p h i", h=4), lincb[:].unsqueeze(1).to_broadcast([128, 4, 128]), ALU.mult, [b_PO1, b_lincb], [b_Pb])
                tt(C16[:, 0, 0:256], st[:, 0:256], cI(C_BD4, 256), ALU.mult, [sb, b_cst], [b_C16])
                mm(PO2[:, 0:256], qkT[:, 0, :], C16[:, 0, 0:256], True, False, [b_qkT, b_C16], [b_PO2], inc=False)
                mm(PO2[:, 0:256], qkT[:, 1, :], sb16[:, c, 0:256], False, False, [b_qkT, b_Sb16], [b_PO2], inc=False)
                for h in range(4):
                    oc = slice(h * 64, (h + 1) * 64)
                    mm(PO2[:, oc], Pf[:, h, :], vtok[:, c, oc], False, False, [b_Pf, b_vtok[c]], [b_PO2], inc=False)
                    mm(PO2[:, oc], Pb[:, h, :], vtok[:, c, oc], False, h == 3, [b_Pb, b_vtok[c]], [b_PO2], inc=(h == 3))
                st_update(0, c)

            def B1(c):
                act(ya[:], PO2[:, 0:256], AF.Identity, [b_PO2], [b_ya])
                gstep(2)

            def B2(c):
                finish_y(l, c, c % 2, ya[:], 4, 64, 6, 256)
                gstep(2)

            def Yt(c):
                y_transpose(l, c, 6)

            def seq_begin(si):
                st_init(0)

            def seq_end(si):
                if half == 0:
                    st_out(si, 0)

            run_pass2(half, Fp, Feq, Fer, M1, M2, B1, B2, Yt, seq_begin, seq_end)
            for _ in gates_gen:
                pass

        def emit_out(l, half, wo0, wo1, per_group=lambda: None):
            v = half
            flush_tail()
            for t in range(8):
                gt = half * 8 + t
                toks = slice(t * 128, (t + 1) * 128)
                for fb, (wt, wb_) in enumerate((wo0, wo1)):
                    ps, psb = (PA, b_PA) if fb == 0 else (PB, b_PB)
                    for kc in range(8):
                        mm(ps[:], yT[:, kc, toks], wt[:, kc, 0:512], kc == 0, kc == 7, [b_yT[t], wb_], [psb], inc=(kc == 7))
                    fs = slice(fb * 512, (fb + 1) * 512)
                    tt(Esb[:, fb, :], ps[:], gate_bc[:, v, fs], ALU.mult, [psb, b_gate], [b_Esb[fb]])
                    tt(x_sb[:, gt, fs], x_sb[:, gt, fs], Esb[:, fb, :], ALU.add, [bx[gt], b_Esb[fb]], [bx[gt]])
                    per_group()

        PHASES.clear()

        def mark(name):
            PHASES.append((name, {k: v for k, v in S.cnt.items() if not k.startswith('dma_')}))
        for l in range(depth):
            mark("L%d mod" % l)
            load_bg(l)
            if l == 0:
                st0 = mod_stream(0, range(16))
                for _ in range(16):
                    st0()
                mod_ss_finish(0)
            if STAGE < 2:
                break
            emit_ret_statics(l)
            emit_gla_statics(l)
            if STAGE < 3:
                break
            for half in range(2):
                mark("L%d h%d norm" % (l, half))
                emit_norm(l, half)
                if STAGE < 4:
                    continue
                blocks = {}
                blocks['CKV'] = load_block(l, [(CK, 128), (CV, 256), (OG, 16), (CL, 32)], win_d)
                blocks['CQZ'] = load_block(l, [(CQ, 128), (CZ, 256)], win_d)
                blocks['KV0'] = load_block(l, [(OK_, 256), (OV, 256)], win_d)
                blocks['QO0'] = load_block(l, [(OQ, 256), (OO, 256)], win_d)

                def c_after_p1():
                    blocks['Z0'] = load_block(l, [(OZ, 256)], win_d)

                def c_after_gates():
                    pass
                mark("L%d h%d C" % (l, half))
                gate_step = mod_stream(l, range(16, 24)) if half == 0 else (lambda: None)
                emit_C(l, half, blocks['CKV'], blocks['CQZ'], c_after_p1, c_after_gates, gate_step)
                blocks['KV1'] = load_block(l, [(OK_ + 256, 256), (OV + 256, 256)], win_d)

                def a0_after_p1():
                    blocks['QO1'] = load_block(l, [(OQ + 256, 256), (OO + 256, 256)], win_d)
                mark("L%d h%d A0" % (l, half))
                emit_A_pair(l, half, 0, blocks['KV0'], blocks['QO0'], blocks['Z0'], a0_after_p1)
                blocks['Z1'] = load_block(l, [(OZ + 256, 256)], win_d)
                blocks['BKV'] = load_block(l, [(BK, 256), (BV, 256)], win_d)

                def a1_after_p1():
                    blocks['BQZ'] = load_block(l, [(BQ, 256), (BZ, 256)], win_d)
                mark("L%d h%d A1" % (l, half))
                emit_A_pair(l, half, 1, blocks['KV1'], blocks['QO1'], blocks['Z1'], a1_after_p1)
                blocks['WO0'] = load_block(l, [(0, 512)], wout_d)
                blocks['WO1'] = load_block(l, [(512, 512)], wout_d)
                mark("L%d h%d B" % (l, half))
                emit_B(l, half, blocks['BKV'], blocks['BQZ'])
                mark("L%d h%d out" % (l, half))
                if half == 0 and l + 1 < depth:
                    ss_step = mod_stream(l + 1, range(16))
                    emit_out(l, half, blocks['WO0'], blocks['WO1'], ss_step)
                    mod_ss_finish(l + 1)
                else:
                    emit_out(l, half, blocks['WO0'], blocks['WO1'])

        mark("final")
        for gt in range(16):
            act(hT[:, gt % 8, :], x_sb[:, gt, :], AF.Square, [bx[gt]], [b_hT[0], b_hT[1], b_ss], accum=ss[:, gt:gt + 1])
        act(rstd[:], ss[:], AF.Ln, [b_ss], [b_rstd], scale=1.0 / D, bias=EPS)
        act(rstd[:], rstd[:], AF.Exp, [b_rstd], [b_rstd], scale=-0.5)
        S.dma('sp', d_misc, fin_bc[:], fing_d.partition_broadcast(128), writes=[b_fin])
        for gt in range(16):
            stt(x_sb[:, gt, :], x_sb[:, gt, :], rstd[:, gt:gt + 1], fin_bc[:], ALU.mult, ALU.mult, [bx[gt], b_rstd, b_fin], [bx[gt]])
            S.dma('sp', d_out, y_d[gt * 128:(gt + 1) * 128, :], x_sb[:, gt, :], reads=[bx[gt]])
        mm(PA[:, 0:128], Nst[:], idf, True, True, [b_Nst, b_cst], [b_PA])
        cp(tmpS[:, 0:128], PA[:, 0:128], [b_PA], [b_tmpS])
        S.dma('sp', d_out, on_d, tmpS[:, 0:128], reads=[b_tmpS])
        S.dma('sp', d_out, om_d, Mst[0:1, :], reads=[b_Mst])
        S.final_wait('sp')
        print("instructions", S.nins, "waits", S.nwait, {k: v for k, v in S.cnt.items() if not k.startswith('dma_')})
    return nc


_NC_CACHE = {}


def kernel(x_prompt, x_sample, state_mlstm_C, state_mlstm_n, state_mlstm_m, state_ret, state_gla, c, c_ctx,
           norm_g, w_ada, b_ada, w_in, mlstm_gate_b, ret_decay_logit, gla_w2, gla_b2, headnorm_g, w_out, final_g,
           _depth=DEPTH):
    f = lambda a: np.ascontiguousarray(np.asarray(a, dtype=np.float32))
    x_prompt, x_sample = f(x_prompt), f(x_sample)
    nc = build(_depth)
    cstv = make_consts()
    shared = {
        "norm_g": f(norm_g), "w_ada": f(w_ada), "b_ada": f(b_ada), "w_in": f(w_in),
        "gate_b": f(mlstm_gate_b).reshape(-1), "ret_logit": f(ret_decay_logit).reshape(-1),
        "gla_w2": f(gla_w2), "gla_b2": f(gla_b2).reshape(DEPTH, 256), "hn_g": f(headnorm_g),
        "w_out": f(w_out), "final_g": f(final_g), "cst": cstv,
    }
    in_maps = []
    for i in range(NCORES):
        xx = np.concatenate([x_prompt[4 * i:4 * i + 4].reshape(1024, D), x_sample[i]], axis=0)
        m = dict(shared)
        m.update({
            "x": np.ascontiguousarray(xx),
            "cvec": np.ascontiguousarray(np.stack([f(c_ctx), f(c)[i]], axis=0)),
            "sC": f(state_mlstm_C)[i], "sn": f(state_mlstm_n)[i], "sm": f(state_mlstm_m)[i].reshape(-1),
            "sret": f(state_ret)[i], "sgla": f(state_gla)[i],
        })
        in_maps.append(m)
    res = run_bass_kernel_spmd(nc, in_maps, core_ids=list(range(NCORES)))
    R = res.results
    y_prompt = np.stack([R[i]["y"][0:1024].reshape(4, 256, D) for i in range(NCORES)], 0).reshape(32, 256, D)
    y_sample = np.stack([R[i]["y"][1024:2048] for i in range(NCORES)], 0)
    oC = np.concatenate([R[i]["oC"] for i in range(NCORES)], 0)
    on = np.concatenate([R[i]["on"].reshape(4, DEPTH, 2, 4, 128) for i in range(NCORES)], 0)
    om = np.concatenate([R[i]["om"].reshape(4, DEPTH, 2, 4) for i in range(NCORES)], 0)
    oret = np.concatenate([R[i]["oret"] for i in range(NCORES)], 0)
    ogla = np.concatenate([R[i]["ogla"] for i in range(NCORES)], 0)
    return (y_prompt.astype(np.float32), y_sample.astype(np.float32), oC.astype(np.float32), on.astype(np.float32),
            om.astype(np.float32), oret.astype(np.float32), ogla.astype(np.float32))
```

```python
import numpy as np
from contextlib import ExitStack
import concourse.bass as bass
import concourse.mybir as mybir
from concourse.bass_utils import run_bass_kernel_spmd

F32 = mybir.dt.float32
BF16 = mybir.dt.bfloat16
AF = mybir.ActivationFunctionType
ALU = mybir.AluOpType
AX = mybir.AxisListType

DEPTH = 4
D = 1024
DIN = 4400
EPS = 1e-6
NCORES = 8
OQ, OK_, OV, OO, OZ, OG = 0, 512, 1024, 1536, 2048, 2560
BQ, BK, BV, BZ = 2576, 2832, 3088, 3344
CQ, CK, CV, CZ, CL = 3600, 3728, 3856, 4112, 4368

C_IDF, C_UINC, C_LINC, C_RIJ, C_RJI, C_POSF, C_POSB, C_ONES = [i * 128 for i in range(8)]
C_COLA, C_COLB = 1024, 1025
C_COS, C_SIN = 1026, 1026 + 256
C_BD2, C_BD4, C_MC2, C_MC4 = 1538, 1538 + 128, 1538 + 384, 1538 + 386
NCST = 1538 + 390


def make_consts():
    c = np.zeros((128, NCST), np.float32)
    p = np.arange(128)[:, None].astype(np.float32)
    f = np.arange(128)[None, :].astype(np.float32)
    c[:, C_IDF:C_IDF + 128] = (p == f)
    c[:, C_UINC:C_UINC + 128] = (p <= f)
    c[:, C_LINC:C_LINC + 128] = (p >= f)
    c[:, C_RIJ:C_RIJ + 128] = np.maximum(f - p, 0)
    c[:, C_RJI:C_RJI + 128] = np.maximum(p - f, 0)
    c[:, C_POSF:C_POSF + 128] = f + 1
    c[:, C_POSB:C_POSB + 128] = 128 - f
    c[:, C_ONES:C_ONES + 128] = 1.0
    c[:, C_COLA] = -(127 - p[:, 0])
    c[:, C_COLB] = -p[:, 0]
    L = 1024
    tok = np.arange(L)
    r = (tok // 64).astype(np.float32)
    col = (tok % 64).astype(np.float32)
    nf = 16
    freqs = (np.float32(10000.0) ** (-np.arange(nf, dtype=np.float32) / np.float32(nf))).astype(np.float32)
    ang = np.concatenate([r[:, None] * freqs, col[:, None] * freqs], axis=-1).astype(np.float32)
    cos = np.cos(ang).astype(np.float32).reshape(8, 128, 32).transpose(1, 0, 2).reshape(128, 256)
    sin = np.sin(ang).astype(np.float32).reshape(8, 128, 32).transpose(1, 0, 2).reshape(128, 256)
    pp = np.arange(128)[:, None]
    c[:, C_BD2:C_BD2 + 128] = (pp // 64 == (np.arange(128)[None, :] // 64))
    c[:, C_BD4:C_BD4 + 256] = (pp // 32 == (np.arange(256)[None, :] // 64))
    c[:, C_MC2:C_MC2 + 2] = (pp // 64 == np.arange(2)[None, :])
    c[:, C_MC4:C_MC4 + 4] = (pp // 32 == np.arange(4)[None, :])
    c[:, C_COS:C_COS + 256] = cos
    c[:, C_SIN:C_SIN + 256] = sin
    return c


import os as _os
ELIDE = int(_os.environ.get("MK_ELIDE", "7"))


class Buf:
    __slots__ = ("name", "w", "r", "excl")

    def __init__(self, name, excl=False):
        self.name = name
        self.w = None
        self.r = []
        self.excl = excl


class Sync:
    def __init__(self, nc, ctx):
        self.nc = nc
        self.eng = {'pe': nc.tensor, 'dve': nc.vector, 'act': nc.scalar, 'pool': nc.gpsimd, 'sp': nc.sync}
        self.sem, self.cnt, self.clk, self.hist, self.closed = {}, {}, {}, {}, {}
        for k in self.eng:
            self.sem[k] = ctx.enter_context(nc.semaphore("s_" + k))
            self.cnt[k] = 0
            self.clk[k] = {}
        self.ctx = ctx
        self.nwait = 0
        self.nins = 0

    def dma_sem(self, name):
        k = 'dma_' + name
        self.sem[k] = self.ctx.enter_context(self.nc.semaphore("s_" + k))
        self.cnt[k] = 0
        self.closed[k] = False
        return k

    def _deps(self, e, reads, writes):
        deps = {}
        if ELIDE in (3, 4, 5, 6, 7):
            return self._deps2(e, reads, writes)
        if ELIDE == 0:
            e = '?'
        elif ELIDE == 1 and e != 'pe':
            e = '?'
        for b in reads:
            if b.w is not None:
                s, v = b.w
                if not (s == e and e == 'pe') and deps.get(s, 0) < v:
                    deps[s] = v
            if b.excl:
                for (s, v) in b.r:
                    if s != e and deps.get(s, 0) < v:
                        deps[s] = v
        for b in writes:
            if b.w is not None:
                s, v = b.w
                if s != e and deps.get(s, 0) < v:
                    deps[s] = v
            for (s, v) in b.r:
                if s != e and deps.get(s, 0) < v:
                    deps[s] = v
        return deps

    def _deps2(self, e, reads, writes):
        deps = {}
        cur = self.cnt.get(e, 0)

        def add(s, v, kind):
            if s == e:
                if v > cur:
                    assert e == 'pe', (e, s, v, cur)
                    return
                if e == 'pe' and ELIDE in (5, 6, 7):
                    return
                if e != 'pe' and ELIDE in (4, 5, 7) and kind != 'raw':
                    return
                if e != 'pe' and ELIDE == 7 and v < cur:
                    return
            if deps.get(s, 0) < v:
                deps[s] = v
        for b in reads:
            if b.w is not None:
                add(b.w[0], b.w[1], 'raw')
            if b.excl:
                for (s, v) in b.r:
                    if s != e:
                        add(s, v, 'rr')
        for b in writes:
            if b.w is not None:
                add(b.w[0], b.w[1], 'waw')
            for (s, v) in b.r:
                add(s, v, 'war')
        return deps

    def _wait1(self, e, s, v):
        clk = self.clk[e]
        if s.startswith('dma_'):
            v = self.cnt[s]
            self.closed[s] = True
        if clk.get(s, 0) >= v:
            return
        assert v <= self.cnt[s], ("wait on not-yet-signalled instruction", e, s, v, self.cnt[s])
        self.eng[e].wait_ge(self.sem[s], v)
        self.nwait += 1
        h = self.hist.get((s, v))
        if h:
            for a, b in h.items():
                if clk.get(a, 0) < b:
                    clk[a] = b
        clk[s] = v

    def _wait(self, e, deps):
        for s, v in deps.items():
            self._wait1(e, s, v)

    @staticmethod
    def _flat(xs):
        out = []
        for x in xs:
            if isinstance(x, (list, tuple)):
                out.extend(Sync._flat(x))
            else:
                out.append(x)
        return out

    def op(self, e, fn, reads=(), writes=(), inc=True):
        reads, writes = self._flat(reads), self._flat(writes)
        self._wait(e, self._deps(e, reads, writes))
        ins = fn(self.eng[e])
        self.nins += 1
        if not inc:
            v = self.cnt[e] + 1
            for b in reads:
                b.r.append((e, v))
            for b in writes:
                b.w = (e, v)
                b.r = []
            return ins
        self.cnt[e] += 1
        v = self.cnt[e]
        ins.then_inc(self.sem[e], 1)
        snap = dict(self.clk[e])
        snap[e] = v
        self.hist[(e, v)] = snap
        for b in reads:
            b.r.append((e, v))
        for b in writes:
            b.w = (e, v)
            b.r = []
        return ins

    def dma(self, e, dsem, out, in_, reads=(), writes=(), slow=False):
        reads, writes = self._flat(reads), self._flat(writes)
        self._wait(e, self._deps(e, reads, writes))
        if self.closed[dsem] and self.clk[e].get(dsem, 0) < self.cnt[dsem]:
            self._wait1(e, dsem, self.cnt[dsem])
        self.closed[dsem] = False
        if slow:
            ins = self.eng[e].dma_start(out=out, in_=in_, allow_slow_non_contiguous=True)
        else:
            ins = self.eng[e].dma_start(out=out, in_=in_)
        self.cnt[dsem] += 16
        v = self.cnt[dsem]
        ins.then_inc(self.sem[dsem], 16)
        self.nins += 1
        self.hist[(dsem, v)] = dict(self.clk[e])
        for b in reads:
            b.r.append((dsem, v))
        for b in writes:
            b.w = (dsem, v)
            b.r = []
        return ins

    def final_wait(self, e):
        for s in self.sem:
            if s.startswith('dma_') and self.cnt[s] > 0 and self.clk[e].get(s, 0) < self.cnt[s]:
                self.eng[e].wait_ge(self.sem[s], self.cnt[s])


import os
STAGE = int(os.environ.get("MK_STAGE", "99"))
SUB = int(os.environ.get("MK_SUB", "99"))
MKX = int(os.environ.get("MK_X", "0"))
ELIDE = int(os.environ.get("MK_ELIDE", "7"))


PHASES = []


def build(depth=DEPTH):
    nc = bass.Bass("TRN2", target_bir_lowering=False)
    din = lambda n, s: nc.dram_tensor(n, list(s), F32, kind="ExternalInput").ap()
    dout = lambda n, s: nc.dram_tensor(n, list(s), F32, kind="ExternalOutput").ap()
    x_d = din("x", (2048, D))
    cvec_d = din("cvec", (2, D))
    sC_d = din("sC", (DEPTH, 2, 4, 128, 128))
    sn_d = din("sn", (DEPTH, 2, 4, 128))
    sm_d = din("sm", (DEPTH * 8,))
    sret_d = din("sret", (DEPTH, 2, 4, 64, 64))
    sgla_d = din("sgla", (DEPTH, 2, 4, 32, 64))
    normg_d = din("norm_g", (DEPTH, D))
    wada_d = din("w_ada", (DEPTH, D, 3 * D))
    bada_d = din("b_ada", (DEPTH, 3 * D))
    win_d = din("w_in", (DEPTH, D, DIN))
    gateb_d = din("gate_b", (DEPTH * 16,))
    retl_d = din("ret_logit", (DEPTH * 8,))
    w2_d = din("gla_w2", (DEPTH, 2, 16, 128))
    b2_d = din("gla_b2", (DEPTH, 256))
    hng_d = din("hn_g", (DEPTH, D))
    wout_d = din("w_out", (DEPTH, D, D))
    fing_d = din("final_g", (D,))
    cst_d = din("cst", (128, NCST))
    y_d = dout("y", (2048, D))
    oC_d = dout("oC", (4, DEPTH, 2, 4, 128, 128))
    on_d = dout("on", (128, 128))
    om_d = dout("om", (1, 128))
    oret_d = dout("oret", (4, DEPTH, 2, 4, 64, 64))
    ogla_d = dout("ogla", (4, DEPTH, 2, 4, 32, 64))

    with ExitStack() as ctx:
        S = Sync(nc, ctx)

        def T(name, shape, dt=F32):
            return ctx.enter_context(nc.sbuf_tensor("sb_" + name, list(shape), dt)), Buf(name)

        def PS(name, shape, dt=F32):
            return ctx.enter_context(nc.psum_tensor("ps_" + name, list(shape), dt))

        x_sb, _ = T("x_sb", (128, 16, D))
        bx = [Buf("x%d" % i) for i in range(16)]
        hT, _ = T("hT", (128, 8, 1024), BF16)
        b_hT = [Buf("hT0"), Buf("hT1")]
        yT, _ = T("yT", (128, 8, 1024), BF16)
        b_yT = [Buf("yT%d" % i) for i in range(8)]
        NSLOT = 4
        Wr = []
        for s in range(NSLOT):
            t, b = T("wr%d" % s, (128, 8, 528), BF16)
            Wr.append((t, b, S.dma_sem("wr%d" % s)))
        ada = []
        for s in range(2):
            t, b = T("ada%d" % s, (128, 8, 128), BF16)
            ada.append((t, b, S.dma_sem("ada%d" % s)))
        cst, b_cst = T("cst", (128, NCST))
        idb, b_idb = T("idb", (128, 128), BF16)
        uincb, b_uincb = T("uincb", (128, 128), BF16)
        lincb, b_lincb = T("lincb", (128, 128), BF16)
        gate_bc, b_gate = T("gate_bc", (128, 2, D))
        bg_bc, b_bg = T("bg_bc", (128, D))
        ngT, b_ngT = T("ngT", (128, DEPTH, 8))
        hngT, b_hngT = T("hngT", (128, DEPTH, 8))
        bshT, b_bshT = T("bshT", (128, DEPTH, 8))
        bscT, b_bscT = T("bscT", (128, DEPTH, 8))
        cT, b_cT = T("cT", (128, 8, 2))
        scT, b_scT = T("scT", (128, 8, 2), BF16)
        screp, b_screp = T("screp", (128, 2, 8, 128), BF16)
        gsT2, _ = T("gsT", (128, 2, 8, 2))
        shT2, _ = T("shT", (128, 2, 8, 2))
        b_gsTp = [Buf("gsT0"), Buf("gsT1")]
        b_shTp = [Buf("shT0"), Buf("shT1")]
        modraw, b_modraw = T("modraw", (128, 16, 2))
        gb_bc, b_gb = T("gb_bc", (128, DEPTH * 16))
        m0_bc, b_m0 = T("m0_bc", (128, DEPTH * 8))
        rl_bc, b_rl = T("rl_bc", (128, DEPTH * 8))
        ss, b_ss = T("ss", (128, 16))
        rstd, b_rstd = T("rstd", (128, 16))
        Gs, b_Gs = T("Gs", (128, 8, 16))
        L1g, b_L1g = T("L1g", (128, 2, 8, 4))
        Fp, b_Fp = T("Fp", (128, 8, 16))
        ug, b_ug = T("ug", (128, 8, 8))
        umax, b_umax = T("umax", (64, 1))
        Abc, b_Abc = T("Abc", (128, 8, 8))
        MP, b_MP = T("MP", (128, 8, 8))
        Sg, b_Sg = T("Sg", (128, 8, 8))
        Mfin, b_Mfin = T("Mfin", (128, 8))
        rg, b_rg = T("rg", (128, 8, 8))
        wg, b_wg = T("wg", (128, 8, 8))
        flo, b_flo = T("flo", (128, 8, 8))
        tmpg, b_tmpg = T("tmpg", (128, 8, 8))
        Mst, b_Mst = T("Mst", (128, 128))
        Nst, b_Nst = T("Nst", (128, 128))
        ktok, _ = T("ktok", (128, 8, 256), BF16)
        b_ktok = [Buf("ktok%d" % i) for i in range(8)]
        vt_f, _ = T("vt_f", (128, 8, 2, 130), BF16)
        b_vtf = [Buf("vtf%d" % i) for i in range(8)]
        b_vtb = [Buf("vtb%d" % i) for i in range(8)]
        b_vtok = [Buf("vtok%d" % i) for i in range(8)]
        vt_b, _ = T("vt_b", (128, 8, 2, 130), BF16)
        vtok, _ = T("vtok", (128, 8, 256), BF16)
        Sb16, b_Sb16 = T("Sb16", (128, 8, 2, 130), BF16)
        Cst = []
        for i in range(4):
            t, b = T("Cst%d" % i, (128, 260))
            Cst.append((t, b, S.dma_sem("cst%d" % i)))
        C16, b_C16 = T("C16", (128, 2, 260), BF16)
        qtok, b_qtok = T("qtok", (128, 256), BF16)
        Esb, _ = T("Esb", (128, 2, 512))
        b_Esb = [[Buf("Esb0lo"), Buf("Esb0hi")], [Buf("Esb1lo"), Buf("Esb1hi")]]
        zsb, _ = T("zsb", (128, 2, 256))
        b_zsb = [Buf("zsb0"), Buf("zsb1")]
        qkT, b_qkT = T("qkT", (128, 4, 128), BF16)
        qhT, b_qhT = T("qhT", (128, 2, 2, 128), BF16)
        Pf, b_Pf = T("Pf", (128, 4, 128), BF16)
        Pb, b_Pb = T("Pb", (128, 4, 128), BF16)
        dn, _ = T("dn", (128, 8))
        b_dn = [Buf("dn_f"), Buf("dn_b")]
        ya, _ = T("ya", (128, 256))
        b_ya = [Buf("ya0"), Buf("ya1")]
        ssy, _ = T("ssy", (128, 8))
        b_ssy = [Buf("ssy%d" % i) for i in range(8)]
        ybf, b_ybf = T("ybf", (128, 256), BF16)
        rot1, b_rot1 = T("rot1", (128, 4, 32))
        rot2, b_rot2 = T("rot2", (128, 4, 32))
        L1r, b_L1r = T("L1r", (128, 8))
        nL1r, b_nL1r = T("nL1r", (128, 8))
        wend, b_wend = T("wend", (128, 2, 4))
        gC, b_gC = T("gC", (128, 8))
        gblk, b_gblk = T("gblk", (128, 2, 2))
        nLblk, b_nLblk = T("nLblk", (128, 2, 2))
        Wq, b_Wq = T("Wq", (128, 2, 2, 128))
        Mret, b_Mret = T("Mret", (128, 4, 128))
        w2s, b_w2s = T("w2s", (33, 256))
        w2x, b_w2x = T("w2x", (33, 256), BF16)
        clrT, b_clrT = T("clrT", (33, 128), BF16)
        Gc, b_Gc = T("Gc", (128, 8, 2))
        xn_parts = [(ktok[:].rearrange("p t c -> p (t c)").rearrange("p (a f) -> p a f", a=2), b_ktok),
                    (vtok[:].rearrange("p t c -> p (t c)").rearrange("p (a f) -> p a f", a=2), b_vtok)]
        EBd = [(vt_f[:].rearrange("p t j n -> p (t j n)").bitcast(F32)[:, 0:1024].rearrange("p (t n) -> p t n", t=8), b_vtf),
               (vt_b[:].rearrange("p t j n -> p (t j n)").bitcast(F32)[:, 0:1024].rearrange("p (t n) -> p t n", t=8), b_vtb)]
        junk, b_junk = ybf, b_ybf
        udiag, b_udiag = ya[0:64, 0:64], b_ya[0]
        tmpS, b_tmpS = zsb[:, 0, :], b_zsb[0]
        tm1, b_tm1 = ya[:, 0:128], b_ya[0]
        tm2, b_tm2 = rot1[:].rearrange("p a b -> p (a b)"), b_rot1
        EBn, b_EBn = ya, b_ya
        L1c, b_L1c = Esb[:, 0, 0:256], b_Esb[0][0]
        fin_bc, b_fin = bg_bc, b_bg
        d_cst = S.dma_sem("cst")
        d_x = S.dma_sem("x")
        d_misc = S.dma_sem("misc")
        d_out = S.dma_sem("out")
        d_st = S.dma_sem("stout")

        PA = PS("PA", (128, 512)); b_PA = Buf("PA", True)
        PB = PS("PB", (128, 512)); b_PB = Buf("PB", True)
        PT0 = PS("PT0", (128, 1024), BF16); b_PT0 = Buf("PT0", True)
        PT1 = PS("PT1", (128, 1024), BF16); b_PT1 = Buf("PT1", True)
        PSc = PS("PSc", (128, 512)); b_PSc = Buf("PSc", True)
        PO1 = PS("PO1", (128, 512)); b_PO1 = Buf("PO1", True)
        PO2 = PS("PO2", (128, 512)); b_PO2 = Buf("PO2", True)
        PX = PS("PX", (128, 512)); b_PU = Buf("PX", True); b_PM = b_PU
        PU = PX[:, 0:260]
        PM = PX[:, 260:512]
        PTs = [(PT0, b_PT0), (PT1, b_PT1)]
        pt_i = [0]

        def nextPT():
            pt_i[0] ^= 1
            return PTs[pt_i[0]]

        ctx.enter_context(nc.Block())

        def ACT(fn, R, W): return S.op('act', fn, R, W)
        def DVE(fn, R, W): return S.op('dve', fn, R, W)
        def PE(fn, R, W, inc=True): return S.op('pe', fn, R, W, inc)

        def mm(out, lhsT, rhs, start, stop, R, W, tp=None, inc=True):
            if tp is not None:
                return PE(lambda e: e.matmul(out, lhsT=lhsT, rhs=rhs, start=start, stop=stop, tile_position=tp), R, W, inc)
            return PE(lambda e: e.matmul(out, lhsT=lhsT, rhs=rhs, start=start, stop=stop), R, W, inc)

        def tr(out, in_, ident, R, W, inc=True):
            return PE(lambda e: e.transpose(out, in_, ident), R, W, inc)

        def act(out, in_, func, R, W, scale=1.0, bias=0.0, accum=None):
            if accum is None:
                return ACT(lambda e: e.activation(out=out, in_=in_, func=func, scale=scale, bias=bias), R, W)
            return ACT(lambda e: e.activation(out=out, in_=in_, func=func, scale=scale, bias=bias, accum_out=accum), R, W)

        def tt(out, in0, in1, op, R, W):
            return DVE(lambda e: e.tensor_tensor(out=out, in0=in0, in1=in1, op=op), R, W)

        def ts(out, in0, s1, s2, op0, R, W, op1=None):
            if op1 is None:
                return DVE(lambda e: e.tensor_scalar(out, in0, s1, None, op0=op0), R, W)
            return DVE(lambda e: e.tensor_scalar(out, in0, s1, s2, op0=op0, op1=op1), R, W)

        def stt(out, in0, scalar, in1, op0, op1, R, W):
            return DVE(lambda e: e.scalar_tensor_tensor(out=out, in0=in0, scalar=scalar, in1=in1, op0=op0, op1=op1), R, W)

        def cp(out, in_, R, W):
            return DVE(lambda e: e.tensor_copy(out, in_), R, W)

        def rsq(out, in_, n, R, W, tmp):
            act(tmp, in_, AF.Ln, R, [W[0]], scale=1.0 / n, bias=EPS)
            act(out, tmp, AF.Exp, [W[0]], W, scale=-0.5)

        cI = lambda c0, n=128: cst[:, c0:c0 + n]
        idf = cI(C_IDF)
        ones = cI(C_ONES)

        S.dma('sp', d_cst, cst[:], cst_d, writes=[b_cst])
        for g in range(4):
            S.dma('sp', d_x, x_sb[:, g * 4:(g + 1) * 4, :],
                  x_d[g * 512:(g + 1) * 512, :].rearrange("(t p) f -> p t f", p=128), writes=bx[g * 4:(g + 1) * 4])
        S.dma('sp', d_misc, gb_bc[:], gateb_d.partition_broadcast(128), writes=[b_gb])
        S.dma('sp', d_misc, m0_bc[:], sm_d.partition_broadcast(128), writes=[b_m0])
        S.dma('sp', d_misc, rl_bc[:], retl_d.partition_broadcast(128), writes=[b_rl])
        for v in range(2):
            S.dma('sp', d_misc, cT[:, :, v], cvec_d[v].rearrange("(kc p) -> p kc", p=128), writes=[b_cT], slow=True)
        for (t_, b_, src, off) in ((ngT, b_ngT, normg_d, 0), (hngT, b_hngT, hng_d, 0),
                                   (bshT, b_bshT, bada_d, 0), (bscT, b_bscT, bada_d, D)):
            for l in range(DEPTH):
                S.dma('sp', d_misc, t_[:, l, :], src[l, off:off + D].rearrange("(kc p) -> p kc", p=128), writes=[b_], slow=True)
        cp(idb[:], idf, [b_cst], [b_idb])
        cp(uincb[:], cI(C_UINC), [b_cst], [b_uincb])
        cp(lincb[:], cI(C_LINC), [b_cst], [b_lincb])
        DVE(lambda e: e.memset(Mst[:], 0.0), [], [b_Mst])
        DVE(lambda e: e.memset(Nst[:], 0.0), [], [b_Nst])
        DVE(lambda e: e.memset(w2s[:], 0.0), [], [b_w2s])
        DVE(lambda e: e.memset(clrT[:], 1.0), [], [b_clrT])
        for i in range(4):
            DVE(lambda e: e.memset(Cst[i][0][:], 0.0), [], [Cst[i][1]])
        act(tmpS[:, 0:16], cT[:].rearrange("p k v -> p (k v)"), AF.Exp, [b_cT], [b_tmpS], scale=-1.0)
        ts(tmpS[:, 0:16], tmpS[:, 0:16], 1.0, None, ALU.add, [b_tmpS], [b_tmpS])
        DVE(lambda e: e.reciprocal(tmpS[:, 0:16], tmpS[:, 0:16]), [b_tmpS], [b_tmpS])
        tt(scT[:].rearrange("p k v -> p (k v)"), tmpS[:, 0:16], cT[:].rearrange("p k v -> p (k v)"), ALU.mult, [b_tmpS, b_cT], [b_scT])
        for v in range(2):
            cp(screp[:, v, :, :], scT[:, :, v:v + 1].to_broadcast([128, 8, 128]), [b_scT], [b_screp])

        ring_state = {'n': 0}

        def load_block(l, pieces, wsrc):
            s = ring_state['n'] % NSLOT
            ring_state['n'] += 1
            wt, wb_, ws = Wr[s]
            c = 0
            for (c0, n) in pieces:
                S.dma('pool', ws, wt[:, :, c:c + n], wsrc[l, :, c0:c0 + n].rearrange("(kc p) n -> p kc n", p=128), writes=[wb_])
                c += n
            return wt, wb_

        def proj(ps, psb, tloc, wt, wb_, c0, n, hb):
            toks = slice(tloc * 128, (tloc + 1) * 128)
            for kc in range(8):
                mm(ps, hT[:, kc, toks], wt[:, kc, c0:c0 + n], kc == 0, kc == 7, [hb, wb_], [psb], inc=(kc == 7))

        ada_state = {'n': 0}

        ada_q = []

        def ada_prefetch(l, blk):
            sl = ada_state['n'] % 2
            ada_state['n'] += 1
            at, ab, asem = ada[sl]
            S.dma('pool', asem, at[:], wada_d[l, :, blk * 128:(blk + 1) * 128].rearrange("(kc p) n -> p kc n", p=128), writes=[ab])
            ada_q.append((l, blk, at, ab))

        def ada_consume(l, blk):
            l_, blk_, at, ab = ada_q.pop(0)
            assert (l_, blk_) == (l, blk), (l_, blk_, l, blk)
            if blk < 16:
                for kc in range(8):
                    mm(PM[:, 0:2], at[:, kc, :], scT[:, kc, :], kc == 0, kc == 7, [ab, b_scT], [b_PM], inc=(kc == 7))
                cp(modraw[:, blk, :], PM[:, 0:2], [b_PM], [b_modraw])
            else:
                cols = slice((blk - 16) * 128, (blk - 15) * 128)
                for v in range(2):
                    for kc in range(8):
                        mm(PO1[:, v * 128:(v + 1) * 128], screp[:, v, kc, :], at[:, kc, :], kc == 0, kc == 7, [ab, b_screp], [b_PO1], inc=(kc == 7))
                tt(gate_bc[:, :, cols], PO1[:, 0:256].rearrange("p (v n) -> p v n", v=2), bg_bc[:, cols].unsqueeze(1).to_broadcast([128, 2, 128]), ALU.add,
                   [b_PO1, b_bg], [b_gate])

        def mod_ss_finish(l):
            par = l % 2
            gsT, shT = gsT2[:, par], shT2[:, par]
            tt(shT, modraw[:, 0:8, :], bshT[:, l, :].unsqueeze(2).to_broadcast([128, 8, 2]), ALU.add, [b_modraw, b_bshT], [b_shTp[par]])
            tt(gsT, modraw[:, 8:16, :], bscT[:, l, :].unsqueeze(2).to_broadcast([128, 8, 2]), ALU.add, [b_modraw, b_bscT], [b_gsTp[par]])
            ts(gsT, gsT, 1.0, None, ALU.add, [b_gsTp[par]], [b_gsTp[par]])
            tt(gsT, gsT, ngT[:, l, :].unsqueeze(2).to_broadcast([128, 8, 2]), ALU.mult, [b_gsTp[par], b_ngT], [b_gsTp[par]])

        def mod_stream(l, blks):
            blks = list(blks)
            state = {'i': 0}
            for b_ in blks[0:2]:
                ada_prefetch(l, b_)

            def step():
                i = state['i']
                if i >= len(blks):
                    return
                ada_consume(l, blks[i])
                if i + 2 < len(blks):
                    ada_prefetch(l, blks[i + 2])
                state['i'] = i + 1
            return step

        def load_bg(l):
            S.dma('sp', d_misc, bg_bc[:], bada_d[l, 2 * D:3 * D].partition_broadcast(128), writes=[b_bg])

        def emit_ret_statics(l):
            lg = rl_bc[:, l * 8:(l + 1) * 8]
            act(L1r[:], lg, AF.Exp, [b_rl], [b_L1r], scale=-1.0)
            act(L1r[:], L1r[:], AF.Ln, [b_L1r], [b_L1r], bias=1.0)
            ts(nL1r[:], L1r[:], -1.0, None, ALU.mult, [b_L1r], [b_nL1r])
            act(wend[:, 0, :], L1r[:, 0:4], AF.Exp, [b_L1r, b_cst], [b_wend], scale=cst[:, C_COLA:C_COLA + 1])
            act(wend[:, 1, :], L1r[:, 4:8], AF.Exp, [b_L1r, b_cst], [b_wend], scale=cst[:, C_COLB:C_COLB + 1])
            ts(wend[:].rearrange("p d h -> p (d h)"), wend[:].rearrange("p d h -> p (d h)"), 0.125, None, ALU.mult, [b_wend], [b_wend])
            act(gC[:], L1r[:], AF.Exp, [b_L1r], [b_gC], scale=-128.0)
            for d in range(2):
                for blk in range(2):
                    for hh in range(2):
                        pr = slice(hh * 64, (hh + 1) * 64)
                        h = blk * 2 + hh
                        cp(gblk[pr, d, blk:blk + 1], gC[pr, d * 4 + h:d * 4 + h + 1], [b_gC], [b_gblk])
                        cp(nLblk[pr, d, blk:blk + 1], nL1r[pr, d * 4 + h:d * 4 + h + 1], [b_nL1r], [b_nLblk])
            for d in range(2):
                for blk in range(2):
                    act(Wq[:, d, blk, :], cI(C_POSF if d == 0 else C_POSB), AF.Exp, [b_cst, b_nLblk], [b_Wq], scale=nLblk[:, d, blk:blk + 1])
            for h in range(4):
                act(tm1, cI(C_RIJ), AF.Exp, [b_cst, b_nL1r], [b_tm1], scale=nL1r[:, h:h + 1])
                act(tm2, cI(C_RJI), AF.Exp, [b_cst, b_nL1r], [b_tm2], scale=nL1r[:, 4 + h:5 + h])
                tt(tm1, tm1, cI(C_UINC), ALU.mult, [b_tm1, b_cst], [b_tm1])
                tt(tm2, tm2, cI(C_LINC), ALU.mult, [b_tm2, b_cst], [b_tm2])
                tt(Mret[:, h, :], tm1, tm2, ALU.add, [b_tm1, b_tm2], [b_Mret])
                ts(Mret[:, h, :], Mret[:, h, :], 0.125, None, ALU.mult, [b_Mret], [b_Mret])

        def emit_gla_statics(l):
            S.dma('sp', d_misc, w2s[0:16, 0:128], w2_d[l, 0], writes=[b_w2s])
            S.dma('sp', d_misc, w2s[16:32, 128:256], w2_d[l, 1], writes=[b_w2s])
            S.dma('sp', d_misc, w2s[32:33, :], b2_d[l:l + 1, :], writes=[b_w2s])
            cp(w2x[:], w2s[:], [b_w2s], [b_w2x])

        def emit_norm(l, half):
            v = half
            gsT, shT = gsT2[:, l % 2], shT2[:, l % 2]
            b_gsT, b_shT = b_gsTp[l % 2], b_shTp[l % 2]
            for t in range(8):
                gt = half * 8 + t
                act(hT[:, t, :], x_sb[:, gt, :], AF.Square, [bx[gt]], [b_hT[0], b_hT[1], b_ss], accum=ss[:, t:t + 1])
            rsq(rstd[:, 0:8], ss[:, 0:8], float(D), [b_ss], [b_rstd], ss[:, 8:16])
            for g in range(2):
                for tl in range(4):
                    t = g * 4 + tl
                    gt = half * 8 + t
                    xp, xb_ = xn_parts[tl // 2]
                    ts(xp[:, tl % 2, :], x_sb[:, gt, :], rstd[:, t:t + 1], None, ALU.mult, [bx[gt], b_rstd], [xb_])
                for kc in range(8):
                    pt, ptb = nextPT()
                    for tl in range(4):
                        xp, xb_ = xn_parts[tl // 2]
                        tr(pt[:, tl * 128:(tl + 1) * 128], xp[:, tl % 2, kc * 128:(kc + 1) * 128], idb[:], [xb_, b_idb], [ptb], inc=(tl == 3))
                    dst = hT[:, kc, g * 512:(g + 1) * 512]
                    if kc % 2 == 0:
                        act(dst, pt[:, 0:512], AF.Identity, [ptb, b_gsT, b_shT], [b_hT[g]], scale=gsT[:, kc, v:v + 1], bias=shT[:, kc, v:v + 1])
                    else:
                        ts(dst, pt[:, 0:512], gsT[:, kc, v:v + 1], shT[:, kc, v:v + 1], ALU.mult, [ptb, b_gsT, b_shT], [b_hT[g]], op1=ALU.add)

        hb_of = lambda t: b_hT[t // 4]

        def sig_inplace(par, n, lo=0):
            bb = b_Esb[par][lo // 256:(lo + n + 255) // 256]
            act(Esb[:, par, lo:lo + n], Esb[:, par, lo:lo + n], AF.Ln, bb, bb, bias=1.0)
            act(Esb[:, par, lo:lo + n], Esb[:, par, lo:lo + n], AF.Exp, bb, bb, scale=-1.0)

        def finish_y(l, t, par, src, nh, dh, kc0, zoff):
            n = nh * dh
            for h in range(nh):
                act(junk[:, h * dh:(h + 1) * dh], src[:, h * dh:(h + 1) * dh], AF.Square, [b_ya[(h * dh) // 128]], [b_junk, b_ssy[h]], accum=ssy[:, h:h + 1])
            rsq(ssy[:, 0:nh], ssy[:, 0:nh], float(dh), b_ssy[0:nh], b_ssy[0:nh], ssy[:, 4:4 + nh])
            tt(zsb[:, par, 0:n], zsb[:, par, 0:n], Esb[:, par, zoff:zoff + n], ALU.mult, [b_zsb[par], b_Esb[par][zoff // 256]], [b_zsb[par]])
            tt(src, src, zsb[:, par, 0:n], ALU.mult, [b_ya, b_zsb[par]], [b_ya])
            tt(ybf[:, 0:n].rearrange("p (h e) -> p h e", h=nh), src.rearrange("p (h e) -> p h e", h=nh),
               ssy[:, 0:nh].unsqueeze(2).to_broadcast([128, nh, dh]), ALU.mult, [b_ya, b_ssy[0:nh]], [b_ybf])

        def y_transpose(l, t, kc0):
            pt, ptb = nextPT()
            for j in range(2):
                tr(pt[:, j * 128:(j + 1) * 128], ybf[:, j * 128:(j + 1) * 128], idb[:], [b_ybf, b_idb], [ptb], inc=(j == 1))
            for j in range(2):
                act(yT[:, kc0 + j, t * 128:(t + 1) * 128], pt[:, j * 128:(j + 1) * 128], AF.Identity, [ptb, b_hngT], [b_yT[t]],
                    scale=hngT[:, l, kc0 + j:kc0 + j + 1])

        seqs_of = lambda half: [(0, 2), (2, 4), (4, 6), (6, 8)] if half == 0 else [(0, 8)]

        pending_tail = []

        def flush_tail():
            while pending_tail:
                pending_tail.pop(0)()

        def run_pass2(half, Fp, Feq, Fer, M1, M2, B1, B2, Yt, seq_begin, seq_end):
            starts = {t0: si for si, (t0, t1) in enumerate(seqs_of(half))}
            ends = {t1 - 1: si for si, (t0, t1) in enumerate(seqs_of(half))}
            Fp(0)
            Feq(0)
            M1(0)
            Fer(0)
            Fp(1)
            for c in range(8):
                if c in starts:
                    seq_begin(starts[c])
                M2(c)
                if c in ends:
                    seq_end(ends[c])
                if c + 1 < 8:
                    Feq(c + 1)
                    M1(c + 1)
                B1(c)
                if c + 1 < 8:
                    Fer(c + 1)
                if c + 2 < 8:
                    Fp(c + 2)
                B2(c)
                if c < 7:
                    Yt(c)
                else:
                    pending_tail.append(lambda: Yt(7))

        def emit_gates(l, half):
            for d in range(2):
                act(L1g[:, d, :, :], Gs[:, :, d * 8 + 4:d * 8 + 8], AF.Exp, [b_Gs], [b_L1g], scale=-1.0)
                yield
            l1flat = L1g[:].rearrange("p d t h -> p (d t h)")
            act(l1flat, l1flat, AF.Ln, [b_L1g], [b_L1g], bias=1.0)
            yield
            mm(PM[:, 0:32], cI(C_UINC), l1flat[:, 0:32], True, True, [b_cst, b_L1g], [b_PM], inc=False)
            yield
            mm(PM[:, 32:64], cI(C_LINC), l1flat[:, 32:64], True, True, [b_cst, b_L1g], [b_PM], inc=False)
            yield
            mm(PM[:, 64:128], ones, l1flat, True, True, [b_cst, b_L1g], [b_PM])
            yield
            for d in range(2):
                cp(Fp[:, :, d * 4:(d + 1) * 4], PM[:, d * 32:(d + 1) * 32].rearrange("p (t h) -> p t h", t=8), [b_PM], [b_Fp])
                yield
                cp(Fp[:, :, 8 + d * 4:12 + d * 4], PM[:, 64 + d * 32:96 + d * 32].rearrange("p (t h) -> p t h", t=8), [b_PM], [b_Fp])
                yield
            for d in range(2):
                tt(ug[:, :, d * 4:(d + 1) * 4], Fp[:, :, d * 4:(d + 1) * 4], Gs[:, :, d * 8:d * 8 + 4], ALU.add, [b_Fp, b_Gs], [b_ug])
                yield
            PE(lambda e: e.transpose(PM[0:64, 0:128], ug[:].rearrange("p t g -> p (t g)"), idf), [b_ug, b_cst], [b_PM])
            yield
            DVE(lambda e: e.reduce_max(umax[:], PM[0:64, 0:128], axis=AX.X), [b_PM], [b_umax])
            yield
            ts(udiag, cst[0:64, C_IDF:C_IDF + 64], umax[:, 0:1], None, ALU.mult, [b_cst, b_umax], [b_udiag])
            yield
            mm(PM[:, 128:192], cst[0:64, C_ONES:C_ONES + 128], udiag, True, True, [b_cst, b_udiag], [b_PM])
            yield
            cp(Abc[:].rearrange("p t g -> p (t g)"), PM[:, 128:192], [b_PM], [b_Abc])
            yield
            if half == 0:
                v4 = lambda X: X[:].rearrange("p (s k) g -> p s k g", k=2)
                mst = Mst[:].rearrange("p (s r) -> p s r", s=4)
                for d in range(2):
                    sl = slice(d * 4, (d + 1) * 4)
                    tl = slice(8 + d * 4, 12 + d * 4)
                    k0, k1 = (0, 1) if d == 0 else (1, 0)
                    DVE(lambda e: e.memset(v4(MP)[:, :, k0, sl], 0.0), [], [b_MP])
                    yield
                    tt(v4(Sg)[:, :, k0, sl], v4(MP)[:, :, k0, sl], v4(Abc)[:, :, k0, sl], ALU.max, [b_MP, b_Abc], [b_Sg])
                    yield
                    tt(v4(MP)[:, :, k1, sl], v4(Sg)[:, :, k0, sl], v4(Fp)[:, :, k0, tl], ALU.subtract, [b_Sg, b_Fp], [b_MP])
                    yield
                    tt(v4(Sg)[:, :, k1, sl], v4(MP)[:, :, k1, sl], v4(Abc)[:, :, k1, sl], ALU.max, [b_MP, b_Abc], [b_Sg])
                    yield
                    tt(mst[:, :, l * 8 + d * 4:l * 8 + d * 4 + 4], v4(Sg)[:, :, k1, sl], v4(Fp)[:, :, k1, tl], ALU.subtract, [b_Sg, b_Fp], [b_Mst])
                    yield
            else:
                orders = [list(range(8)), list(range(7, -1, -1))]
                for d in range(2):
                    sl = slice(d * 4, (d + 1) * 4)
                    cp(MP[:, orders[d][0], sl], m0_bc[:, l * 8 + d * 4:l * 8 + d * 4 + 4], [b_m0], [b_MP])
                    yield
                for i in range(8):
                    for d in range(2):
                        sl = slice(d * 4, (d + 1) * 4)
                        tl = slice(8 + d * 4, 12 + d * 4)
                        c = orders[d][i]
                        tt(Sg[:, c, sl], MP[:, c, sl], Abc[:, c, sl], ALU.max, [b_MP, b_Abc], [b_Sg])
                        yield
                        if i + 1 < 8:
                            tt(MP[:, orders[d][i + 1], sl], Sg[:, c, sl], Fp[:, c, tl], ALU.subtract, [b_Sg, b_Fp], [b_MP])
                            yield
            tt(tmpg[:], MP[:], Sg[:], ALU.subtract, [b_MP, b_Sg], [b_tmpg])
            yield
            act(rg[:], tmpg[:], AF.Exp, [b_tmpg], [b_rg])
            yield
            tt(tmpg[:], ug[:], Sg[:], ALU.subtract, [b_ug, b_Sg], [b_tmpg])
            yield
            act(wg[:], tmpg[:], AF.Exp, [b_tmpg], [b_wg])
            yield
            tt(tmpg[:], Fp[:, :, 0:8], Sg[:], ALU.subtract, [b_Fp, b_Sg], [b_tmpg])
            yield
            act(flo[:], tmpg[:], AF.Exp, [b_tmpg], [b_flo])
            yield

        def state_init_A(l, half, j, d, h, slot=None):
            st, sb, ssem = Cst[j * 2 + (d if slot is None else slot)]
            if half == 0:
                DVE(lambda e: e.memset(st[:, 0:130], 0.0), [], [sb])
            else:
                S.dma('sp', ssem, st[:, 0:128], sC_d[l, d, h], writes=[sb])
                S.dma('sp', ssem, st[:, 128:129], sn_d[l, d, h].rearrange("(p o) -> p o", o=1), writes=[sb], slow=True)

        def state_out_A(l, si, j, d, h, slot=None):
            st, sb, ssem = Cst[j * 2 + (d if slot is None else slot)]
            S.dma('sp', ssem, oC_d[si, l, d, h], st[:, 0:128], reads=[sb])
            col = ((si * DEPTH + l) * 2 + d) * 4 + h
            cp(Nst[:, col:col + 1], st[:, 128:129], [sb], [b_Nst])

        def emit_A_pair(l, half, p, wKV, wQO, wZ, after_p1=lambda: None):
            (wkv, bkv), (wqo, bqo), (wz, bz) = wKV, wQO, wZ
            h0 = 2 * p
            ksc = 128.0 ** -0.5
            for t in range(8):
                ps, psb = (PA, b_PA) if t % 2 == 0 else (PB, b_PB)
                proj(ps[:], psb, t, wkv, bkv, 0, 512, hb_of(t))
                if t % 2 == 0:
                    act(ktok[:, t, :], ps[:, 0:256], AF.Identity, [psb], [b_ktok[t]], scale=ksc)
                    for j in range(2):
                        h = h0 + j
                        act(vt_f[:, t, j, 0:128], ps[:, 256 + j * 128:384 + j * 128], AF.Identity, [psb, b_wg], [b_vtf[t]], scale=wg[:, t, h:h + 1])
                        act(vt_b[:, t, j, 0:128], ps[:, 256 + j * 128:384 + j * 128], AF.Identity, [psb, b_wg], [b_vtb[t]], scale=wg[:, t, 4 + h:5 + h])
                else:
                    ts(ktok[:, t, :], ps[:, 0:256], ksc, None, ALU.mult, [psb], [b_ktok[t]])
                    for j in range(2):
                        h = h0 + j
                        ts(vt_f[:, t, j, 0:128], ps[:, 256 + j * 128:384 + j * 128], wg[:, t, h:h + 1], None, ALU.mult, [psb, b_wg], [b_vtf[t]])
                        ts(vt_b[:, t, j, 0:128], ps[:, 256 + j * 128:384 + j * 128], wg[:, t, 4 + h:5 + h], None, ALU.mult, [psb, b_wg], [b_vtb[t]])
                if t == 1:
                    flush_tail()
            for j in range(2):
                cp(vt_f[:, :, j, 128:129], wg[:, :, h0 + j:h0 + j + 1], [b_wg], [b_vtf])
                cp(vt_b[:, :, j, 128:129], wg[:, :, 4 + h0 + j:5 + h0 + j], [b_wg], [b_vtb])
            after_p1()
            for si, (t0, t1) in enumerate(seqs_of(half)):
                slot = 1 - (si % 2)
                for j in range(2):
                    state_init_A(l, half, j, 1, h0 + j, slot)
                cur = [slot, slot]
                for c in range(t1 - 1, t0 - 1, -1):
                    for j in range(2):
                        h = h0 + j
                        st, sb, _ = Cst[j * 2 + cur[j]]
                        if half == 1:
                            cur[j] = 1 - cur[j]
                        so, sob, _ = Cst[j * 2 + cur[j]]
                        pu, pub = (PU, b_PU) if j == 0 else (PSc, b_PSc)
                        act(Sb16[:, c, j, :], st[:, 0:130], AF.Identity, [sb, b_rg], [b_Sb16], scale=rg[:, c, 4 + h:5 + h])
                        mm(pu[:, 0:129], ktok[:, c, j * 128:(j + 1) * 128], vt_b[:, c, j, 0:129], True, True, [b_ktok[c], b_vtb[c]], [pub])
                        stt(so[:, 0:129], st[:, 0:129], rg[:, c, 4 + h:5 + h], pu[:, 0:129], ALU.mult, ALU.add, [sb, b_rg, pub], [sob])
                if half == 0:
                    for j in range(2):
                        state_out_A(l, si, j, 1, h0 + j, slot)

            def Fp(c):
                proj(PA[:], b_PA, c, wqo, bqo, 0, 512, hb_of(c))
                proj(PB[:, 0:256], b_PB, c, wz, bz, 0, 256, hb_of(c))

            def Feq(c):
                act(qtok[:], PA[:, 0:256], AF.Identity, [b_PA], [b_qtok])

            def Fer(c):
                par = c % 2
                act(Esb[:, par, 0:256], PA[:, 256:512], AF.Exp, [b_PA], [b_Esb[par][0]], scale=-1.0)
                act(Esb[:, par, 256:512], PB[:, 0:256], AF.Exp, [b_PB], [b_Esb[par][1]], scale=-1.0)
                act(zsb[:, par, :], PB[:, 0:256], AF.Identity, [b_PB], [b_zsb[par]])
                sig_inplace(par, 512)

            def M1(c):
                pt, ptb = nextPT()
                for j in range(2):
                    tr(pt[:, j * 128:(j + 1) * 128], qtok[:, j * 128:(j + 1) * 128], idb[:], [b_qtok, b_idb], [ptb], inc=False)
                    tr(pt[:, (2 + j) * 128:(3 + j) * 128], ktok[:, c, j * 128:(j + 1) * 128], idb[:], [b_ktok[c], b_idb], [ptb], inc=(j == 1))
                cp(qkT[:].rearrange("p a b -> p (a b)"), pt[:, 0:512], [ptb], [b_qkT])
                for j in range(2):
                    mm(PSc[:, j * 128:(j + 1) * 128], qkT[:, 2 + j, :], qkT[:, j, :], True, True, [b_qkT], [b_PSc], inc=(j == 1))

            def M2(c):
                psv = PSc[:, 0:256].rearrange("p (j i) -> p j i", j=2)
                tt(Pf[:, 0:2, :], psv, uincb[:].unsqueeze(1).to_broadcast([128, 2, 128]), ALU.mult, [b_PSc, b_uincb], [b_Pf])
                tt(Pb[:, 0:2, :], psv, lincb[:].unsqueeze(1).to_broadcast([128, 2, 128]), ALU.mult, [b_PSc, b_lincb], [b_Pb])
                for j in range(2):
                    h = h0 + j
                    st, sb, _ = Cst[j * 2 + 0]
                    act(C16[:, j, 0:130], st[:, 0:130], AF.Identity, [sb, b_rg], [b_C16], scale=rg[:, c, h:h + 1])
                for j in range(2):
                    mm(PO1[:, j * 256:j * 256 + 129], Pf[:, j, :], vt_f[:, c, j, 0:129], True, False, [b_Pf, b_vtf[c]], [b_PO1], inc=False)
                    mm(PO1[:, j * 256:j * 256 + 129], qkT[:, j, :], C16[:, j, 0:129], False, True, [b_qkT, b_C16], [b_PO1], inc=(j == 1))
                for j in range(2):
                    mm(PO2[:, j * 256:j * 256 + 129], Pb[:, j, :], vt_b[:, c, j, 0:129], True, False, [b_Pb, b_vtb[c]], [b_PO2], inc=False)
                    mm(PO2[:, j * 256:j * 256 + 129], qkT[:, j, :], Sb16[:, c, j, 0:129], False, True, [b_qkT, b_Sb16], [b_PO2], inc=(j == 1))
                for j in range(2):
                    st, sb, _ = Cst[j * 2 + 0]
                    mm(PU[:, 0:129], ktok[:, c, j * 128:(j + 1) * 128], vt_f[:, c, j, 0:129], True, True, [b_ktok[c], b_vtf[c]], [b_PU])
                    stt(st[:, 0:129], st[:, 0:129], rg[:, c, h0 + j:h0 + j + 1], PU[:, 0:129], ALU.mult, ALU.add, [sb, b_rg, b_PU], [sb])

            def B1(c):
                po1 = PO1[:].rearrange("p (j n) -> p j n", j=2)
                po2 = PO2[:].rearrange("p (j n) -> p j n", j=2)
                act(dn[:, 0:2].unsqueeze(2), po1[:, :, 128:129], AF.Abs, [b_PO1], [b_dn[0]])
                act(dn[:, 2:4].unsqueeze(2), po2[:, :, 128:129], AF.Abs, [b_PO2], [b_dn[1]])
                tt(dn[:, 0:2], dn[:, 0:2], flo[:, c, h0:h0 + 2], ALU.max, [b_dn[0], b_flo], [b_dn[0]])
                tt(dn[:, 2:4], dn[:, 2:4], flo[:, c, 4 + h0:6 + h0], ALU.max, [b_dn[1], b_flo], [b_dn[1]])
                DVE(lambda e: e.reciprocal(dn[:, 0:4], dn[:, 0:4]), [b_dn], [b_dn])
                tt(ya[:].rearrange("p (j e) -> p j e", j=2), po1[:, :, 0:128], dn[:, 0:2].unsqueeze(2).to_broadcast([128, 2, 128]), ALU.mult,
                   [b_PO1, b_dn], [b_ya])
                for j in range(2):
                    stt(ya[:, j * 128:(j + 1) * 128], po2[:, j, 0:128], dn[:, 2 + j:3 + j], ya[:, j * 128:(j + 1) * 128], ALU.mult, ALU.add,
                        [b_PO2, b_dn, b_ya[j]], [b_ya[j]])

            def B2(c):
                par = c % 2
                tt(ya[:], ya[:], Esb[:, par, 0:256], ALU.mult, [b_ya, b_Esb[par][0]], [b_ya])
                finish_y(l, c, par, ya[:], 2, 128, h0, 256)

            def Yt(c):
                y_transpose(l, c, h0)

            def seq_begin(si):
                for j in range(2):
                    state_init_A(l, half, j, 0, h0 + j)

            def seq_end(si):
                if half == 0:
                    for j in range(2):
                        state_out_A(l, si, j, 0, h0 + j)

            run_pass2(half, Fp, Feq, Fer, M1, M2, B1, B2, Yt, seq_begin, seq_end)

        def rotary(dst, ps, t, R, W):
            cs = cst[:, C_COS + t * 32:C_COS + (t + 1) * 32].unsqueeze(1).to_broadcast([128, 4, 32])
            sn = cst[:, C_SIN + t * 32:C_SIN + (t + 1) * 32].unsqueeze(1).to_broadcast([128, 4, 32])
            pv = ps.rearrange("p (h e) -> p h e", h=4)
            dv = dst.rearrange("p (h e) -> p h e", h=4)
            t1, t2 = pv[:, :, 0:32], pv[:, :, 32:64]
            tt(rot1[:], t1, cs, ALU.mult, R + [b_cst], [b_rot1])
            tt(rot2[:], t2, sn, ALU.mult, R + [b_cst], [b_rot2])
            tt(dv[:, :, 0:32], rot1[:], rot2[:], ALU.subtract, [b_rot1, b_rot2], W)
            tt(rot1[:], t1, sn, ALU.mult, R + [b_cst], [b_rot1])
            tt(rot2[:], t2, cs, ALU.mult, R + [b_cst], [b_rot2])
            tt(dv[:, :, 32:64], rot1[:], rot2[:], ALU.add, [b_rot1, b_rot2], W)

        def emit_B(l, half, wKV, wQZ):
            (wkv, bkv), (wqz, bqz) = wKV, wQZ
            vhf = vt_f[:].rearrange("p t j n -> p t (j n)")
            vhb = vt_b[:].rearrange("p t j n -> p t (j n)")
            sb16 = Sb16[:].rearrange("p t j n -> p t (j n)")
            for t in range(8):
                ps, psb = (PA, b_PA) if t % 2 == 0 else (PB, b_PB)
                proj(ps[:], psb, t, wkv, bkv, 0, 512, hb_of(t))
                pav = ps[:, 256:512].rearrange("p (h e) -> p h e", h=4)
                if half == 0:
                    act(ktok[:, t, :], ps[:, 0:256], AF.Identity, [psb], [b_ktok[t]])
                else:
                    rotary(ktok[:, t, :], ps[:, 0:256], t, [psb], [b_ktok[t]])
                act(vtok[:, t, :], ps[:, 256:512], AF.Identity, [psb], [b_vtok[t]])
                tt(vhf[:, t, 0:256].rearrange("p (h e) -> p h e", h=4), pav, wend[:, 0, :].unsqueeze(2).to_broadcast([128, 4, 64]), ALU.mult, [psb, b_wend], [b_vtf[t]])
                tt(vhb[:, t, 0:256].rearrange("p (h e) -> p h e", h=4), pav, wend[:, 1, :].unsqueeze(2).to_broadcast([128, 4, 64]), ALU.mult, [psb, b_wend], [b_vtb[t]])
                if t == 1:
                    flush_tail()

            if MKX == 21:
                return

            def st_init(blk, d, slot=None):
                st, sb, ssem = Cst[blk * 2 + (d if slot is None else slot)]
                if half == 0:
                    DVE(lambda e: e.memset(st[:, 0:128], 0.0), [], [sb])
                else:
                    for hh in range(2):
                        S.dma('sp', ssem, st[hh * 64:(hh + 1) * 64, hh * 64:(hh + 1) * 64], sret_d[l, d, blk * 2 + hh], writes=[sb])

            def st_out(si, blk, d, slot=None):
                st, sb, ssem = Cst[blk * 2 + (d if slot is None else slot)]
                for hh in range(2):
                    S.dma('sp', ssem, oret_d[si, l, d, blk * 2 + hh], st[hh * 64:(hh + 1) * 64, hh * 64:(hh + 1) * 64], reads=[sb])

            for si, (t0, t1) in enumerate(seqs_of(half)):
                slot = 1 - (si % 2)
                for blk in range(2):
                    st_init(blk, 1, slot)
                for c in range(t1 - 1, t0 - 1, -1):
                    for blk in range(2):
                        st, sb, _ = Cst[blk * 2 + slot]
                        bs = slice(blk * 128, (blk + 1) * 128)
                        pu, pub = (PU, b_PU) if blk == 0 else (PSc, b_PSc)
                        tt(sb16[:, c, bs], st[:, 0:128], cI(C_BD2), ALU.mult, [sb, b_cst], [b_Sb16])
                        mm(pu[:, 0:128], ktok[:, c, bs], vhb[:, c, bs], True, True, [b_ktok[c], b_vtb[c]], [pub])
                        stt(st[:, 0:128], st[:, 0:128], gblk[:, 1, blk:blk + 1], pu[:, 0:128], ALU.mult, ALU.add, [sb, b_gblk, pub], [sb])
                if half == 0:
                    for blk in range(2):
                        st_out(si, blk, 1, slot)
            if MKX == 22:
                return

            def Fp(c):
                proj(PA[:], b_PA, c, wqz, bqz, 0, 512, hb_of(c))

            def Feq(c):
                if half == 0:
                    act(qtok[:], PA[:, 0:256], AF.Identity, [b_PA], [b_qtok])
                else:
                    rotary(qtok[:], PA[:, 0:256], c, [b_PA], [b_qtok])

            def Fer(c):
                par = c % 2
                act(Esb[:, par, 256:512], PA[:, 256:512], AF.Exp, [b_PA], [b_Esb[par][1]], scale=-1.0)
                act(zsb[:, par, :], PA[:, 256:512], AF.Identity, [b_PA], [b_zsb[par]])
                sig_inplace(par, 256, 256)

            def M1(c):
                pt, ptb = nextPT()
                for blk in range(2):
                    tr(pt[:, blk * 128:(blk + 1) * 128], qtok[:, blk * 128:(blk + 1) * 128], idb[:], [b_qtok, b_idb], [ptb], inc=False)
                    tr(pt[:, (2 + blk) * 128:(3 + blk) * 128], ktok[:, c, blk * 128:(blk + 1) * 128], idb[:], [b_ktok[c], b_idb], [ptb], inc=(blk == 1))
                cp(qkT[:].rearrange("p a b -> p (a b)"), pt[:, 0:512], [ptb], [b_qkT])
                for d in range(2):
                    tt(qhT[:, d, :, :], qkT[:, 0:2, :], Wq[:, d, :, :], ALU.mult, [b_qkT, b_Wq], [b_qhT])
                qbd = Pb[:].rearrange("p a b -> p (a b)").rearrange("p (k a i) -> p k a i", k=2, a=2)
                for blk in range(2):
                    tt(qbd[:, blk, :, :], qkT[:, blk, :].unsqueeze(1).to_broadcast([128, 2, 128]),
                       cst[:, C_MC2:C_MC2 + 2].unsqueeze(2).to_broadcast([128, 2, 128]), ALU.mult, [b_qkT, b_cst], [b_Pb])
                for blk in range(2):
                    mm(PSc[:, blk * 256:(blk + 1) * 256], qkT[:, 2 + blk, :], qbd[:, blk, :, :].rearrange("p a i -> p (a i)"), True, True,
                       [b_qkT, b_Pb], [b_PSc], inc=(blk == 1))

            def M2(c):
                tt(Pf[:].rearrange("p a b -> p (a b)"), PSc[:], Mret[:].rearrange("p a b -> p (a b)"), ALU.mult, [b_PSc, b_Mret], [b_Pf])
                for blk in range(2):
                    st, sb, _ = Cst[blk * 2 + 0]
                    tt(C16[:, blk, 0:128], st[:, 0:128], cI(C_BD2), ALU.mult, [sb, b_cst], [b_C16])
                for blk in range(2):
                    bs = slice(blk * 128, (blk + 1) * 128)
                    mm(PO2[:, bs], qhT[:, 0, blk, :], C16[:, blk, 0:128], True, False, [b_qhT, b_C16], [b_PO2], inc=False)
                    mm(PO2[:, bs], qhT[:, 1, blk, :], sb16[:, c, bs], False, False, [b_qhT, b_Sb16], [b_PO2], inc=False)
                    for hh in range(2):
                        h = blk * 2 + hh
                        oc = slice(h * 64, (h + 1) * 64)
                        mm(PO2[:, oc], Pf[:, h, :], vtok[:, c, oc], False, hh == 1, [b_Pf, b_vtok[c]], [b_PO2], inc=(blk == 1 and hh == 1))
                for blk in range(2):
                    st, sb, _ = Cst[blk * 2 + 0]
                    bs = slice(blk * 128, (blk + 1) * 128)
                    mm(PU[:, 0:128], ktok[:, c, bs], vhf[:, c, bs], True, True, [b_ktok[c], b_vtf[c]], [b_PU])
                    stt(st[:, 0:128], st[:, 0:128], gblk[:, 0, blk:blk + 1], PU[:, 0:128], ALU.mult, ALU.add, [sb, b_gblk, b_PU], [sb])

            def B1(c):
                act(ya[:], PO2[:, 0:256], AF.Identity, [b_PO2], [b_ya])

            def B2(c):
                finish_y(l, c, c % 2, ya[:], 4, 64, 4, 256)

            def Yt(c):
                y_transpose(l, c, 4)

            def seq_begin(si):
                for blk in range(2):
                    st_init(blk, 0)

            def seq_end(si):
                if half == 0:
                    for blk in range(2):
                        st_out(si, blk, 0)

            run_pass2(half, Fp, Feq, Fer, M1, M2, B1, B2, Yt, seq_begin, seq_end)

        def emit_C(l, half, wKV, wQZ, after_p1, after_gates, per_tile=lambda: None):
            (wkv, bkv), (wqz, bqz) = wKV, wQZ
            sb16 = Sb16[:].rearrange("p t j n -> p t (j n)")
            for t in range(8):
                toks = slice(t * 128, (t + 1) * 128)
                for kc in range(8):
                    mm(PB[0:32, 0:128], wkv[:, kc, 400:432], hT[:, kc, toks], kc == 0, kc == 7, [bkv, hb_of(t)], [b_PB], inc=(kc == 7))
                act(clrT[0:32, :], PB[0:32, 0:128], AF.Identity, [b_PB], [b_clrT])
                mm(PB[:, 256:512], clrT[:], w2x[:], True, True, [b_clrT, b_w2x], [b_PB])
                act(L1c, PB[:, 256:512], AF.Exp, [b_PB], [b_L1c], scale=-1.0)
                act(L1c, L1c, AF.Ln, [b_L1c], [b_L1c], bias=1.0)
                proj(PA[:, 0:400], b_PA, t, wkv, bkv, 0, 400, hb_of(t))
                mm(PB[:, 0:128], cI(C_UINC), L1c[:, 0:128], True, True, [b_cst, b_L1c], [b_PB], inc=False)
                mm(PB[:, 128:256], cI(C_LINC), L1c[:, 128:256], True, True, [b_cst, b_L1c], [b_PB])
                for d in range(2):
                    mm(PM[:, 200 + d:201 + d], L1c[:, d * 128:(d + 1) * 128], cst[:, C_ONES:C_ONES + 1], True, True, [b_L1c, b_cst], [b_PM], inc=(d == 1))
                act(Gc[:, t, :], PM[:, 200:202], AF.Exp, [b_PM], [b_Gc], scale=-1.0 / 16)
                act(EBn[:], PB[:, 0:256], AF.Exp, [b_PB], [b_EBn], scale=1.0 / 16)
                for d in range(2):
                    act(EBd[d][0][:, t, :], PB[:, d * 128:(d + 1) * 128], AF.Exp, [b_PB], [EBd[d][1]], scale=-1.0 / 16)
                act(vtok[:, t, :], PA[:, 128:384], AF.Identity, [b_PA], [b_vtok[t]])
                stt(ktok[:, t, :].rearrange("p (d n) -> p d n", d=2), EBn[:].rearrange("p (d n) -> p d n", d=2), 32.0 ** -0.5,
                    PA[:, 0:128].unsqueeze(1).to_broadcast([128, 2, 128]), ALU.mult, ALU.mult, [b_EBn, b_PA], [b_ktok])
                tt(Gs[:, t, :], PA[:, 384:400], gb_bc[:, l * 16:(l + 1) * 16], ALU.add, [b_PA, b_gb], [b_Gs])
                per_tile()
                if t == 1:
                    flush_tail()
            after_p1()
            gates_gen = emit_gates(l, half)

            def gstep(n=1):
                for _ in range(n):
                    next(gates_gen, None)
            after_gates()

            def st_init(d, slot=None):
                st, sb, ssem = Cst[d if slot is None else slot]
                DVE(lambda e: e.memset(st[:, 0:256], 0.0), [], [sb])
                if half == 1:
                    for h in range(4):
                        S.dma('sp', ssem, st[h * 32:(h + 1) * 32, h * 64:(h + 1) * 64], sgla_d[l, d, h], writes=[sb])

            def st_out(si, d, slot=None):
                st, sb, ssem = Cst[d if slot is None else slot]
                for h in range(4):
                    S.dma('sp', ssem, ogla_d[si, l, d, h], st[h * 32:(h + 1) * 32, h * 64:(h + 1) * 64], reads=[sb])

            def st_update(d, c, slot=None):
                st, sb, _ = Cst[d if slot is None else slot]
                mm(PU[:, 0:256], ktok[:, c, d * 128:(d + 1) * 128], vtok[:, c, :], True, True, [b_ktok[c], b_vtok[c]], [b_PU])
                tt(st[:, 0:256], st[:, 0:256], PU[:, 0:256], ALU.add, [sb, b_PU], [sb])
                ts(st[:, 0:256], st[:, 0:256], Gc[:, c, d:d + 1], None, ALU.mult, [sb, b_Gc], [sb])

            for si, (t0, t1) in enumerate(seqs_of(half)):
                slot = 1 + (si % 2)
                st, sb, _ = Cst[slot]
                st_init(1, slot)
                for c in range(t1 - 1, t0 - 1, -1):
                    tt(sb16[:, c, 0:256], st[:, 0:256], cI(C_BD4, 256), ALU.mult, [sb, b_cst], [b_Sb16])
                    gstep(2)
                    st_update(1, c, slot)
                    gstep(2)
                if half == 0:
                    st_out(si, 1, slot)

            qt = qtok[:].rearrange("p (d n) -> p d n", d=2)

            def Fp(c):
                proj(PA[:, 0:384], b_PA, c, wqz, bqz, 0, 384, hb_of(c))

            def Feq(c):
                for d in range(2):
                    tt(qt[:, d, :], EBd[d][0][:, c, :], PA[:, 0:128], ALU.mult, [EBd[d][1], b_PA], [b_qtok])

            def Fer(c):
                par = c % 2
                act(Esb[:, par, 256:512], PA[:, 128:384], AF.Exp, [b_PA], [b_Esb[par][1]], scale=-1.0)
                act(zsb[:, par, :], PA[:, 128:384], AF.Identity, [b_PA], [b_zsb[par]])
                sig_inplace(par, 256, 256)

            def M1(c):
                pt, ptb = nextPT()
                for d in range(2):
                    tr(pt[:, d * 128:(d + 1) * 128], qt[:, d, :], idb[:], [b_qtok, b_idb], [ptb], inc=False)
                    tr(pt[:, (2 + d) * 128:(3 + d) * 128], ktok[:, c, d * 128:(d + 1) * 128], idb[:], [b_ktok[c], b_idb], [ptb], inc=(d == 1))
                cp(qkT[:].rearrange("p a b -> p (a b)"), pt[:, 0:512], [ptb], [b_qkT])
                mc4 = cst[:, C_MC4:C_MC4 + 4].unsqueeze(2).to_broadcast([128, 4, 128])
                tt(Pf[:], qkT[:, 0, :].unsqueeze(1).to_broadcast([128, 4, 128]), mc4, ALU.mult, [b_qkT, b_cst], [b_Pf])
                tt(Pb[:], qkT[:, 1, :].unsqueeze(1).to_broadcast([128, 4, 128]), mc4, ALU.mult, [b_qkT, b_cst], [b_Pb])
                mm(PSc[:], qkT[:, 2, :], Pf[:].rearrange("p a b -> p (a b)"), True, True, [b_qkT, b_Pf], [b_PSc])
                mm(PO1[:], qkT[:, 3, :], Pb[:].rearrange("p a b -> p (a b)"), True, True, [b_qkT, b_Pb], [b_PO1])

            def M2(c):
                st, sb, _ = Cst[0]
                tt(Pf[:], PSc[:].rearrange("p (h i) -> p h i", h=4), uincb[:].unsqueeze(1).to_broadcast([128, 4, 128]), ALU.mult, [b_PSc, b_uincb], [b_Pf])
                tt(Pb[:], PO1[:].rearrange("p (h i) -> p h i", h=4), lincb[:].unsqueeze(1).to_broadcast([128, 4, 128]), ALU.mult, [b_PO1, b_lincb], [b_Pb])
                tt(C16[:, 0, 0:256], st[:, 0:256], cI(C_BD4, 256), ALU.mult, [sb, b_cst], [b_C16])
                mm(PO2[:, 0:256], qkT[:, 0, :], C16[:, 0, 0:256], True, False, [b_qkT, b_C16], [b_PO2], inc=False)
                mm(PO2[:, 0:256], qkT[:, 1, :], sb16[:, c, 0:256], False, False, [b_qkT, b_Sb16], [b_PO2], inc=False)
                for h in range(4):
                    oc = slice(h * 64, (h + 1) * 64)
                    mm(PO2[:, oc], Pf[:, h, :], vtok[:, c, oc], False, False, [b_Pf, b_vtok[c]], [b_PO2], inc=False)
                    mm(PO2[:, oc], Pb[:, h, :], vtok[:, c, oc], False, h == 3, [b_Pb, b_vtok[c]], [b_PO2], inc=(h == 3))
                st_update(0, c)

            def B1(c):
                act(ya[:], PO2[:, 0:256], AF.Identity, [b_PO2], [b_ya])
                gstep(2)

            def B2(c):
                finish_y(l, c, c % 2, ya[:], 4, 64, 6, 256)
                gstep(2)

            def Yt(c):
                y_transpose(l, c, 6)

            def seq_begin(si):
                st_init(0)

            def seq_end(si):
                if half == 0:
                    st_out(si, 0)

            run_pass2(half, Fp, Feq, Fer, M1, M2, B1, B2, Yt, seq_begin, seq_end)
            for _ in gates_gen:
                pass

        def emit_out(l, half, wo0, wo1, per_group=lambda: None):
            v = half
            flush_tail()
            for t in range(8):
                gt = half * 8 + t
                toks = slice(t * 128, (t + 1) * 128)
                for fb, (wt, wb_) in enumerate((wo0, wo1)):
                    ps, psb = (PA, b_PA) if fb == 0 else (PB, b_PB)
                    for kc in range(8):
                        mm(ps[:], yT[:, kc, toks], wt[:, kc, 0:512], kc == 0, kc == 7, [b_yT[t], wb_], [psb], inc=(kc == 7))
                    fs = slice(fb * 512, (fb + 1) * 512)
                    tt(Esb[:, fb, :], ps[:], gate_bc[:, v, fs], ALU.mult, [psb, b_gate], [b_Esb[fb]])
                    tt(x_sb[:, gt, fs], x_sb[:, gt, fs], Esb[:, fb, :], ALU.add, [bx[gt], b_Esb[fb]], [bx[gt]])
                    per_group()

        PHASES.clear()

        def mark(name):
            PHASES.append((name, {k: v for k, v in S.cnt.items() if not k.startswith('dma_')}))
        for l in range(depth):
            mark("L%d mod" % l)
            load_bg(l)
            if l == 0:
                st0 = mod_stream(0, range(16))
                for _ in range(16):
                    st0()
                mod_ss_finish(0)
            if STAGE < 2:
                break
            emit_ret_statics(l)
            emit_gla_statics(l)
            if STAGE < 3:
                break
            for half in range(2):
                mark("L%d h%d norm" % (l, half))
                emit_norm(l, half)
                if STAGE < 4:
                    continue
                blocks = {}
                blocks['CKV'] = load_block(l, [(CK, 128), (CV, 256), (OG, 16), (CL, 32)], win_d)
                blocks['CQZ'] = load_block(l, [(CQ, 128), (CZ, 256)], win_d)
                blocks['KV0'] = load_block(l, [(OK_, 256), (OV, 256)], win_d)
                blocks['QO0'] = load_block(l, [(OQ, 256), (OO, 256)], win_d)

                def c_after_p1():
                    blocks['Z0'] = load_block(l, [(OZ, 256)], win_d)

                def c_after_gates():
                    pass
                mark("L%d h%d C" % (l, half))
                gate_step = mod_stream(l, range(16, 24)) if half == 0 else (lambda: None)
                emit_C(l, half, blocks['CKV'], blocks['CQZ'], c_after_p1, c_after_gates, gate_step)
                blocks['KV1'] = load_block(l, [(OK_ + 256, 256), (OV + 256, 256)], win_d)

                def a0_after_p1():
                    blocks['QO1'] = load_block(l, [(OQ + 256, 256), (OO + 256, 256)], win_d)
                mark("L%d h%d A0" % (l, half))
                emit_A_pair(l, half, 0, blocks['KV0'], blocks['QO0'], blocks['Z0'], a0_after_p1)
                blocks['Z1'] = load_block(l, [(OZ + 256, 256)], win_d)
                blocks['BKV'] = load_block(l, [(BK, 256), (BV, 256)], win_d)

                def a1_after_p1():
                    blocks['BQZ'] = load_block(l, [(BQ, 256), (BZ, 256)], win_d)
                mark("L%d h%d A1" % (l, half))
                emit_A_pair(l, half, 1, blocks['KV1'], blocks['QO1'], blocks['Z1'], a1_after_p1)
                blocks['WO0'] = load_block(l, [(0, 512)], wout_d)
                blocks['WO1'] = load_block(l, [(512, 512)], wout_d)
                mark("L%d h%d B" % (l, half))
                emit_B(l, half, blocks['BKV'], blocks['BQZ'])
                mark("L%d h%d out" % (l, half))
                if half == 0 and l + 1 < depth:
                    ss_step = mod_stream(l + 1, range(16))
                    emit_out(l, half, blocks['WO0'], blocks['WO1'], ss_step)
                    mod_ss_finish(l + 1)
                else:
                    emit_out(l, half, blocks['WO0'], blocks['WO1'])

        mark("final")
        for gt in range(16):
            act(hT[:, gt % 8, :], x_sb[:, gt, :], AF.Square, [bx[gt]], [b_hT[0], b_hT[1], b_ss], accum=ss[:, gt:gt + 1])
        act(rstd[:], ss[:], AF.Ln, [b_ss], [b_rstd], scale=1.0 / D, bias=EPS)
        act(rstd[:], rstd[:], AF.Exp, [b_rstd], [b_rstd], scale=-0.5)
        S.dma('sp', d_misc, fin_bc[:], fing_d.partition_broadcast(128), writes=[b_fin])
        for gt in range(16):
            stt(x_sb[:, gt, :], x_sb[:, gt, :], rstd[:, gt:gt + 1], fin_bc[:], ALU.mult, ALU.mult, [bx[gt], b_rstd, b_fin], [bx[gt]])
            S.dma('sp', d_out, y_d[gt * 128:(gt + 1) * 128, :], x_sb[:, gt, :], reads=[bx[gt]])
        mm(PA[:, 0:128], Nst[:], idf, True, True, [b_Nst, b_cst], [b_PA])
        cp(tmpS[:, 0:128], PA[:, 0:128], [b_PA], [b_tmpS])
        S.dma('sp', d_out, on_d, tmpS[:, 0:128], reads=[b_tmpS])
        S.dma('sp', d_out, om_d, Mst[0:1, :], reads=[b_Mst])
        S.final_wait('sp')
        print("instructions", S.nins, "waits", S.nwait, {k: v for k, v in S.cnt.items() if not k.startswith('dma_')})
    return nc


_NC_CACHE = {}


def kernel(x_prompt, x_sample, state_mlstm_C, state_mlstm_n, state_mlstm_m, state_ret, state_gla, c, c_ctx,
           norm_g, w_ada, b_ada, w_in, mlstm_gate_b, ret_decay_logit, gla_w2, gla_b2, headnorm_g, w_out, final_g,
           _depth=DEPTH):
    f = lambda a: np.ascontiguousarray(np.asarray(a, dtype=np.float32))
    x_prompt, x_sample = f(x_prompt), f(x_sample)
    nc = build(_depth)
    cstv = make_consts()
    shared = {
        "norm_g": f(norm_g), "w_ada": f(w_ada), "b_ada": f(b_ada), "w_in": f(w_in),
        "gate_b": f(mlstm_gate_b).reshape(-1), "ret_logit": f(ret_decay_logit).reshape(-1),
        "gla_w2": f(gla_w2), "gla_b2": f(gla_b2).reshape(DEPTH, 256), "hn_g": f(headnorm_g),
        "w_out": f(w_out), "final_g": f(final_g), "cst": cstv,
    }
    in_maps = []
    for i in range(NCORES):
        xx = np.concatenate([x_prompt[4 * i:4 * i + 4].reshape(1024, D), x_sample[i]], axis=0)
        m = dict(shared)
        m.update({
            "x": np.ascontiguousarray(xx),
            "cvec": np.ascontiguousarray(np.stack([f(c_ctx), f(c)[i]], axis=0)),
            "sC": f(state_mlstm_C)[i], "sn": f(state_mlstm_n)[i], "sm": f(state_mlstm_m)[i].reshape(-1),
            "sret": f(state_ret)[i], "sgla": f(state_gla)[i],
        })
        in_maps.append(m)
    res = run_bass_kernel_spmd(nc, in_maps, core_ids=list(range(NCORES)))
    R = res.results
    y_prompt = np.stack([R[i]["y"][0:1024].reshape(4, 256, D) for i in range(NCORES)], 0).reshape(32, 256, D)
    y_sample = np.stack([R[i]["y"][1024:2048] for i in range(NCORES)], 0)
    oC = np.concatenate([R[i]["oC"] for i in range(NCORES)], 0)
    on = np.concatenate([R[i]["on"].reshape(4, DEPTH, 2, 4, 128) for i in range(NCORES)], 0)
    om = np.concatenate([R[i]["om"].reshape(4, DEPTH, 2, 4) for i in range(NCORES)], 0)
    oret = np.concatenate([R[i]["oret"] for i in range(NCORES)], 0)
    ogla = np.concatenate([R[i]["ogla"] for i in range(NCORES)], 0)
    return (y_prompt.astype(np.float32), y_sample.astype(np.float32), oC.astype(np.float32), on.astype(np.float32),
            om.astype(np.float32), oret.astype(np.float32), ogla.astype(np.float32))
```

```python
import numpy as np
from contextlib import ExitStack
import concourse.bass as bass
import concourse.mybir as mybir
from concourse.bass_utils import run_bass_kernel_spmd

F32 = mybir.dt.float32
BF16 = mybir.dt.bfloat16
AF = mybir.ActivationFunctionType
ALU = mybir.AluOpType
AX = mybir.AxisListType

DEPTH = 4
D = 1024
DIN = 4400
EPS = 1e-6
NCORES = 8
OQ, OK_, OV, OO, OZ, OG = 0, 512, 1024, 1536, 2048, 2560
BQ, BK, BV, BZ = 2576, 2832, 3088, 3344
CQ, CK, CV, CZ, CL = 3600, 3728, 3856, 4112, 4368

C_IDF, C_UINC, C_LINC, C_RIJ, C_RJI, C_POSF, C_POSB, C_ONES = [i * 128 for i in range(8)]
C_COLA, C_COLB = 1024, 1025
C_COS, C_SIN = 1026, 1026 + 256
C_BD2, C_BD4, C_MC2, C_MC4 = 1538, 1538 + 128, 1538 + 384, 1538 + 386
NCST = 1538 + 390


def make_consts():
    c = np.zeros((128, NCST), np.float32)
    p = np.arange(128)[:, None].astype(np.float32)
    f = np.arange(128)[None, :].astype(np.float32)
    c[:, C_IDF:C_IDF + 128] = (p == f)
    c[:, C_UINC:C_UINC + 128] = (p <= f)
    c[:, C_LINC:C_LINC + 128] = (p >= f)
    c[:, C_RIJ:C_RIJ + 128] = np.maximum(f - p, 0)
    c[:, C_RJI:C_RJI + 128] = np.maximum(p - f, 0)
    c[:, C_POSF:C_POSF + 128] = f + 1
    c[:, C_POSB:C_POSB + 128] = 128 - f
    c[:, C_ONES:C_ONES + 128] = 1.0
    c[:, C_COLA] = -(127 - p[:, 0])
    c[:, C_COLB] = -p[:, 0]
    L = 1024
    tok = np.arange(L)
    r = (tok // 64).astype(np.float32)
    col = (tok % 64).astype(np.float32)
    nf = 16
    freqs = (np.float32(10000.0) ** (-np.arange(nf, dtype=np.float32) / np.float32(nf))).astype(np.float32)
    ang = np.concatenate([r[:, None] * freqs, col[:, None] * freqs], axis=-1).astype(np.float32)
    cos = np.cos(ang).astype(np.float32).reshape(8, 128, 32).transpose(1, 0, 2).reshape(128, 256)
    sin = np.sin(ang).astype(np.float32).reshape(8, 128, 32).transpose(1, 0, 2).reshape(128, 256)
    pp = np.arange(128)[:, None]
    c[:, C_BD2:C_BD2 + 128] = (pp // 64 == (np.arange(128)[None, :] // 64))
    c[:, C_BD4:C_BD4 + 256] = (pp // 32 == (np.arange(256)[None, :] // 64))
    c[:, C_MC2:C_MC2 + 2] = (pp // 64 == np.arange(2)[None, :])
    c[:, C_MC4:C_MC4 + 4] = (pp // 32 == np.arange(4)[None, :])
    c[:, C_COS:C_COS + 256] = cos
    c[:, C_SIN:C_SIN + 256] = sin
    return c


import os as _os
ELIDE = int(_os.environ.get("MK_ELIDE", "5"))


class Buf:
    __slots__ = ("name", "w", "r", "excl")

    def __init__(self, name, excl=False):
        self.name = name
        self.w = None
        self.r = []
        self.excl = excl


class Sync:
    def __init__(self, nc, ctx):
        self.nc = nc
        self.eng = {'pe': nc.tensor, 'dve': nc.vector, 'act': nc.scalar, 'pool': nc.gpsimd, 'sp': nc.sync}
        self.sem, self.cnt, self.clk, self.hist, self.closed = {}, {}, {}, {}, {}
        for k in self.eng:
            self.sem[k] = ctx.enter_context(nc.semaphore("s_" + k))
            self.cnt[k] = 0
            self.clk[k] = {}
        self.ctx = ctx
        self.nwait = 0
        self.nins = 0

    def dma_sem(self, name):
        k = 'dma_' + name
        self.sem[k] = self.ctx.enter_context(self.nc.semaphore("s_" + k))
        self.cnt[k] = 0
        self.closed[k] = False
        return k

    def _deps(self, e, reads, writes):
        deps = {}
        if ELIDE in (3, 4, 5, 6):
            return self._deps2(e, reads, writes)
        if ELIDE == 0:
            e = '?'
        elif ELIDE == 1 and e != 'pe':
            e = '?'
        for b in reads:
            if b.w is not None:
                s, v = b.w
                if not (s == e and e == 'pe') and deps.get(s, 0) < v:
                    deps[s] = v
            if b.excl:
                for (s, v) in b.r:
                    if s != e and deps.get(s, 0) < v:
                        deps[s] = v
        for b in writes:
            if b.w is not None:
                s, v = b.w
                if s != e and deps.get(s, 0) < v:
                    deps[s] = v
            for (s, v) in b.r:
                if s != e and deps.get(s, 0) < v:
                    deps[s] = v
        return deps

    def _deps2(self, e, reads, writes):
        deps = {}
        cur = self.cnt.get(e, 0)

        def add(s, v, kind):
            if s == e:
                if v > cur:
                    assert e == 'pe', (e, s, v, cur)
                    return
                if e == 'pe' and ELIDE in (5, 6):
                    return
                if e != 'pe' and ELIDE in (4, 5) and kind != 'raw':
                    return
            if deps.get(s, 0) < v:
                deps[s] = v
        for b in reads:
            if b.w is not None:
                add(b.w[0], b.w[1], 'raw')
            if b.excl:
                for (s, v) in b.r:
                    if s != e:
                        add(s, v, 'rr')
        for b in writes:
            if b.w is not None:
                add(b.w[0], b.w[1], 'waw')
            for (s, v) in b.r:
                add(s, v, 'war')
        return deps

    def _wait1(self, e, s, v):
        clk = self.clk[e]
        if s.startswith('dma_'):
            v = self.cnt[s]
            self.closed[s] = True
        if clk.get(s, 0) >= v:
            return
        assert v <= self.cnt[s], ("wait on not-yet-signalled instruction", e, s, v, self.cnt[s])
        self.eng[e].wait_ge(self.sem[s], v)
        self.nwait += 1
        h = self.hist.get((s, v))
        if h:
            for a, b in h.items():
                if clk.get(a, 0) < b:
                    clk[a] = b
        clk[s] = v

    def _wait(self, e, deps):
        for s, v in deps.items():
            self._wait1(e, s, v)

    @staticmethod
    def _flat(xs):
        out = []
        for x in xs:
            if isinstance(x, (list, tuple)):
                out.extend(Sync._flat(x))
            else:
                out.append(x)
        return out

    def op(self, e, fn, reads=(), writes=(), inc=True):
        reads, writes = self._flat(reads), self._flat(writes)
        self._wait(e, self._deps(e, reads, writes))
        ins = fn(self.eng[e])
        self.nins += 1
        if not inc:
            v = self.cnt[e] + 1
            for b in reads:
                b.r.append((e, v))
            for b in writes:
                b.w = (e, v)
                b.r = []
            return ins
        self.cnt[e] += 1
        v = self.cnt[e]
        ins.then_inc(self.sem[e], 1)
        snap = dict(self.clk[e])
        snap[e] = v
        self.hist[(e, v)] = snap
        for b in reads:
            b.r.append((e, v))
        for b in writes:
            b.w = (e, v)
            b.r = []
        return ins

    def dma(self, e, dsem, out, in_, reads=(), writes=(), slow=False):
        reads, writes = self._flat(reads), self._flat(writes)
        self._wait(e, self._deps(e, reads, writes))
        if self.closed[dsem] and self.clk[e].get(dsem, 0) < self.cnt[dsem]:
            self._wait1(e, dsem, self.cnt[dsem])
        self.closed[dsem] = False
        if slow:
            ins = self.eng[e].dma_start(out=out, in_=in_, allow_slow_non_contiguous=True)
        else:
            ins = self.eng[e].dma_start(out=out, in_=in_)
        self.cnt[dsem] += 16
        v = self.cnt[dsem]
        ins.then_inc(self.sem[dsem], 16)
        self.nins += 1
        self.hist[(dsem, v)] = dict(self.clk[e])
        for b in reads:
            b.r.append((dsem, v))
        for b in writes:
            b.w = (dsem, v)
            b.r = []
        return ins

    def final_wait(self, e):
        for s in self.sem:
            if s.startswith('dma_') and self.cnt[s] > 0 and self.clk[e].get(s, 0) < self.cnt[s]:
                self.eng[e].wait_ge(self.sem[s], self.cnt[s])


import os
STAGE = int(os.environ.get("MK_STAGE", "99"))
SUB = int(os.environ.get("MK_SUB", "99"))
MKX = int(os.environ.get("MK_X", "0"))
ELIDE = int(os.environ.get("MK_ELIDE", "5"))


PHASES = []


def build(depth=DEPTH):
    nc = bass.Bass("TRN2", target_bir_lowering=False)
    din = lambda n, s: nc.dram_tensor(n, list(s), F32, kind="ExternalInput").ap()
    dout = lambda n, s: nc.dram_tensor(n, list(s), F32, kind="ExternalOutput").ap()
    x_d = din("x", (2048, D))
    cvec_d = din("cvec", (2, D))
    sC_d = din("sC", (DEPTH, 2, 4, 128, 128))
    sn_d = din("sn", (DEPTH, 2, 4, 128))
    sm_d = din("sm", (DEPTH * 8,))
    sret_d = din("sret", (DEPTH, 2, 4, 64, 64))
    sgla_d = din("sgla", (DEPTH, 2, 4, 32, 64))
    normg_d = din("norm_g", (DEPTH, D))
    wada_d = din("w_ada", (DEPTH, D, 3 * D))
    bada_d = din("b_ada", (DEPTH, 3 * D))
    win_d = din("w_in", (DEPTH, D, DIN))
    gateb_d = din("gate_b", (DEPTH * 16,))
    retl_d = din("ret_logit", (DEPTH * 8,))
    w2_d = din("gla_w2", (DEPTH, 2, 16, 128))
    b2_d = din("gla_b2", (DEPTH, 256))
    hng_d = din("hn_g", (DEPTH, D))
    wout_d = din("w_out", (DEPTH, D, D))
    fing_d = din("final_g", (D,))
    cst_d = din("cst", (128, NCST))
    y_d = dout("y", (2048, D))
    oC_d = dout("oC", (4, DEPTH, 2, 4, 128, 128))
    on_d = dout("on", (128, 128))
    om_d = dout("om", (1, 128))
    oret_d = dout("oret", (4, DEPTH, 2, 4, 64, 64))
    ogla_d = dout("ogla", (4, DEPTH, 2, 4, 32, 64))

    with ExitStack() as ctx:
        S = Sync(nc, ctx)

        def T(name, shape, dt=F32):
            return ctx.enter_context(nc.sbuf_tensor("sb_" + name, list(shape), dt)), Buf(name)

        def PS(name, shape, dt=F32):
            return ctx.enter_context(nc.psum_tensor("ps_" + name, list(shape), dt))

        x_sb, _ = T("x_sb", (128, 16, D))
        bx = [Buf("x%d" % i) for i in range(16)]
        hT, _ = T("hT", (128, 8, 1024), BF16)
        b_hT = [Buf("hT0"), Buf("hT1")]
        yT, _ = T("yT", (128, 8, 1024), BF16)
        b_yT = [Buf("yT%d" % i) for i in range(8)]
        NSLOT = 4
        Wr = []
        for s in range(NSLOT):
            t, b = T("wr%d" % s, (128, 8, 528), BF16)
            Wr.append((t, b, S.dma_sem("wr%d" % s)))
        ada = []
        for s in range(2):
            t, b = T("ada%d" % s, (128, 8, 128), BF16)
            ada.append((t, b, S.dma_sem("ada%d" % s)))
        cst, b_cst = T("cst", (128, NCST))
        idb, b_idb = T("idb", (128, 128), BF16)
        uincb, b_uincb = T("uincb", (128, 128), BF16)
        lincb, b_lincb = T("lincb", (128, 128), BF16)
        gate_bc, b_gate = T("gate_bc", (128, 2, D))
        bg_bc, b_bg = T("bg_bc", (128, D))
        ngT, b_ngT = T("ngT", (128, DEPTH, 8))
        hngT, b_hngT = T("hngT", (128, DEPTH, 8))
        bshT, b_bshT = T("bshT", (128, DEPTH, 8))
        bscT, b_bscT = T("bscT", (128, DEPTH, 8))
        cT, b_cT = T("cT", (128, 8, 2))
        scT, b_scT = T("scT", (128, 8, 2), BF16)
        screp, b_screp = T("screp", (128, 2, 8, 128), BF16)
        gsT2, _ = T("gsT", (128, 2, 8, 2))
        shT2, _ = T("shT", (128, 2, 8, 2))
        b_gsTp = [Buf("gsT0"), Buf("gsT1")]
        b_shTp = [Buf("shT0"), Buf("shT1")]
        modraw, b_modraw = T("modraw", (128, 16, 2))
        gb_bc, b_gb = T("gb_bc", (128, DEPTH * 16))
        m0_bc, b_m0 = T("m0_bc", (128, DEPTH * 8))
        rl_bc, b_rl = T("rl_bc", (128, DEPTH * 8))
        ss, b_ss = T("ss", (128, 16))
        rstd, b_rstd = T("rstd", (128, 16))
        Gs, b_Gs = T("Gs", (128, 8, 16))
        L1g, b_L1g = T("L1g", (128, 2, 8, 4))
        Fp, b_Fp = T("Fp", (128, 8, 16))
        ug, b_ug = T("ug", (128, 8, 8))
        umax, b_umax = T("umax", (64, 1))
        Abc, b_Abc = T("Abc", (128, 8, 8))
        MP, b_MP = T("MP", (128, 8, 8))
        Sg, b_Sg = T("Sg", (128, 8, 8))
        Mfin, b_Mfin = T("Mfin", (128, 8))
        rg, b_rg = T("rg", (128, 8, 8))
        wg, b_wg = T("wg", (128, 8, 8))
        flo, b_flo = T("flo", (128, 8, 8))
        tmpg, b_tmpg = T("tmpg", (128, 8, 8))
        Mst, b_Mst = T("Mst", (128, 128))
        Nst, b_Nst = T("Nst", (128, 128))
        ktok, _ = T("ktok", (128, 8, 256), BF16)
        b_ktok = [Buf("ktok%d" % i) for i in range(8)]
        vt_f, _ = T("vt_f", (128, 8, 2, 130), BF16)
        b_vtf = [Buf("vtf%d" % i) for i in range(8)]
        b_vtb = [Buf("vtb%d" % i) for i in range(8)]
        b_vtok = [Buf("vtok%d" % i) for i in range(8)]
        vt_b, _ = T("vt_b", (128, 8, 2, 130), BF16)
        vtok, _ = T("vtok", (128, 8, 256), BF16)
        Sb16, b_Sb16 = T("Sb16", (128, 8, 2, 130), BF16)
        Cst = []
        for i in range(4):
            t, b = T("Cst%d" % i, (128, 260))
            Cst.append((t, b, S.dma_sem("cst%d" % i)))
        C16, b_C16 = T("C16", (128, 2, 260), BF16)
        qtok, b_qtok = T("qtok", (128, 256), BF16)
        Esb, _ = T("Esb", (128, 2, 512))
        b_Esb = [[Buf("Esb0lo"), Buf("Esb0hi")], [Buf("Esb1lo"), Buf("Esb1hi")]]
        zsb, _ = T("zsb", (128, 2, 256))
        b_zsb = [Buf("zsb0"), Buf("zsb1")]
        qkT, b_qkT = T("qkT", (128, 4, 128), BF16)
        qhT, b_qhT = T("qhT", (128, 2, 2, 128), BF16)
        Pf, b_Pf = T("Pf", (128, 4, 128), BF16)
        Pb, b_Pb = T("Pb", (128, 4, 128), BF16)
        dn, _ = T("dn", (128, 8))
        b_dn = [Buf("dn_f"), Buf("dn_b")]
        ya, _ = T("ya", (128, 256))
        b_ya = [Buf("ya0"), Buf("ya1")]
        ssy, _ = T("ssy", (128, 8))
        b_ssy = [Buf("ssy%d" % i) for i in range(8)]
        ybf, b_ybf = T("ybf", (128, 256), BF16)
        rot1, b_rot1 = T("rot1", (128, 4, 32))
        rot2, b_rot2 = T("rot2", (128, 4, 32))
        L1r, b_L1r = T("L1r", (128, 8))
        nL1r, b_nL1r = T("nL1r", (128, 8))
        wend, b_wend = T("wend", (128, 2, 4))
        gC, b_gC = T("gC", (128, 8))
        gblk, b_gblk = T("gblk", (128, 2, 2))
        nLblk, b_nLblk = T("nLblk", (128, 2, 2))
        Wq, b_Wq = T("Wq", (128, 2, 2, 128))
        Mret, b_Mret = T("Mret", (128, 4, 128))
        w2s, b_w2s = T("w2s", (33, 256))
        w2x, b_w2x = T("w2x", (33, 256), BF16)
        clrT, b_clrT = T("clrT", (33, 128), BF16)
        Gc, b_Gc = T("Gc", (128, 8, 2))
        xn_parts = [(ktok[:].rearrange("p t c -> p (t c)").rearrange("p (a f) -> p a f", a=2), b_ktok),
                    (vtok[:].rearrange("p t c -> p (t c)").rearrange("p (a f) -> p a f", a=2), b_vtok)]
        EBd = [(vt_f[:].rearrange("p t j n -> p (t j n)").bitcast(F32)[:, 0:1024].rearrange("p (t n) -> p t n", t=8), b_vtf),
               (vt_b[:].rearrange("p t j n -> p (t j n)").bitcast(F32)[:, 0:1024].rearrange("p (t n) -> p t n", t=8), b_vtb)]
        junk, b_junk = ybf, b_ybf
        udiag, b_udiag = ya[0:64, 0:64], b_ya[0]
        tmpS, b_tmpS = zsb[:, 0, :], b_zsb[0]
        tm1, b_tm1 = ya[:, 0:128], b_ya[0]
        tm2, b_tm2 = rot1[:].rearrange("p a b -> p (a b)"), b_rot1
        EBn, b_EBn = ya, b_ya
        L1c, b_L1c = Esb[:, 0, 0:256], b_Esb[0][0]
        fin_bc, b_fin = bg_bc, b_bg
        d_cst = S.dma_sem("cst")
        d_x = S.dma_sem("x")
        d_misc = S.dma_sem("misc")
        d_out = S.dma_sem("out")
        d_st = S.dma_sem("stout")

        PA = PS("PA", (128, 512)); b_PA = Buf("PA", True)
        PB = PS("PB", (128, 512)); b_PB = Buf("PB", True)
        PT0 = PS("PT0", (128, 1024), BF16); b_PT0 = Buf("PT0", True)
        PT1 = PS("PT1", (128, 1024), BF16); b_PT1 = Buf("PT1", True)
        PSc = PS("PSc", (128, 512)); b_PSc = Buf("PSc", True)
        PO1 = PS("PO1", (128, 512)); b_PO1 = Buf("PO1", True)
        PO2 = PS("PO2", (128, 512)); b_PO2 = Buf("PO2", True)
        PX = PS("PX", (128, 512)); b_PU = Buf("PX", True); b_PM = b_PU
        PU = PX[:, 0:260]
        PM = PX[:, 260:512]
        PTs = [(PT0, b_PT0), (PT1, b_PT1)]
        pt_i = [0]

        def nextPT():
            pt_i[0] ^= 1
            return PTs[pt_i[0]]

        ctx.enter_context(nc.Block())

        def ACT(fn, R, W): return S.op('act', fn, R, W)
        def DVE(fn, R, W): return S.op('dve', fn, R, W)
        def PE(fn, R, W, inc=True): return S.op('pe', fn, R, W, inc)

        def mm(out, lhsT, rhs, start, stop, R, W, tp=None, inc=True):
            if tp is not None:
                return PE(lambda e: e.matmul(out, lhsT=lhsT, rhs=rhs, start=start, stop=stop, tile_position=tp), R, W, inc)
            return PE(lambda e: e.matmul(out, lhsT=lhsT, rhs=rhs, start=start, stop=stop), R, W, inc)

        def tr(out, in_, ident, R, W, inc=True):
            return PE(lambda e: e.transpose(out, in_, ident), R, W, inc)

        def act(out, in_, func, R, W, scale=1.0, bias=0.0, accum=None):
            if accum is None:
                return ACT(lambda e: e.activation(out=out, in_=in_, func=func, scale=scale, bias=bias), R, W)
            return ACT(lambda e: e.activation(out=out, in_=in_, func=func, scale=scale, bias=bias, accum_out=accum), R, W)

        def tt(out, in0, in1, op, R, W):
            return DVE(lambda e: e.tensor_tensor(out=out, in0=in0, in1=in1, op=op), R, W)

        def ts(out, in0, s1, s2, op0, R, W, op1=None):
            if op1 is None:
                return DVE(lambda e: e.tensor_scalar(out, in0, s1, None, op0=op0), R, W)
            return DVE(lambda e: e.tensor_scalar(out, in0, s1, s2, op0=op0, op1=op1), R, W)

        def stt(out, in0, scalar, in1, op0, op1, R, W):
            return DVE(lambda e: e.scalar_tensor_tensor(out=out, in0=in0, scalar=scalar, in1=in1, op0=op0, op1=op1), R, W)

        def cp(out, in_, R, W):
            return DVE(lambda e: e.tensor_copy(out, in_), R, W)

        def rsq(out, in_, n, R, W, tmp):
            act(tmp, in_, AF.Ln, R, [W[0]], scale=1.0 / n, bias=EPS)
            act(out, tmp, AF.Exp, [W[0]], W, scale=-0.5)

        cI = lambda c0, n=128: cst[:, c0:c0 + n]
        idf = cI(C_IDF)
        ones = cI(C_ONES)

        S.dma('sp', d_cst, cst[:], cst_d, writes=[b_cst])
        for g in range(4):
            S.dma('sp', d_x, x_sb[:, g * 4:(g + 1) * 4, :],
                  x_d[g * 512:(g + 1) * 512, :].rearrange("(t p) f -> p t f", p=128), writes=bx[g * 4:(g + 1) * 4])
        S.dma('sp', d_misc, gb_bc[:], gateb_d.partition_broadcast(128), writes=[b_gb])
        S.dma('sp', d_misc, m0_bc[:], sm_d.partition_broadcast(128), writes=[b_m0])
        S.dma('sp', d_misc, rl_bc[:], retl_d.partition_broadcast(128), writes=[b_rl])
        for v in range(2):
            S.dma('sp', d_misc, cT[:, :, v], cvec_d[v].rearrange("(kc p) -> p kc", p=128), writes=[b_cT], slow=True)
        for (t_, b_, src, off) in ((ngT, b_ngT, normg_d, 0), (hngT, b_hngT, hng_d, 0),
                                   (bshT, b_bshT, bada_d, 0), (bscT, b_bscT, bada_d, D)):
            for l in range(DEPTH):
                S.dma('sp', d_misc, t_[:, l, :], src[l, off:off + D].rearrange("(kc p) -> p kc", p=128), writes=[b_], slow=True)
        cp(idb[:], idf, [b_cst], [b_idb])
        cp(uincb[:], cI(C_UINC), [b_cst], [b_uincb])
        cp(lincb[:], cI(C_LINC), [b_cst], [b_lincb])
        DVE(lambda e: e.memset(Mst[:], 0.0), [], [b_Mst])
        DVE(lambda e: e.memset(Nst[:], 0.0), [], [b_Nst])
        DVE(lambda e: e.memset(w2s[:], 0.0), [], [b_w2s])
        DVE(lambda e: e.memset(clrT[:], 1.0), [], [b_clrT])
        for i in range(4):
            DVE(lambda e: e.memset(Cst[i][0][:], 0.0), [], [Cst[i][1]])
        act(tmpS[:, 0:16], cT[:].rearrange("p k v -> p (k v)"), AF.Exp, [b_cT], [b_tmpS], scale=-1.0)
        ts(tmpS[:, 0:16], tmpS[:, 0:16], 1.0, None, ALU.add, [b_tmpS], [b_tmpS])
        DVE(lambda e: e.reciprocal(tmpS[:, 0:16], tmpS[:, 0:16]), [b_tmpS], [b_tmpS])
        tt(scT[:].rearrange("p k v -> p (k v)"), tmpS[:, 0:16], cT[:].rearrange("p k v -> p (k v)"), ALU.mult, [b_tmpS, b_cT], [b_scT])
        for v in range(2):
            cp(screp[:, v, :, :], scT[:, :, v:v + 1].to_broadcast([128, 8, 128]), [b_scT], [b_screp])

        ring_state = {'n': 0}

        def load_block(l, pieces, wsrc):
            s = ring_state['n'] % NSLOT
            ring_state['n'] += 1
            wt, wb_, ws = Wr[s]
            c = 0
            for (c0, n) in pieces:
                S.dma('pool', ws, wt[:, :, c:c + n], wsrc[l, :, c0:c0 + n].rearrange("(kc p) n -> p kc n", p=128), writes=[wb_])
                c += n
            return wt, wb_

        def proj(ps, psb, tloc, wt, wb_, c0, n, hb):
            toks = slice(tloc * 128, (tloc + 1) * 128)
            for kc in range(8):
                mm(ps, hT[:, kc, toks], wt[:, kc, c0:c0 + n], kc == 0, kc == 7, [hb, wb_], [psb], inc=(kc == 7))

        ada_state = {'n': 0}

        ada_q = []

        def ada_prefetch(l, blk):
            sl = ada_state['n'] % 2
            ada_state['n'] += 1
            at, ab, asem = ada[sl]
            S.dma('pool', asem, at[:], wada_d[l, :, blk * 128:(blk + 1) * 128].rearrange("(kc p) n -> p kc n", p=128), writes=[ab])
            ada_q.append((l, blk, at, ab))

        def ada_consume(l, blk):
            l_, blk_, at, ab = ada_q.pop(0)
            assert (l_, blk_) == (l, blk), (l_, blk_, l, blk)
            if blk < 16:
                for kc in range(8):
                    mm(PM[:, 0:2], at[:, kc, :], scT[:, kc, :], kc == 0, kc == 7, [ab, b_scT], [b_PM], inc=(kc == 7))
                cp(modraw[:, blk, :], PM[:, 0:2], [b_PM], [b_modraw])
            else:
                cols = slice((blk - 16) * 128, (blk - 15) * 128)
                for v in range(2):
                    for kc in range(8):
                        mm(PO1[:, v * 128:(v + 1) * 128], screp[:, v, kc, :], at[:, kc, :], kc == 0, kc == 7, [ab, b_screp], [b_PO1], inc=(kc == 7))
                tt(gate_bc[:, :, cols], PO1[:, 0:256].rearrange("p (v n) -> p v n", v=2), bg_bc[:, cols].unsqueeze(1).to_broadcast([128, 2, 128]), ALU.add,
                   [b_PO1, b_bg], [b_gate])

        def mod_ss_finish(l):
            par = l % 2
            gsT, shT = gsT2[:, par], shT2[:, par]
            tt(shT, modraw[:, 0:8, :], bshT[:, l, :].unsqueeze(2).to_broadcast([128, 8, 2]), ALU.add, [b_modraw, b_bshT], [b_shTp[par]])
            tt(gsT, modraw[:, 8:16, :], bscT[:, l, :].unsqueeze(2).to_broadcast([128, 8, 2]), ALU.add, [b_modraw, b_bscT], [b_gsTp[par]])
            ts(gsT, gsT, 1.0, None, ALU.add, [b_gsTp[par]], [b_gsTp[par]])
            tt(gsT, gsT, ngT[:, l, :].unsqueeze(2).to_broadcast([128, 8, 2]), ALU.mult, [b_gsTp[par], b_ngT], [b_gsTp[par]])

        def mod_stream(l, blks):
            blks = list(blks)
            state = {'i': 0}
            for b_ in blks[0:2]:
                ada_prefetch(l, b_)

            def step():
                i = state['i']
                if i >= len(blks):
                    return
                ada_consume(l, blks[i])
                if i + 2 < len(blks):
                    ada_prefetch(l, blks[i + 2])
                state['i'] = i + 1
            return step

        def load_bg(l):
            S.dma('sp', d_misc, bg_bc[:], bada_d[l, 2 * D:3 * D].partition_broadcast(128), writes=[b_bg])

        def emit_ret_statics(l):
            lg = rl_bc[:, l * 8:(l + 1) * 8]
            act(L1r[:], lg, AF.Exp, [b_rl], [b_L1r], scale=-1.0)
            act(L1r[:], L1r[:], AF.Ln, [b_L1r], [b_L1r], bias=1.0)
            ts(nL1r[:], L1r[:], -1.0, None, ALU.mult, [b_L1r], [b_nL1r])
            act(wend[:, 0, :], L1r[:, 0:4], AF.Exp, [b_L1r, b_cst], [b_wend], scale=cst[:, C_COLA:C_COLA + 1])
            act(wend[:, 1, :], L1r[:, 4:8], AF.Exp, [b_L1r, b_cst], [b_wend], scale=cst[:, C_COLB:C_COLB + 1])
            ts(wend[:].rearrange("p d h -> p (d h)"), wend[:].rearrange("p d h -> p (d h)"), 0.125, None, ALU.mult, [b_wend], [b_wend])
            act(gC[:], L1r[:], AF.Exp, [b_L1r], [b_gC], scale=-128.0)
            for d in range(2):
                for blk in range(2):
                    for hh in range(2):
                        pr = slice(hh * 64, (hh + 1) * 64)
                        h = blk * 2 + hh
                        cp(gblk[pr, d, blk:blk + 1], gC[pr, d * 4 + h:d * 4 + h + 1], [b_gC], [b_gblk])
                        cp(nLblk[pr, d, blk:blk + 1], nL1r[pr, d * 4 + h:d * 4 + h + 1], [b_nL1r], [b_nLblk])
            for d in range(2):
                for blk in range(2):
                    act(Wq[:, d, blk, :], cI(C_POSF if d == 0 else C_POSB), AF.Exp, [b_cst, b_nLblk], [b_Wq], scale=nLblk[:, d, blk:blk + 1])
            for h in range(4):
                act(tm1, cI(C_RIJ), AF.Exp, [b_cst, b_nL1r], [b_tm1], scale=nL1r[:, h:h + 1])
                act(tm2, cI(C_RJI), AF.Exp, [b_cst, b_nL1r], [b_tm2], scale=nL1r[:, 4 + h:5 + h])
                tt(tm1, tm1, cI(C_UINC), ALU.mult, [b_tm1, b_cst], [b_tm1])
                tt(tm2, tm2, cI(C_LINC), ALU.mult, [b_tm2, b_cst], [b_tm2])
                tt(Mret[:, h, :], tm1, tm2, ALU.add, [b_tm1, b_tm2], [b_Mret])
                ts(Mret[:, h, :], Mret[:, h, :], 0.125, None, ALU.mult, [b_Mret], [b_Mret])

        def emit_gla_statics(l):
            S.dma('sp', d_misc, w2s[0:16, 0:128], w2_d[l, 0], writes=[b_w2s])
            S.dma('sp', d_misc, w2s[16:32, 128:256], w2_d[l, 1], writes=[b_w2s])
            S.dma('sp', d_misc, w2s[32:33, :], b2_d[l:l + 1, :], writes=[b_w2s])
            cp(w2x[:], w2s[:], [b_w2s], [b_w2x])

        def emit_norm(l, half):
            v = half
            gsT, shT = gsT2[:, l % 2], shT2[:, l % 2]
            b_gsT, b_shT = b_gsTp[l % 2], b_shTp[l % 2]
            for t in range(8):
                gt = half * 8 + t
                act(hT[:, t, :], x_sb[:, gt, :], AF.Square, [bx[gt]], [b_hT[0], b_hT[1], b_ss], accum=ss[:, t:t + 1])
            rsq(rstd[:, 0:8], ss[:, 0:8], float(D), [b_ss], [b_rstd], ss[:, 8:16])
            for g in range(2):
                for tl in range(4):
                    t = g * 4 + tl
                    gt = half * 8 + t
                    xp, xb_ = xn_parts[tl // 2]
                    ts(xp[:, tl % 2, :], x_sb[:, gt, :], rstd[:, t:t + 1], None, ALU.mult, [bx[gt], b_rstd], [xb_])
                for kc in range(8):
                    pt, ptb = nextPT()
                    for tl in range(4):
                        xp, xb_ = xn_parts[tl // 2]
                        tr(pt[:, tl * 128:(tl + 1) * 128], xp[:, tl % 2, kc * 128:(kc + 1) * 128], idb[:], [xb_, b_idb], [ptb], inc=(tl == 3))
                    dst = hT[:, kc, g * 512:(g + 1) * 512]
                    if kc % 2 == 0:
                        act(dst, pt[:, 0:512], AF.Identity, [ptb, b_gsT, b_shT], [b_hT[g]], scale=gsT[:, kc, v:v + 1], bias=shT[:, kc, v:v + 1])
                    else:
                        ts(dst, pt[:, 0:512], gsT[:, kc, v:v + 1], shT[:, kc, v:v + 1], ALU.mult, [ptb, b_gsT, b_shT], [b_hT[g]], op1=ALU.add)

        hb_of = lambda t: b_hT[t // 4]

        def sig_inplace(par, n, lo=0):
            bb = b_Esb[par][lo // 256:(lo + n + 255) // 256]
            act(Esb[:, par, lo:lo + n], Esb[:, par, lo:lo + n], AF.Ln, bb, bb, bias=1.0)
            act(Esb[:, par, lo:lo + n], Esb[:, par, lo:lo + n], AF.Exp, bb, bb, scale=-1.0)

        def finish_y(l, t, par, src, nh, dh, kc0, zoff):
            n = nh * dh
            for h in range(nh):
                act(junk[:, h * dh:(h + 1) * dh], src[:, h * dh:(h + 1) * dh], AF.Square, [b_ya[(h * dh) // 128]], [b_junk, b_ssy[h]], accum=ssy[:, h:h + 1])
            rsq(ssy[:, 0:nh], ssy[:, 0:nh], float(dh), b_ssy[0:nh], b_ssy[0:nh], ssy[:, 4:4 + nh])
            tt(zsb[:, par, 0:n], zsb[:, par, 0:n], Esb[:, par, zoff:zoff + n], ALU.mult, [b_zsb[par], b_Esb[par][zoff // 256]], [b_zsb[par]])
            tt(src, src, zsb[:, par, 0:n], ALU.mult, [b_ya, b_zsb[par]], [b_ya])
            tt(ybf[:, 0:n].rearrange("p (h e) -> p h e", h=nh), src.rearrange("p (h e) -> p h e", h=nh),
               ssy[:, 0:nh].unsqueeze(2).to_broadcast([128, nh, dh]), ALU.mult, [b_ya, b_ssy[0:nh]], [b_ybf])

        def y_transpose(l, t, kc0):
            pt, ptb = nextPT()
            for j in range(2):
                tr(pt[:, j * 128:(j + 1) * 128], ybf[:, j * 128:(j + 1) * 128], idb[:], [b_ybf, b_idb], [ptb], inc=(j == 1))
            for j in range(2):
                act(yT[:, kc0 + j, t * 128:(t + 1) * 128], pt[:, j * 128:(j + 1) * 128], AF.Identity, [ptb, b_hngT], [b_yT[t]],
                    scale=hngT[:, l, kc0 + j:kc0 + j + 1])

        seqs_of = lambda half: [(0, 2), (2, 4), (4, 6), (6, 8)] if half == 0 else [(0, 8)]

        pending_tail = []

        def flush_tail():
            while pending_tail:
                pending_tail.pop(0)()

        def run_pass2(half, Fp, Feq, Fer, M1, M2, B1, B2, Yt, seq_begin, seq_end):
            starts = {t0: si for si, (t0, t1) in enumerate(seqs_of(half))}
            ends = {t1 - 1: si for si, (t0, t1) in enumerate(seqs_of(half))}
            Fp(0)
            Feq(0)
            M1(0)
            Fer(0)
            Fp(1)
            for c in range(8):
                if c in starts:
                    seq_begin(starts[c])
                M2(c)
                if c in ends:
                    seq_end(ends[c])
                if c + 1 < 8:
                    Feq(c + 1)
                    M1(c + 1)
                B1(c)
                if c + 1 < 8:
                    Fer(c + 1)
                if c + 2 < 8:
                    Fp(c + 2)
                B2(c)
                if c < 7:
                    Yt(c)
                else:
                    pending_tail.append(lambda: Yt(7))

        def emit_gates(l, half):
            for d in range(2):
                act(L1g[:, d, :, :], Gs[:, :, d * 8 + 4:d * 8 + 8], AF.Exp, [b_Gs], [b_L1g], scale=-1.0)
                yield
            l1flat = L1g[:].rearrange("p d t h -> p (d t h)")
            act(l1flat, l1flat, AF.Ln, [b_L1g], [b_L1g], bias=1.0)
            yield
            mm(PM[:, 0:32], cI(C_UINC), l1flat[:, 0:32], True, True, [b_cst, b_L1g], [b_PM], inc=False)
            yield
            mm(PM[:, 32:64], cI(C_LINC), l1flat[:, 32:64], True, True, [b_cst, b_L1g], [b_PM], inc=False)
            yield
            mm(PM[:, 64:128], ones, l1flat, True, True, [b_cst, b_L1g], [b_PM])
            yield
            for d in range(2):
                cp(Fp[:, :, d * 4:(d + 1) * 4], PM[:, d * 32:(d + 1) * 32].rearrange("p (t h) -> p t h", t=8), [b_PM], [b_Fp])
                yield
                cp(Fp[:, :, 8 + d * 4:12 + d * 4], PM[:, 64 + d * 32:96 + d * 32].rearrange("p (t h) -> p t h", t=8), [b_PM], [b_Fp])
                yield
            for d in range(2):
                tt(ug[:, :, d * 4:(d + 1) * 4], Fp[:, :, d * 4:(d + 1) * 4], Gs[:, :, d * 8:d * 8 + 4], ALU.add, [b_Fp, b_Gs], [b_ug])
                yield
            PE(lambda e: e.transpose(PM[0:64, 0:128], ug[:].rearrange("p t g -> p (t g)"), idf), [b_ug, b_cst], [b_PM])
            yield
            DVE(lambda e: e.reduce_max(umax[:], PM[0:64, 0:128], axis=AX.X), [b_PM], [b_umax])
            yield
            ts(udiag, cst[0:64, C_IDF:C_IDF + 64], umax[:, 0:1], None, ALU.mult, [b_cst, b_umax], [b_udiag])
            yield
            mm(PM[:, 128:192], cst[0:64, C_ONES:C_ONES + 128], udiag, True, True, [b_cst, b_udiag], [b_PM])
            yield
            cp(Abc[:].rearrange("p t g -> p (t g)"), PM[:, 128:192], [b_PM], [b_Abc])
            yield
            if half == 0:
                v4 = lambda X: X[:].rearrange("p (s k) g -> p s k g", k=2)
                mst = Mst[:].rearrange("p (s r) -> p s r", s=4)
                for d in range(2):
                    sl = slice(d * 4, (d + 1) * 4)
                    tl = slice(8 + d * 4, 12 + d * 4)
                    k0, k1 = (0, 1) if d == 0 else (1, 0)
                    DVE(lambda e: e.memset(v4(MP)[:, :, k0, sl], 0.0), [], [b_MP])
                    yield
                    tt(v4(Sg)[:, :, k0, sl], v4(MP)[:, :, k0, sl], v4(Abc)[:, :, k0, sl], ALU.max, [b_MP, b_Abc], [b_Sg])
                    yield
                    tt(v4(MP)[:, :, k1, sl], v4(Sg)[:, :, k0, sl], v4(Fp)[:, :, k0, tl], ALU.subtract, [b_Sg, b_Fp], [b_MP])
                    yield
                    tt(v4(Sg)[:, :, k1, sl], v4(MP)[:, :, k1, sl], v4(Abc)[:, :, k1, sl], ALU.max, [b_MP, b_Abc], [b_Sg])
                    yield
                    tt(mst[:, :, l * 8 + d * 4:l * 8 + d * 4 + 4], v4(Sg)[:, :, k1, sl], v4(Fp)[:, :, k1, tl], ALU.subtract, [b_Sg, b_Fp], [b_Mst])
                    yield
            else:
                orders = [list(range(8)), list(range(7, -1, -1))]
                for d in range(2):
                    sl = slice(d * 4, (d + 1) * 4)
                    cp(MP[:, orders[d][0], sl], m0_bc[:, l * 8 + d * 4:l * 8 + d * 4 + 4], [b_m0], [b_MP])
                    yield
                for i in range(8):
                    for d in range(2):
                        sl = slice(d * 4, (d + 1) * 4)
                        tl = slice(8 + d * 4, 12 + d * 4)
                        c = orders[d][i]
                        tt(Sg[:, c, sl], MP[:, c, sl], Abc[:, c, sl], ALU.max, [b_MP, b_Abc], [b_Sg])
                        yield
                        if i + 1 < 8:
                            tt(MP[:, orders[d][i + 1], sl], Sg[:, c, sl], Fp[:, c, tl], ALU.subtract, [b_Sg, b_Fp], [b_MP])
                            yield
            tt(tmpg[:], MP[:], Sg[:], ALU.subtract, [b_MP, b_Sg], [b_tmpg])
            yield
            act(rg[:], tmpg[:], AF.Exp, [b_tmpg], [b_rg])
            yield
            tt(tmpg[:], ug[:], Sg[:], ALU.subtract, [b_ug, b_Sg], [b_tmpg])
            yield
            act(wg[:], tmpg[:], AF.Exp, [b_tmpg], [b_wg])
            yield
            tt(tmpg[:], Fp[:, :, 0:8], Sg[:], ALU.subtract, [b_Fp, b_Sg], [b_tmpg])
            yield
            act(flo[:], tmpg[:], AF.Exp, [b_tmpg], [b_flo])
            yield

        def state_init_A(l, half, j, d, h, slot=None):
            st, sb, ssem = Cst[j * 2 + (d if slot is None else slot)]
            if half == 0:
                DVE(lambda e: e.memset(st[:, 0:130], 0.0), [], [sb])
            else:
                S.dma('sp', ssem, st[:, 0:128], sC_d[l, d, h], writes=[sb])
                S.dma('sp', ssem, st[:, 128:129], sn_d[l, d, h].rearrange("(p o) -> p o", o=1), writes=[sb], slow=True)

        def state_out_A(l, si, j, d, h, slot=None):
            st, sb, ssem = Cst[j * 2 + (d if slot is None else slot)]
            S.dma('sp', ssem, oC_d[si, l, d, h], st[:, 0:128], reads=[sb])
            col = ((si * DEPTH + l) * 2 + d) * 4 + h
            cp(Nst[:, col:col + 1], st[:, 128:129], [sb], [b_Nst])

        def emit_A_pair(l, half, p, wKV, wQO, wZ, after_p1=lambda: None):
            (wkv, bkv), (wqo, bqo), (wz, bz) = wKV, wQO, wZ
            h0 = 2 * p
            ksc = 128.0 ** -0.5
            for t in range(8):
                ps, psb = (PA, b_PA) if t % 2 == 0 else (PB, b_PB)
                proj(ps[:], psb, t, wkv, bkv, 0, 512, hb_of(t))
                if t % 2 == 0:
                    act(ktok[:, t, :], ps[:, 0:256], AF.Identity, [psb], [b_ktok[t]], scale=ksc)
                    for j in range(2):
                        h = h0 + j
                        act(vt_f[:, t, j, 0:128], ps[:, 256 + j * 128:384 + j * 128], AF.Identity, [psb, b_wg], [b_vtf[t]], scale=wg[:, t, h:h + 1])
                        act(vt_b[:, t, j, 0:128], ps[:, 256 + j * 128:384 + j * 128], AF.Identity, [psb, b_wg], [b_vtb[t]], scale=wg[:, t, 4 + h:5 + h])
                else:
                    ts(ktok[:, t, :], ps[:, 0:256], ksc, None, ALU.mult, [psb], [b_ktok[t]])
                    for j in range(2):
                        h = h0 + j
                        ts(vt_f[:, t, j, 0:128], ps[:, 256 + j * 128:384 + j * 128], wg[:, t, h:h + 1], None, ALU.mult, [psb, b_wg], [b_vtf[t]])
                        ts(vt_b[:, t, j, 0:128], ps[:, 256 + j * 128:384 + j * 128], wg[:, t, 4 + h:5 + h], None, ALU.mult, [psb, b_wg], [b_vtb[t]])
                if t == 1:
                    flush_tail()
            for j in range(2):
                cp(vt_f[:, :, j, 128:129], wg[:, :, h0 + j:h0 + j + 1], [b_wg], [b_vtf])
                cp(vt_b[:, :, j, 128:129], wg[:, :, 4 + h0 + j:5 + h0 + j], [b_wg], [b_vtb])
            after_p1()
            for si, (t0, t1) in enumerate(seqs_of(half)):
                slot = 1 - (si % 2)
                for j in range(2):
                    state_init_A(l, half, j, 1, h0 + j, slot)
                cur = [slot, slot]
                for c in range(t1 - 1, t0 - 1, -1):
                    for j in range(2):
                        h = h0 + j
                        st, sb, _ = Cst[j * 2 + cur[j]]
                        if half == 1:
                            cur[j] = 1 - cur[j]
                        so, sob, _ = Cst[j * 2 + cur[j]]
                        pu, pub = (PU, b_PU) if j == 0 else (PSc, b_PSc)
                        act(Sb16[:, c, j, :], st[:, 0:130], AF.Identity, [sb, b_rg], [b_Sb16], scale=rg[:, c, 4 + h:5 + h])
                        mm(pu[:, 0:129], ktok[:, c, j * 128:(j + 1) * 128], vt_b[:, c, j, 0:129], True, True, [b_ktok[c], b_vtb[c]], [pub])
                        stt(so[:, 0:129], st[:, 0:129], rg[:, c, 4 + h:5 + h], pu[:, 0:129], ALU.mult, ALU.add, [sb, b_rg, pub], [sob])
                if half == 0:
                    for j in range(2):
                        state_out_A(l, si, j, 1, h0 + j, slot)

            def Fp(c):
                proj(PA[:], b_PA, c, wqo, bqo, 0, 512, hb_of(c))
                proj(PB[:, 0:256], b_PB, c, wz, bz, 0, 256, hb_of(c))

            def Feq(c):
                act(qtok[:], PA[:, 0:256], AF.Identity, [b_PA], [b_qtok])

            def Fer(c):
                par = c % 2
                act(Esb[:, par, 0:256], PA[:, 256:512], AF.Exp, [b_PA], [b_Esb[par][0]], scale=-1.0)
                act(Esb[:, par, 256:512], PB[:, 0:256], AF.Exp, [b_PB], [b_Esb[par][1]], scale=-1.0)
                act(zsb[:, par, :], PB[:, 0:256], AF.Identity, [b_PB], [b_zsb[par]])
                sig_inplace(par, 512)

            def M1(c):
                pt, ptb = nextPT()
                for j in range(2):
                    tr(pt[:, j * 128:(j + 1) * 128], qtok[:, j * 128:(j + 1) * 128], idb[:], [b_qtok, b_idb], [ptb], inc=False)
                    tr(pt[:, (2 + j) * 128:(3 + j) * 128], ktok[:, c, j * 128:(j + 1) * 128], idb[:], [b_ktok[c], b_idb], [ptb], inc=(j == 1))
                cp(qkT[:].rearrange("p a b -> p (a b)"), pt[:, 0:512], [ptb], [b_qkT])
                for j in range(2):
                    mm(PSc[:, j * 128:(j + 1) * 128], qkT[:, 2 + j, :], qkT[:, j, :], True, True, [b_qkT], [b_PSc], inc=(j == 1))

            def M2(c):
                psv = PSc[:, 0:256].rearrange("p (j i) -> p j i", j=2)
                tt(Pf[:, 0:2, :], psv, uincb[:].unsqueeze(1).to_broadcast([128, 2, 128]), ALU.mult, [b_PSc, b_uincb], [b_Pf])
                tt(Pb[:, 0:2, :], psv, lincb[:].unsqueeze(1).to_broadcast([128, 2, 128]), ALU.mult, [b_PSc, b_lincb], [b_Pb])
                for j in range(2):
                    h = h0 + j
                    st, sb, _ = Cst[j * 2 + 0]
                    act(C16[:, j, 0:130], st[:, 0:130], AF.Identity, [sb, b_rg], [b_C16], scale=rg[:, c, h:h + 1])
                for j in range(2):
                    mm(PO1[:, j * 256:j * 256 + 129], Pf[:, j, :], vt_f[:, c, j, 0:129], True, False, [b_Pf, b_vtf[c]], [b_PO1], inc=False)
                    mm(PO1[:, j * 256:j * 256 + 129], qkT[:, j, :], C16[:, j, 0:129], False, True, [b_qkT, b_C16], [b_PO1], inc=(j == 1))
                for j in range(2):
                    mm(PO2[:, j * 256:j * 256 + 129], Pb[:, j, :], vt_b[:, c, j, 0:129], True, False, [b_Pb, b_vtb[c]], [b_PO2], inc=False)
                    mm(PO2[:, j * 256:j * 256 + 129], qkT[:, j, :], Sb16[:, c, j, 0:129], False, True, [b_qkT, b_Sb16], [b_PO2], inc=(j == 1))
                for j in range(2):
                    st, sb, _ = Cst[j * 2 + 0]
                    mm(PU[:, 0:129], ktok[:, c, j * 128:(j + 1) * 128], vt_f[:, c, j, 0:129], True, True, [b_ktok[c], b_vtf[c]], [b_PU])
                    stt(st[:, 0:129], st[:, 0:129], rg[:, c, h0 + j:h0 + j + 1], PU[:, 0:129], ALU.mult, ALU.add, [sb, b_rg, b_PU], [sb])

            def B1(c):
                po1 = PO1[:].rearrange("p (j n) -> p j n", j=2)
                po2 = PO2[:].rearrange("p (j n) -> p j n", j=2)
                act(dn[:, 0:2].unsqueeze(2), po1[:, :, 128:129], AF.Abs, [b_PO1], [b_dn[0]])
                act(dn[:, 2:4].unsqueeze(2), po2[:, :, 128:129], AF.Abs, [b_PO2], [b_dn[1]])
                tt(dn[:, 0:2], dn[:, 0:2], flo[:, c, h0:h0 + 2], ALU.max, [b_dn[0], b_flo], [b_dn[0]])
                tt(dn[:, 2:4], dn[:, 2:4], flo[:, c, 4 + h0:6 + h0], ALU.max, [b_dn[1], b_flo], [b_dn[1]])
                DVE(lambda e: e.reciprocal(dn[:, 0:4], dn[:, 0:4]), [b_dn], [b_dn])
                tt(ya[:].rearrange("p (j e) -> p j e", j=2), po1[:, :, 0:128], dn[:, 0:2].unsqueeze(2).to_broadcast([128, 2, 128]), ALU.mult,
                   [b_PO1, b_dn], [b_ya])
                for j in range(2):
                    stt(ya[:, j * 128:(j + 1) * 128], po2[:, j, 0:128], dn[:, 2 + j:3 + j], ya[:, j * 128:(j + 1) * 128], ALU.mult, ALU.add,
                        [b_PO2, b_dn, b_ya[j]], [b_ya[j]])

            def B2(c):
                par = c % 2
                tt(ya[:], ya[:], Esb[:, par, 0:256], ALU.mult, [b_ya, b_Esb[par][0]], [b_ya])
                finish_y(l, c, par, ya[:], 2, 128, h0, 256)

            def Yt(c):
                y_transpose(l, c, h0)

            def seq_begin(si):
                for j in range(2):
                    state_init_A(l, half, j, 0, h0 + j)

            def seq_end(si):
                if half == 0:
                    for j in range(2):
                        state_out_A(l, si, j, 0, h0 + j)

            run_pass2(half, Fp, Feq, Fer, M1, M2, B1, B2, Yt, seq_begin, seq_end)

        def rotary(dst, ps, t, R, W):
            cs = cst[:, C_COS + t * 32:C_COS + (t + 1) * 32].unsqueeze(1).to_broadcast([128, 4, 32])
            sn = cst[:, C_SIN + t * 32:C_SIN + (t + 1) * 32].unsqueeze(1).to_broadcast([128, 4, 32])
            pv = ps.rearrange("p (h e) -> p h e", h=4)
            dv = dst.rearrange("p (h e) -> p h e", h=4)
            t1, t2 = pv[:, :, 0:32], pv[:, :, 32:64]
            tt(rot1[:], t1, cs, ALU.mult, R + [b_cst], [b_rot1])
            tt(rot2[:], t2, sn, ALU.mult, R + [b_cst], [b_rot2])
            tt(dv[:, :, 0:32], rot1[:], rot2[:], ALU.subtract, [b_rot1, b_rot2], W)
            tt(rot1[:], t1, sn, ALU.mult, R + [b_cst], [b_rot1])
            tt(rot2[:], t2, cs, ALU.mult, R + [b_cst], [b_rot2])
            tt(dv[:, :, 32:64], rot1[:], rot2[:], ALU.add, [b_rot1, b_rot2], W)

        def emit_B(l, half, wKV, wQZ):
            (wkv, bkv), (wqz, bqz) = wKV, wQZ
            vhf = vt_f[:].rearrange("p t j n -> p t (j n)")
            vhb = vt_b[:].rearrange("p t j n -> p t (j n)")
            sb16 = Sb16[:].rearrange("p t j n -> p t (j n)")
            for t in range(8):
                ps, psb = (PA, b_PA) if t % 2 == 0 else (PB, b_PB)
                proj(ps[:], psb, t, wkv, bkv, 0, 512, hb_of(t))
                pav = ps[:, 256:512].rearrange("p (h e) -> p h e", h=4)
                if half == 0:
                    act(ktok[:, t, :], ps[:, 0:256], AF.Identity, [psb], [b_ktok[t]])
                else:
                    rotary(ktok[:, t, :], ps[:, 0:256], t, [psb], [b_ktok[t]])
                act(vtok[:, t, :], ps[:, 256:512], AF.Identity, [psb], [b_vtok[t]])
                tt(vhf[:, t, 0:256].rearrange("p (h e) -> p h e", h=4), pav, wend[:, 0, :].unsqueeze(2).to_broadcast([128, 4, 64]), ALU.mult, [psb, b_wend], [b_vtf[t]])
                tt(vhb[:, t, 0:256].rearrange("p (h e) -> p h e", h=4), pav, wend[:, 1, :].unsqueeze(2).to_broadcast([128, 4, 64]), ALU.mult, [psb, b_wend], [b_vtb[t]])
                if t == 1:
                    flush_tail()

            if MKX == 21:
                return

            def st_init(blk, d, slot=None):
                st, sb, ssem = Cst[blk * 2 + (d if slot is None else slot)]
                if half == 0:
                    DVE(lambda e: e.memset(st[:, 0:128], 0.0), [], [sb])
                else:
                    for hh in range(2):
                        S.dma('sp', ssem, st[hh * 64:(hh + 1) * 64, hh * 64:(hh + 1) * 64], sret_d[l, d, blk * 2 + hh], writes=[sb])

            def st_out(si, blk, d, slot=None):
                st, sb, ssem = Cst[blk * 2 + (d if slot is None else slot)]
                for hh in range(2):
                    S.dma('sp', ssem, oret_d[si, l, d, blk * 2 + hh], st[hh * 64:(hh + 1) * 64, hh * 64:(hh + 1) * 64], reads=[sb])

            for si, (t0, t1) in enumerate(seqs_of(half)):
                slot = 1 - (si % 2)
                for blk in range(2):
                    st_init(blk, 1, slot)
                for c in range(t1 - 1, t0 - 1, -1):
                    for blk in range(2):
                        st, sb, _ = Cst[blk * 2 + slot]
                        bs = slice(blk * 128, (blk + 1) * 128)
                        pu, pub = (PU, b_PU) if blk == 0 else (PSc, b_PSc)
                        tt(sb16[:, c, bs], st[:, 0:128], cI(C_BD2), ALU.mult, [sb, b_cst], [b_Sb16])
                        mm(pu[:, 0:128], ktok[:, c, bs], vhb[:, c, bs], True, True, [b_ktok[c], b_vtb[c]], [pub])
                        stt(st[:, 0:128], st[:, 0:128], gblk[:, 1, blk:blk + 1], pu[:, 0:128], ALU.mult, ALU.add, [sb, b_gblk, pub], [sb])
                if half == 0:
                    for blk in range(2):
                        st_out(si, blk, 1, slot)
            if MKX == 22:
                return

            def Fp(c):
                proj(PA[:], b_PA, c, wqz, bqz, 0, 512, hb_of(c))

            def Feq(c):
                if half == 0:
                    act(qtok[:], PA[:, 0:256], AF.Identity, [b_PA], [b_qtok])
                else:
                    rotary(qtok[:], PA[:, 0:256], c, [b_PA], [b_qtok])

            def Fer(c):
                par = c % 2
                act(Esb[:, par, 256:512], PA[:, 256:512], AF.Exp, [b_PA], [b_Esb[par][1]], scale=-1.0)
                act(zsb[:, par, :], PA[:, 256:512], AF.Identity, [b_PA], [b_zsb[par]])
                sig_inplace(par, 256, 256)

            def M1(c):
                pt, ptb = nextPT()
                for blk in range(2):
                    tr(pt[:, blk * 128:(blk + 1) * 128], qtok[:, blk * 128:(blk + 1) * 128], idb[:], [b_qtok, b_idb], [ptb], inc=False)
                    tr(pt[:, (2 + blk) * 128:(3 + blk) * 128], ktok[:, c, blk * 128:(blk + 1) * 128], idb[:], [b_ktok[c], b_idb], [ptb], inc=(blk == 1))
                cp(qkT[:].rearrange("p a b -> p (a b)"), pt[:, 0:512], [ptb], [b_qkT])
                for d in range(2):
                    tt(qhT[:, d, :, :], qkT[:, 0:2, :], Wq[:, d, :, :], ALU.mult, [b_qkT, b_Wq], [b_qhT])
                qbd = Pb[:].rearrange("p a b -> p (a b)").rearrange("p (k a i) -> p k a i", k=2, a=2)
                for blk in range(2):
                    tt(qbd[:, blk, :, :], qkT[:, blk, :].unsqueeze(1).to_broadcast([128, 2, 128]),
                       cst[:, C_MC2:C_MC2 + 2].unsqueeze(2).to_broadcast([128, 2, 128]), ALU.mult, [b_qkT, b_cst], [b_Pb])
                for blk in range(2):
                    mm(PSc[:, blk * 256:(blk + 1) * 256], qkT[:, 2 + blk, :], qbd[:, blk, :, :].rearrange("p a i -> p (a i)"), True, True,
                       [b_qkT, b_Pb], [b_PSc], inc=(blk == 1))

            def M2(c):
                tt(Pf[:].rearrange("p a b -> p (a b)"), PSc[:], Mret[:].rearrange("p a b -> p (a b)"), ALU.mult, [b_PSc, b_Mret], [b_Pf])
                for blk in range(2):
                    st, sb, _ = Cst[blk * 2 + 0]
                    tt(C16[:, blk, 0:128], st[:, 0:128], cI(C_BD2), ALU.mult, [sb, b_cst], [b_C16])
                for blk in range(2):
                    bs = slice(blk * 128, (blk + 1) * 128)
                    mm(PO2[:, bs], qhT[:, 0, blk, :], C16[:, blk, 0:128], True, False, [b_qhT, b_C16], [b_PO2], inc=False)
                    mm(PO2[:, bs], qhT[:, 1, blk, :], sb16[:, c, bs], False, False, [b_qhT, b_Sb16], [b_PO2], inc=False)
                    for hh in range(2):
                        h = blk * 2 + hh
                        oc = slice(h * 64, (h + 1) * 64)
                        mm(PO2[:, oc], Pf[:, h, :], vtok[:, c, oc], False, hh == 1, [b_Pf, b_vtok[c]], [b_PO2], inc=(blk == 1 and hh == 1))
                for blk in range(2):
                    st, sb, _ = Cst[blk * 2 + 0]
                    bs = slice(blk * 128, (blk + 1) * 128)
                    mm(PU[:, 0:128], ktok[:, c, bs], vhf[:, c, bs], True, True, [b_ktok[c], b_vtf[c]], [b_PU])
                    stt(st[:, 0:128], st[:, 0:128], gblk[:, 0, blk:blk + 1], PU[:, 0:128], ALU.mult, ALU.add, [sb, b_gblk, b_PU], [sb])

            def B1(c):
                act(ya[:], PO2[:, 0:256], AF.Identity, [b_PO2], [b_ya])

            def B2(c):
                finish_y(l, c, c % 2, ya[:], 4, 64, 4, 256)

            def Yt(c):
                y_transpose(l, c, 4)

            def seq_begin(si):
                for blk in range(2):
                    st_init(blk, 0)

            def seq_end(si):
                if half == 0:
                    for blk in range(2):
                        st_out(si, blk, 0)

            run_pass2(half, Fp, Feq, Fer, M1, M2, B1, B2, Yt, seq_begin, seq_end)

        def emit_C(l, half, wKV, wQZ, after_p1, after_gates, per_tile=lambda: None):
            (wkv, bkv), (wqz, bqz) = wKV, wQZ
            sb16 = Sb16[:].rearrange("p t j n -> p t (j n)")
            for t in range(8):
                toks = slice(t * 128, (t + 1) * 128)
                for kc in range(8):
                    mm(PB[0:32, 0:128], wkv[:, kc, 400:432], hT[:, kc, toks], kc == 0, kc == 7, [bkv, hb_of(t)], [b_PB], inc=(kc == 7))
                act(clrT[0:32, :], PB[0:32, 0:128], AF.Identity, [b_PB], [b_clrT])
                mm(PB[:, 256:512], clrT[:], w2x[:], True, True, [b_clrT, b_w2x], [b_PB])
                act(L1c, PB[:, 256:512], AF.Exp, [b_PB], [b_L1c], scale=-1.0)
                act(L1c, L1c, AF.Ln, [b_L1c], [b_L1c], bias=1.0)
                proj(PA[:, 0:400], b_PA, t, wkv, bkv, 0, 400, hb_of(t))
                mm(PB[:, 0:128], cI(C_UINC), L1c[:, 0:128], True, True, [b_cst, b_L1c], [b_PB], inc=False)
                mm(PB[:, 128:256], cI(C_LINC), L1c[:, 128:256], True, True, [b_cst, b_L1c], [b_PB])
                for d in range(2):
                    mm(PM[:, 200 + d:201 + d], L1c[:, d * 128:(d + 1) * 128], cst[:, C_ONES:C_ONES + 1], True, True, [b_L1c, b_cst], [b_PM], inc=(d == 1))
                act(Gc[:, t, :], PM[:, 200:202], AF.Exp, [b_PM], [b_Gc], scale=-1.0 / 16)
                act(EBn[:], PB[:, 0:256], AF.Exp, [b_PB], [b_EBn], scale=1.0 / 16)
                for d in range(2):
                    act(EBd[d][0][:, t, :], PB[:, d * 128:(d + 1) * 128], AF.Exp, [b_PB], [EBd[d][1]], scale=-1.0 / 16)
                act(vtok[:, t, :], PA[:, 128:384], AF.Identity, [b_PA], [b_vtok[t]])
                stt(ktok[:, t, :].rearrange("p (d n) -> p d n", d=2), EBn[:].rearrange("p (d n) -> p d n", d=2), 32.0 ** -0.5,
                    PA[:, 0:128].unsqueeze(1).to_broadcast([128, 2, 128]), ALU.mult, ALU.mult, [b_EBn, b_PA], [b_ktok])
                tt(Gs[:, t, :], PA[:, 384:400], gb_bc[:, l * 16:(l + 1) * 16], ALU.add, [b_PA, b_gb], [b_Gs])
                per_tile()
                if t == 1:
                    flush_tail()
            after_p1()
            gates_gen = emit_gates(l, half)

            def gstep(n=1):
                for _ in range(n):
                    next(gates_gen, None)
            after_gates()

            def st_init(d, slot=None):
                st, sb, ssem = Cst[d if slot is None else slot]
                DVE(lambda e: e.memset(st[:, 0:256], 0.0), [], [sb])
                if half == 1:
                    for h in range(4):
                        S.dma('sp', ssem, st[h * 32:(h + 1) * 32, h * 64:(h + 1) * 64], sgla_d[l, d, h], writes=[sb])

            def st_out(si, d, slot=None):
                st, sb, ssem = Cst[d if slot is None else slot]
                for h in range(4):
                    S.dma('sp', ssem, ogla_d[si, l, d, h], st[h * 32:(h + 1) * 32, h * 64:(h + 1) * 64], reads=[sb])

            def st_update(d, c, slot=None):
                st, sb, _ = Cst[d if slot is None else slot]
                mm(PU[:, 0:256], ktok[:, c, d * 128:(d + 1) * 128], vtok[:, c, :], True, True, [b_ktok[c], b_vtok[c]], [b_PU])
                tt(st[:, 0:256], st[:, 0:256], PU[:, 0:256], ALU.add, [sb, b_PU], [sb])
                ts(st[:, 0:256], st[:, 0:256], Gc[:, c, d:d + 1], None, ALU.mult, [sb, b_Gc], [sb])

            for si, (t0, t1) in enumerate(seqs_of(half)):
                slot = 1 + (si % 2)
                st, sb, _ = Cst[slot]
                st_init(1, slot)
                for c in range(t1 - 1, t0 - 1, -1):
                    tt(sb16[:, c, 0:256], st[:, 0:256], cI(C_BD4, 256), ALU.mult, [sb, b_cst], [b_Sb16])
                    gstep(2)
                    st_update(1, c, slot)
                    gstep(2)
                if half == 0:
                    st_out(si, 1, slot)

            qt = qtok[:].rearrange("p (d n) -> p d n", d=2)

            def Fp(c):
                proj(PA[:, 0:384], b_PA, c, wqz, bqz, 0, 384, hb_of(c))

            def Feq(c):
                for d in range(2):
                    tt(qt[:, d, :], EBd[d][0][:, c, :], PA[:, 0:128], ALU.mult, [EBd[d][1], b_PA], [b_qtok])

            def Fer(c):
                par = c % 2
                act(Esb[:, par, 256:512], PA[:, 128:384], AF.Exp, [b_PA], [b_Esb[par][1]], scale=-1.0)
                act(zsb[:, par, :], PA[:, 128:384], AF.Identity, [b_PA], [b_zsb[par]])
                sig_inplace(par, 256, 256)

            def M1(c):
                pt, ptb = nextPT()
                for d in range(2):
                    tr(pt[:, d * 128:(d + 1) * 128], qt[:, d, :], idb[:], [b_qtok, b_idb], [ptb], inc=False)
                    tr(pt[:, (2 + d) * 128:(3 + d) * 128], ktok[:, c, d * 128:(d + 1) * 128], idb[:], [b_ktok[c], b_idb], [ptb], inc=(d == 1))
                cp(qkT[:].rearrange("p a b -> p (a b)"), pt[:, 0:512], [ptb], [b_qkT])
                mc4 = cst[:, C_MC4:C_MC4 + 4].unsqueeze(2).to_broadcast([128, 4, 128])
                tt(Pf[:], qkT[:, 0, :].unsqueeze(1).to_broadcast([128, 4, 128]), mc4, ALU.mult, [b_qkT, b_cst], [b_Pf])
                tt(Pb[:], qkT[:, 1, :].unsqueeze(1).to_broadcast([128, 4, 128]), mc4, ALU.mult, [b_qkT, b_cst], [b_Pb])
                mm(PSc[:], qkT[:, 2, :], Pf[:].rearrange("p a b -> p (a b)"), True, True, [b_qkT, b_Pf], [b_PSc])
                mm(PO1[:], qkT[:, 3, :], Pb[:].rearrange("p a b -> p (a b)"), True, True, [b_qkT, b_Pb], [b_PO1])

            def M2(c):
                st, sb, _ = Cst[0]
                tt(Pf[:], PSc[:].rearrange("p (h i) -> p h i", h=4), uincb[:].unsqueeze(1).to_broadcast([128, 4, 128]), ALU.mult, [b_PSc, b_uincb], [b_Pf])
                tt(Pb[:], PO1[:].rearrange("p (h i) -> p h i", h=4), lincb[:].unsqueeze(1).to_broadcast([128, 4, 128]), ALU.mult, [b_PO1, b_lincb], [b_Pb])
                tt(C16[:, 0, 0:256], st[:, 0:256], cI(C_BD4, 256), ALU.mult, [sb, b_cst], [b_C16])
                mm(PO2[:, 0:256], qkT[:, 0, :], C16[:, 0, 0:256], True, False, [b_qkT, b_C16], [b_PO2], inc=False)
                mm(PO2[:, 0:256], qkT[:, 1, :], sb16[:, c, 0:256], False, False, [b_qkT, b_Sb16], [b_PO2], inc=False)
                for h in range(4):
                    oc = slice(h * 64, (h + 1) * 64)
                    mm(PO2[:, oc], Pf[:, h, :], vtok[:, c, oc], False, False, [b_Pf, b_vtok[c]], [b_PO2], inc=False)
                    mm(PO2[:, oc], Pb[:, h, :], vtok[:, c, oc], False, h == 3, [b_Pb, b_vtok[c]], [b_PO2], inc=(h == 3))
                st_update(0, c)

            def B1(c):
                act(ya[:], PO2[:, 0:256], AF.Identity, [b_PO2], [b_ya])
                gstep(2)

            def B2(c):
                finish_y(l, c, c % 2, ya[:], 4, 64, 6, 256)
                gstep(2)

            def Yt(c):
                y_transpose(l, c, 6)

            def seq_begin(si):
                st_init(0)

            def seq_end(si):
                if half == 0:
                    st_out(si, 0)

            run_pass2(half, Fp, Feq, Fer, M1, M2, B1, B2, Yt, seq_begin, seq_end)
            for _ in gates_gen:
                pass

        def emit_out(l, half, wo0, wo1, per_group=lambda: None):
            v = half
            flush_tail()
            for t in range(8):
                gt = half * 8 + t
                toks = slice(t * 128, (t + 1) * 128)
                for fb, (wt, wb_) in enumerate((wo0, wo1)):
                    ps, psb = (PA, b_PA) if fb == 0 else (PB, b_PB)
                    for kc in range(8):
                        mm(ps[:], yT[:, kc, toks], wt[:, kc, 0:512], kc == 0, kc == 7, [b_yT[t], wb_], [psb], inc=(kc == 7))
                    fs = slice(fb * 512, (fb + 1) * 512)
                    tt(Esb[:, fb, :], ps[:], gate_bc[:, v, fs], ALU.mult, [psb, b_gate], [b_Esb[fb]])
                    tt(x_sb[:, gt, fs], x_sb[:, gt, fs], Esb[:, fb, :], ALU.add, [bx[gt], b_Esb[fb]], [bx[gt]])
                    per_group()

        PHASES.clear()

        def mark(name):
            PHASES.append((name, {k: v for k, v in S.cnt.items() if not k.startswith('dma_')}))
        for l in range(depth):
            mark("L%d mod" % l)
            load_bg(l)
            if l == 0:
                for blk in range(16):
                    wt, wb_, ws = Wr[blk // 4]
                    cs = slice((blk % 4) * 128, (blk % 4 + 1) * 128)
                    S.dma('pool', ws, wt[:, :, cs], wada_d[0, :, blk * 128:(blk + 1) * 128].rearrange("(kc p) n -> p kc n", p=128), writes=[wb_])
                for blk in range(16):
                    wt, wb_, ws = Wr[blk // 4]
                    cs = slice((blk % 4) * 128, (blk % 4 + 1) * 128)
                    for kc in range(8):
                        mm(PM[:, (blk % 8) * 2:(blk % 8) * 2 + 2], wt[:, kc, cs], scT[:, kc, :], kc == 0, kc == 7, [wb_, b_scT], [b_PM], inc=(kc == 7))
                    cp(modraw[:, blk, :], PM[:, (blk % 8) * 2:(blk % 8) * 2 + 2], [b_PM], [b_modraw])
                mod_ss_finish(0)
            if STAGE < 2:
                break
            emit_ret_statics(l)
            emit_gla_statics(l)
            if STAGE < 3:
                break
            for half in range(2):
                mark("L%d h%d norm" % (l, half))
                emit_norm(l, half)
                if STAGE < 4:
                    continue
                blocks = {}
                blocks['CKV'] = load_block(l, [(CK, 128), (CV, 256), (OG, 16), (CL, 32)], win_d)
                blocks['CQZ'] = load_block(l, [(CQ, 128), (CZ, 256)], win_d)
                blocks['KV0'] = load_block(l, [(OK_, 256), (OV, 256)], win_d)
                blocks['QO0'] = load_block(l, [(OQ, 256), (OO, 256)], win_d)

                def c_after_p1():
                    blocks['Z0'] = load_block(l, [(OZ, 256)], win_d)

                def c_after_gates():
                    pass
                mark("L%d h%d C" % (l, half))
                gate_step = mod_stream(l, range(16, 24)) if half == 0 else (lambda: None)
                emit_C(l, half, blocks['CKV'], blocks['CQZ'], c_after_p1, c_after_gates, gate_step)
                blocks['KV1'] = load_block(l, [(OK_ + 256, 256), (OV + 256, 256)], win_d)

                def a0_after_p1():
                    blocks['QO1'] = load_block(l, [(OQ + 256, 256), (OO + 256, 256)], win_d)
                mark("L%d h%d A0" % (l, half))
                emit_A_pair(l, half, 0, blocks['KV0'], blocks['QO0'], blocks['Z0'], a0_after_p1)
                blocks['Z1'] = load_block(l, [(OZ + 256, 256)], win_d)
                blocks['BKV'] = load_block(l, [(BK, 256), (BV, 256)], win_d)

                def a1_after_p1():
                    blocks['BQZ'] = load_block(l, [(BQ, 256), (BZ, 256)], win_d)
                mark("L%d h%d A1" % (l, half))
                emit_A_pair(l, half, 1, blocks['KV1'], blocks['QO1'], blocks['Z1'], a1_after_p1)
                blocks['WO0'] = load_block(l, [(0, 512)], wout_d)
                blocks['WO1'] = load_block(l, [(512, 512)], wout_d)
                mark("L%d h%d B" % (l, half))
                emit_B(l, half, blocks['BKV'], blocks['BQZ'])
                mark("L%d h%d out" % (l, half))
                if half == 0 and l + 1 < depth:
                    ss_step = mod_stream(l + 1, range(16))
                    emit_out(l, half, blocks['WO0'], blocks['WO1'], ss_step)
                    mod_ss_finish(l + 1)
                else:
                    emit_out(l, half, blocks['WO0'], blocks['WO1'])

        mark("final")
        for gt in range(16):
            act(hT[:, gt % 8, :], x_sb[:, gt, :], AF.Square, [bx[gt]], [b_hT[0], b_hT[1], b_ss], accum=ss[:, gt:gt + 1])
        act(rstd[:], ss[:], AF.Ln, [b_ss], [b_rstd], scale=1.0 / D, bias=EPS)
        act(rstd[:], rstd[:], AF.Exp, [b_rstd], [b_rstd], scale=-0.5)
        S.dma('sp', d_misc, fin_bc[:], fing_d.partition_broadcast(128), writes=[b_fin])
        for gt in range(16):
            stt(x_sb[:, gt, :], x_sb[:, gt, :], rstd[:, gt:gt + 1], fin_bc[:], ALU.mult, ALU.mult, [bx[gt], b_rstd, b_fin], [bx[gt]])
            S.dma('sp', d_out, y_d[gt * 128:(gt + 1) * 128, :], x_sb[:, gt, :], reads=[bx[gt]])
        mm(PA[:, 0:128], Nst[:], idf, True, True, [b_Nst, b_cst], [b_PA])
        cp(tmpS[:, 0:128], PA[:, 0:128], [b_PA], [b_tmpS])
        S.dma('sp', d_out, on_d, tmpS[:, 0:128], reads=[b_tmpS])
        S.dma('sp', d_out, om_d, Mst[0:1, :], reads=[b_Mst])
        S.final_wait('sp')
        print("instructions", S.nins, "waits", S.nwait, {k: v for k, v in S.cnt.items() if not k.startswith('dma_')})
    return nc


_NC_CACHE = {}


def kernel(x_prompt, x_sample, state_mlstm_C, state_mlstm_n, state_mlstm_m, state_ret, state_gla, c, c_ctx,
           norm_g, w_ada, b_ada, w_in, mlstm_gate_b, ret_decay_logit, gla_w2, gla_b2, headnorm_g, w_out, final_g,
           _depth=DEPTH):
    f = lambda a: np.ascontiguousarray(np.asarray(a, dtype=np.float32))
    x_prompt, x_sample = f(x_prompt), f(x_sample)
    nc = build(_depth)
    cstv = make_consts()
    shared = {
        "norm_g": f(norm_g), "w_ada": f(w_ada), "b_ada": f(b_ada), "w_in": f(w_in),
        "gate_b": f(mlstm_gate_b).reshape(-1), "ret_logit": f(ret_decay_logit).reshape(-1),
        "gla_w2": f(gla_w2), "gla_b2": f(gla_b2).reshape(DEPTH, 256), "hn_g": f(headnorm_g),
        "w_out": f(w_out), "final_g": f(final_g), "cst": cstv,
    }
    in_maps = []
    for i in range(NCORES):
        xx = np.concatenate([x_prompt[4 * i:4 * i + 4].reshape(1024, D), x_sample[i]], axis=0)
        m = dict(shared)
        m.update({
            "x": np.ascontiguousarray(xx),
            "cvec": np.ascontiguousarray(np.stack([f(c_ctx), f(c)[i]], axis=0)),
            "sC": f(state_mlstm_C)[i], "sn": f(state_mlstm_n)[i], "sm": f(state_mlstm_m)[i].reshape(-1),
            "sret": f(state_ret)[i], "sgla": f(state_gla)[i],
        })
        in_maps.append(m)
    res = run_bass_kernel_spmd(nc, in_maps, core_ids=list(range(NCORES)))
    R = res.results
    y_prompt = np.stack([R[i]["y"][0:1024].reshape(4, 256, D) for i in range(NCORES)], 0).reshape(32, 256, D)
    y_sample = np.stack([R[i]["y"][1024:2048] for i in range(NCORES)], 0)
    oC = np.concatenate([R[i]["oC"] for i in range(NCORES)], 0)
    on = np.concatenate([R[i]["on"].reshape(4, DEPTH, 2, 4, 128) for i in range(NCORES)], 0)
    om = np.concatenate([R[i]["om"].reshape(4, DEPTH, 2, 4) for i in range(NCORES)], 0)
    oret = np.concatenate([R[i]["oret"] for i in range(NCORES)], 0)
    ogla = np.concatenate([R[i]["ogla"] for i in range(NCORES)], 0)
    return (y_prompt.astype(np.float32), y_sample.astype(np.float32), oC.astype(np.float32), on.astype(np.float32),
            om.astype(np.float32), oret.astype(np.float32), ogla.astype(np.float32))
```

```python
import numpy as np
from contextlib import ExitStack
import concourse.bass as bass
import concourse.mybir as mybir
from concourse.bass_utils import run_bass_kernel_spmd

F32 = mybir.dt.float32
BF16 = mybir.dt.bfloat16
AF = mybir.ActivationFunctionType
ALU = mybir.AluOpType
AX = mybir.AxisListType

DEPTH = 4
D = 1024
DIN = 4400
EPS = 1e-6
NCORES = 8
OQ, OK_, OV, OO, OZ, OG = 0, 512, 1024, 1536, 2048, 2560
BQ, BK, BV, BZ = 2576, 2832, 3088, 3344
CQ, CK, CV, CZ, CL = 3600, 3728, 3856, 4112, 4368

C_IDF, C_UINC, C_LINC, C_RIJ, C_RJI, C_POSF, C_POSB, C_ONES = [i * 128 for i in range(8)]
C_COLA, C_COLB = 1024, 1025
C_COS, C_SIN = 1026, 1026 + 256
C_BD2, C_BD4, C_MC2, C_MC4 = 1538, 1538 + 128, 1538 + 384, 1538 + 386
NCST = 1538 + 390


def make_consts():
    c = np.zeros((128, NCST), np.float32)
    p = np.arange(128)[:, None].astype(np.float32)
    f = np.arange(128)[None, :].astype(np.float32)
    c[:, C_IDF:C_IDF + 128] = (p == f)
    c[:, C_UINC:C_UINC + 128] = (p <= f)
    c[:, C_LINC:C_LINC + 128] = (p >= f)
    c[:, C_RIJ:C_RIJ + 128] = np.maximum(f - p, 0)
    c[:, C_RJI:C_RJI + 128] = np.maximum(p - f, 0)
    c[:, C_POSF:C_POSF + 128] = f + 1
    c[:, C_POSB:C_POSB + 128] = 128 - f
    c[:, C_ONES:C_ONES + 128] = 1.0
    c[:, C_COLA] = -(127 - p[:, 0])
    c[:, C_COLB] = -p[:, 0]
    L = 1024
    tok = np.arange(L)
    r = (tok // 64).astype(np.float32)
    col = (tok % 64).astype(np.float32)
    nf = 16
    freqs = (np.float32(10000.0) ** (-np.arange(nf, dtype=np.float32) / np.float32(nf))).astype(np.float32)
    ang = np.concatenate([r[:, None] * freqs, col[:, None] * freqs], axis=-1).astype(np.float32)
    cos = np.cos(ang).astype(np.float32).reshape(8, 128, 32).transpose(1, 0, 2).reshape(128, 256)
    sin = np.sin(ang).astype(np.float32).reshape(8, 128, 32).transpose(1, 0, 2).reshape(128, 256)
    pp = np.arange(128)[:, None]
    c[:, C_BD2:C_BD2 + 128] = (pp // 64 == (np.arange(128)[None, :] // 64))
    c[:, C_BD4:C_BD4 + 256] = (pp // 32 == (np.arange(256)[None, :] // 64))
    c[:, C_MC2:C_MC2 + 2] = (pp // 64 == np.arange(2)[None, :])
    c[:, C_MC4:C_MC4 + 4] = (pp // 32 == np.arange(4)[None, :])
    c[:, C_COS:C_COS + 256] = cos
    c[:, C_SIN:C_SIN + 256] = sin
    return c


import os as _os
ELIDE = int(_os.environ.get("MK_ELIDE", "5"))


class Buf:
    __slots__ = ("name", "w", "r", "excl")

    def __init__(self, name, excl=False):
        self.name = name
        self.w = None
        self.r = []
        self.excl = excl


class Sync:
    def __init__(self, nc, ctx):
        self.nc = nc
        self.eng = {'pe': nc.tensor, 'dve': nc.vector, 'act': nc.scalar, 'pool': nc.gpsimd, 'sp': nc.sync}
        self.sem, self.cnt, self.clk, self.hist, self.closed = {}, {}, {}, {}, {}
        for k in self.eng:
            self.sem[k] = ctx.enter_context(nc.semaphore("s_" + k))
            self.cnt[k] = 0
            self.clk[k] = {}
        self.ctx = ctx
        self.nwait = 0
        self.nins = 0

    def dma_sem(self, name):
        k = 'dma_' + name
        self.sem[k] = self.ctx.enter_context(self.nc.semaphore("s_" + k))
        self.cnt[k] = 0
        self.closed[k] = False
        return k

    def _deps(self, e, reads, writes):
        deps = {}
        if ELIDE in (3, 4, 5, 6):
            return self._deps2(e, reads, writes)
        if ELIDE == 0:
            e = '?'
        elif ELIDE == 1 and e != 'pe':
            e = '?'
        for b in reads:
            if b.w is not None:
                s, v = b.w
                if not (s == e and e == 'pe') and deps.get(s, 0) < v:
                    deps[s] = v
            if b.excl:
                for (s, v) in b.r:
                    if s != e and deps.get(s, 0) < v:
                        deps[s] = v
        for b in writes:
            if b.w is not None:
                s, v = b.w
                if s != e and deps.get(s, 0) < v:
                    deps[s] = v
            for (s, v) in b.r:
                if s != e and deps.get(s, 0) < v:
                    deps[s] = v
        return deps

    def _deps2(self, e, reads, writes):
        deps = {}
        cur = self.cnt.get(e, 0)

        def add(s, v, kind):
            if s == e:
                if v > cur:
                    assert e == 'pe', (e, s, v, cur)
                    return
                if e == 'pe' and ELIDE in (5, 6):
                    return
                if e != 'pe' and ELIDE in (4, 5) and kind != 'raw':
                    return
            if deps.get(s, 0) < v:
                deps[s] = v
        for b in reads:
            if b.w is not None:
                add(b.w[0], b.w[1], 'raw')
            if b.excl:
                for (s, v) in b.r:
                    if s != e:
                        add(s, v, 'rr')
        for b in writes:
            if b.w is not None:
                add(b.w[0], b.w[1], 'waw')
            for (s, v) in b.r:
                add(s, v, 'war')
        return deps

    def _wait1(self, e, s, v):
        clk = self.clk[e]
        if s.startswith('dma_'):
            v = self.cnt[s]
            self.closed[s] = True
        if clk.get(s, 0) >= v:
            return
        assert v <= self.cnt[s], ("wait on not-yet-signalled instruction", e, s, v, self.cnt[s])
        self.eng[e].wait_ge(self.sem[s], v)
        self.nwait += 1
        h = self.hist.get((s, v))
        if h:
            for a, b in h.items():
                if clk.get(a, 0) < b:
                    clk[a] = b
        clk[s] = v

    def _wait(self, e, deps):
        for s, v in deps.items():
            self._wait1(e, s, v)

    @staticmethod
    def _flat(xs):
        out = []
        for x in xs:
            if isinstance(x, (list, tuple)):
                out.extend(Sync._flat(x))
            else:
                out.append(x)
        return out

    def op(self, e, fn, reads=(), writes=(), inc=True):
        reads, writes = self._flat(reads), self._flat(writes)
        self._wait(e, self._deps(e, reads, writes))
        ins = fn(self.eng[e])
        self.nins += 1
        if not inc:
            v = self.cnt[e] + 1
            for b in reads:
                b.r.append((e, v))
            for b in writes:
                b.w = (e, v)
                b.r = []
            return ins
        self.cnt[e] += 1
        v = self.cnt[e]
        ins.then_inc(self.sem[e], 1)
        snap = dict(self.clk[e])
        snap[e] = v
        self.hist[(e, v)] = snap
        for b in reads:
            b.r.append((e, v))
        for b in writes:
            b.w = (e, v)
            b.r = []
        return ins

    def dma(self, e, dsem, out, in_, reads=(), writes=(), slow=False):
        reads, writes = self._flat(reads), self._flat(writes)
        self._wait(e, self._deps(e, reads, writes))
        if self.closed[dsem] and self.clk[e].get(dsem, 0) < self.cnt[dsem]:
            self._wait1(e, dsem, self.cnt[dsem])
        self.closed[dsem] = False
        if slow:
            ins = self.eng[e].dma_start(out=out, in_=in_, allow_slow_non_contiguous=True)
        else:
            ins = self.eng[e].dma_start(out=out, in_=in_)
        self.cnt[dsem] += 16
        v = self.cnt[dsem]
        ins.then_inc(self.sem[dsem], 16)
        self.nins += 1
        self.hist[(dsem, v)] = dict(self.clk[e])
        for b in reads:
            b.r.append((dsem, v))
        for b in writes:
            b.w = (dsem, v)
            b.r = []
        return ins

    def final_wait(self, e):
        for s in self.sem:
            if s.startswith('dma_') and self.cnt[s] > 0 and self.clk[e].get(s, 0) < self.cnt[s]:
                self.eng[e].wait_ge(self.sem[s], self.cnt[s])


import os
STAGE = int(os.environ.get("MK_STAGE", "99"))
SUB = int(os.environ.get("MK_SUB", "99"))
MKX = int(os.environ.get("MK_X", "0"))
ELIDE = int(os.environ.get("MK_ELIDE", "5"))


PHASES = []


def build(depth=DEPTH):
    nc = bass.Bass("TRN2", target_bir_lowering=False)
    din = lambda n, s: nc.dram_tensor(n, list(s), F32, kind="ExternalInput").ap()
    dout = lambda n, s: nc.dram_tensor(n, list(s), F32, kind="ExternalOutput").ap()
    x_d = din("x", (2048, D))
    cvec_d = din("cvec", (2, D))
    sC_d = din("sC", (DEPTH, 2, 4, 128, 128))
    sn_d = din("sn", (DEPTH, 2, 4, 128))
    sm_d = din("sm", (DEPTH * 8,))
    sret_d = din("sret", (DEPTH, 2, 4, 64, 64))
    sgla_d = din("sgla", (DEPTH, 2, 4, 32, 64))
    normg_d = din("norm_g", (DEPTH, D))
    wada_d = din("w_ada", (DEPTH, D, 3 * D))
    bada_d = din("b_ada", (DEPTH, 3 * D))
    win_d = din("w_in", (DEPTH, D, DIN))
    gateb_d = din("gate_b", (DEPTH * 16,))
    retl_d = din("ret_logit", (DEPTH * 8,))
    w2_d = din("gla_w2", (DEPTH, 2, 16, 128))
    b2_d = din("gla_b2", (DEPTH, 256))
    hng_d = din("hn_g", (DEPTH, D))
    wout_d = din("w_out", (DEPTH, D, D))
    fing_d = din("final_g", (D,))
    cst_d = din("cst", (128, NCST))
    y_d = dout("y", (2048, D))
    oC_d = dout("oC", (4, DEPTH, 2, 4, 128, 128))
    on_d = dout("on", (128, 128))
    om_d = dout("om", (1, 128))
    oret_d = dout("oret", (4, DEPTH, 2, 4, 64, 64))
    ogla_d = dout("ogla", (4, DEPTH, 2, 4, 32, 64))

    with ExitStack() as ctx:
        S = Sync(nc, ctx)

        def T(name, shape, dt=F32):
            return ctx.enter_context(nc.sbuf_tensor("sb_" + name, list(shape), dt)), Buf(name)

        def PS(name, shape, dt=F32):
            return ctx.enter_context(nc.psum_tensor("ps_" + name, list(shape), dt))

        x_sb, _ = T("x_sb", (128, 16, D))
        bx = [Buf("x%d" % i) for i in range(16)]
        hT, _ = T("hT", (128, 8, 1024), BF16)
        b_hT = [Buf("hT0"), Buf("hT1")]
        yT, _ = T("yT", (128, 8, 1024), BF16)
        b_yT = [Buf("yT%d" % i) for i in range(8)]
        NSLOT = 4
        Wr = []
        for s in range(NSLOT):
            t, b = T("wr%d" % s, (128, 8, 528), BF16)
            Wr.append((t, b, S.dma_sem("wr%d" % s)))
        ada = []
        for s in range(2):
            t, b = T("ada%d" % s, (128, 8, 128), BF16)
            ada.append((t, b, S.dma_sem("ada%d" % s)))
        cst, b_cst = T("cst", (128, NCST))
        idb, b_idb = T("idb", (128, 128), BF16)
        uincb, b_uincb = T("uincb", (128, 128), BF16)
        lincb, b_lincb = T("lincb", (128, 128), BF16)
        gate_bc, b_gate = T("gate_bc", (128, 2, D))
        bg_bc, b_bg = T("bg_bc", (128, D))
        ngT, b_ngT = T("ngT", (128, DEPTH, 8))
        hngT, b_hngT = T("hngT", (128, DEPTH, 8))
        bshT, b_bshT = T("bshT", (128, DEPTH, 8))
        bscT, b_bscT = T("bscT", (128, DEPTH, 8))
        cT, b_cT = T("cT", (128, 8, 2))
        scT, b_scT = T("scT", (128, 8, 2), BF16)
        screp, b_screp = T("screp", (128, 2, 8, 128), BF16)
        gsT2, _ = T("gsT", (128, 2, 8, 2))
        shT2, _ = T("shT", (128, 2, 8, 2))
        b_gsTp = [Buf("gsT0"), Buf("gsT1")]
        b_shTp = [Buf("shT0"), Buf("shT1")]
        modraw, b_modraw = T("modraw", (128, 16, 2))
        gb_bc, b_gb = T("gb_bc", (128, DEPTH * 16))
        m0_bc, b_m0 = T("m0_bc", (128, DEPTH * 8))
        rl_bc, b_rl = T("rl_bc", (128, DEPTH * 8))
        ss, b_ss = T("ss", (128, 16))
        rstd, b_rstd = T("rstd", (128, 16))
        Gs, b_Gs = T("Gs", (128, 8, 16))
        L1g, b_L1g = T("L1g", (128, 2, 8, 4))
        Fp, b_Fp = T("Fp", (128, 8, 16))
        ug, b_ug = T("ug", (128, 8, 8))
        umax, b_umax = T("umax", (64, 1))
        Abc, b_Abc = T("Abc", (128, 8, 8))
        MP, b_MP = T("MP", (128, 8, 8))
        Sg, b_Sg = T("Sg", (128, 8, 8))
        Mfin, b_Mfin = T("Mfin", (128, 8))
        rg, b_rg = T("rg", (128, 8, 8))
        wg, b_wg = T("wg", (128, 8, 8))
        flo, b_flo = T("flo", (128, 8, 8))
        tmpg, b_tmpg = T("tmpg", (128, 8, 8))
        Mst, b_Mst = T("Mst", (128, 128))
        Nst, b_Nst = T("Nst", (128, 128))
        ktok, _ = T("ktok", (128, 8, 256), BF16)
        b_ktok = [Buf("ktok%d" % i) for i in range(8)]
        vt_f, _ = T("vt_f", (128, 8, 2, 130), BF16)
        b_vtf = [Buf("vtf%d" % i) for i in range(8)]
        b_vtb = [Buf("vtb%d" % i) for i in range(8)]
        b_vtok = [Buf("vtok%d" % i) for i in range(8)]
        vt_b, _ = T("vt_b", (128, 8, 2, 130), BF16)
        vtok, _ = T("vtok", (128, 8, 256), BF16)
        Sb16, b_Sb16 = T("Sb16", (128, 8, 2, 130), BF16)
        Cst = []
        for i in range(4):
            t, b = T("Cst%d" % i, (128, 260))
            Cst.append((t, b, S.dma_sem("cst%d" % i)))
        C16, b_C16 = T("C16", (128, 2, 260), BF16)
        qtok, b_qtok = T("qtok", (128, 256), BF16)
        Esb, _ = T("Esb", (128, 2, 512))
        b_Esb = [[Buf("Esb0lo"), Buf("Esb0hi")], [Buf("Esb1lo"), Buf("Esb1hi")]]
        zsb, _ = T("zsb", (128, 2, 256))
        b_zsb = [Buf("zsb0"), Buf("zsb1")]
        qkT, b_qkT = T("qkT", (128, 4, 128), BF16)
        qhT, b_qhT = T("qhT", (128, 2, 2, 128), BF16)
        Pf, b_Pf = T("Pf", (128, 4, 128), BF16)
        Pb, b_Pb = T("Pb", (128, 4, 128), BF16)
        dn, _ = T("dn", (128, 8))
        b_dn = [Buf("dn_f"), Buf("dn_b")]
        ya, _ = T("ya", (128, 256))
        b_ya = [Buf("ya0"), Buf("ya1")]
        ssy, _ = T("ssy", (128, 8))
        b_ssy = [Buf("ssy%d" % i) for i in range(8)]
        ybf, b_ybf = T("ybf", (128, 256), BF16)
        rot1, b_rot1 = T("rot1", (128, 4, 32))
        rot2, b_rot2 = T("rot2", (128, 4, 32))
        L1r, b_L1r = T("L1r", (128, 8))
        nL1r, b_nL1r = T("nL1r", (128, 8))
        wend, b_wend = T("wend", (128, 2, 4))
        gC, b_gC = T("gC", (128, 8))
        gblk, b_gblk = T("gblk", (128, 2, 2))
        nLblk, b_nLblk = T("nLblk", (128, 2, 2))
        Wq, b_Wq = T("Wq", (128, 2, 2, 128))
        Mret, b_Mret = T("Mret", (128, 4, 128))
        w2s, b_w2s = T("w2s", (33, 256))
        w2x, b_w2x = T("w2x", (33, 256), BF16)
        clrT, b_clrT = T("clrT", (33, 128), BF16)
        Gc, b_Gc = T("Gc", (128, 8, 2))
        xn_parts = [(ktok[:].rearrange("p t c -> p (t c)").rearrange("p (a f) -> p a f", a=2), b_ktok),
                    (vtok[:].rearrange("p t c -> p (t c)").rearrange("p (a f) -> p a f", a=2), b_vtok)]
        EBd = [(vt_f[:].rearrange("p t j n -> p (t j n)").bitcast(F32)[:, 0:1024].rearrange("p (t n) -> p t n", t=8), b_vtf),
               (vt_b[:].rearrange("p t j n -> p (t j n)").bitcast(F32)[:, 0:1024].rearrange("p (t n) -> p t n", t=8), b_vtb)]
        junk, b_junk = ybf, b_ybf
        udiag, b_udiag = ya[0:64, 0:64], b_ya[0]
        tmpS, b_tmpS = zsb[:, 0, :], b_zsb[0]
        tm1, b_tm1 = ya[:, 0:128], b_ya[0]
        tm2, b_tm2 = rot1[:].rearrange("p a b -> p (a b)"), b_rot1
        EBn, b_EBn = ya, b_ya
        L1c, b_L1c = Esb[:, 0, 0:256], b_Esb[0][0]
        fin_bc, b_fin = bg_bc, b_bg
        d_cst = S.dma_sem("cst")
        d_x = S.dma_sem("x")
        d_misc = S.dma_sem("misc")
        d_out = S.dma_sem("out")
        d_st = S.dma_sem("stout")

        PA = PS("PA", (128, 512)); b_PA = Buf("PA", True)
        PB = PS("PB", (128, 512)); b_PB = Buf("PB", True)
        PT0 = PS("PT0", (128, 1024), BF16); b_PT0 = Buf("PT0", True)
        PT1 = PS("PT1", (128, 1024), BF16); b_PT1 = Buf("PT1", True)
        PSc = PS("PSc", (128, 512)); b_PSc = Buf("PSc", True)
        PO1 = PS("PO1", (128, 512)); b_PO1 = Buf("PO1", True)
        PO2 = PS("PO2", (128, 512)); b_PO2 = Buf("PO2", True)
        PX = PS("PX", (128, 512)); b_PU = Buf("PX", True); b_PM = b_PU
        PU = PX[:, 0:260]
        PM = PX[:, 260:512]
        PTs = [(PT0, b_PT0), (PT1, b_PT1)]
        pt_i = [0]

        def nextPT():
            pt_i[0] ^= 1
            return PTs[pt_i[0]]

        ctx.enter_context(nc.Block())

        def ACT(fn, R, W): return S.op('act', fn, R, W)
        def DVE(fn, R, W): return S.op('dve', fn, R, W)
        def PE(fn, R, W, inc=True): return S.op('pe', fn, R, W, inc)

        def mm(out, lhsT, rhs, start, stop, R, W, tp=None, inc=True):
            if tp is not None:
                return PE(lambda e: e.matmul(out, lhsT=lhsT, rhs=rhs, start=start, stop=stop, tile_position=tp), R, W, inc)
            return PE(lambda e: e.matmul(out, lhsT=lhsT, rhs=rhs, start=start, stop=stop), R, W, inc)

        def tr(out, in_, ident, R, W, inc=True):
            return PE(lambda e: e.transpose(out, in_, ident), R, W, inc)

        def act(out, in_, func, R, W, scale=1.0, bias=0.0, accum=None):
            if accum is None:
                return ACT(lambda e: e.activation(out=out, in_=in_, func=func, scale=scale, bias=bias), R, W)
            return ACT(lambda e: e.activation(out=out, in_=in_, func=func, scale=scale, bias=bias, accum_out=accum), R, W)

        def tt(out, in0, in1, op, R, W):
            return DVE(lambda e: e.tensor_tensor(out=out, in0=in0, in1=in1, op=op), R, W)

        def ts(out, in0, s1, s2, op0, R, W, op1=None):
            if op1 is None:
                return DVE(lambda e: e.tensor_scalar(out, in0, s1, None, op0=op0), R, W)
            return DVE(lambda e: e.tensor_scalar(out, in0, s1, s2, op0=op0, op1=op1), R, W)

        def stt(out, in0, scalar, in1, op0, op1, R, W):
            return DVE(lambda e: e.scalar_tensor_tensor(out=out, in0=in0, scalar=scalar, in1=in1, op0=op0, op1=op1), R, W)

        def cp(out, in_, R, W):
            return DVE(lambda e: e.tensor_copy(out, in_), R, W)

        def rsq(out, in_, n, R, W, tmp):
            act(tmp, in_, AF.Ln, R, [W[0]], scale=1.0 / n, bias=EPS)
            act(out, tmp, AF.Exp, [W[0]], W, scale=-0.5)

        cI = lambda c0, n=128: cst[:, c0:c0 + n]
        idf = cI(C_IDF)
        ones = cI(C_ONES)

        S.dma('sp', d_cst, cst[:], cst_d, writes=[b_cst])
        for g in range(4):
            S.dma('sp', d_x, x_sb[:, g * 4:(g + 1) * 4, :],
                  x_d[g * 512:(g + 1) * 512, :].rearrange("(t p) f -> p t f", p=128), writes=bx[g * 4:(g + 1) * 4])
        S.dma('sp', d_misc, gb_bc[:], gateb_d.partition_broadcast(128), writes=[b_gb])
        S.dma('sp', d_misc, m0_bc[:], sm_d.partition_broadcast(128), writes=[b_m0])
        S.dma('sp', d_misc, rl_bc[:], retl_d.partition_broadcast(128), writes=[b_rl])
        for v in range(2):
            S.dma('sp', d_misc, cT[:, :, v], cvec_d[v].rearrange("(kc p) -> p kc", p=128), writes=[b_cT], slow=True)
        for (t_, b_, src, off) in ((ngT, b_ngT, normg_d, 0), (hngT, b_hngT, hng_d, 0),
                                   (bshT, b_bshT, bada_d, 0), (bscT, b_bscT, bada_d, D)):
            for l in range(DEPTH):
                S.dma('sp', d_misc, t_[:, l, :], src[l, off:off + D].rearrange("(kc p) -> p kc", p=128), writes=[b_], slow=True)
        cp(idb[:], idf, [b_cst], [b_idb])
        cp(uincb[:], cI(C_UINC), [b_cst], [b_uincb])
        cp(lincb[:], cI(C_LINC), [b_cst], [b_lincb])
        DVE(lambda e: e.memset(Mst[:], 0.0), [], [b_Mst])
        DVE(lambda e: e.memset(Nst[:], 0.0), [], [b_Nst])
        DVE(lambda e: e.memset(w2s[:], 0.0), [], [b_w2s])
        DVE(lambda e: e.memset(clrT[:], 1.0), [], [b_clrT])
        for i in range(4):
            DVE(lambda e: e.memset(Cst[i][0][:], 0.0), [], [Cst[i][1]])
        act(tmpS[:, 0:16], cT[:].rearrange("p k v -> p (k v)"), AF.Exp, [b_cT], [b_tmpS], scale=-1.0)
        ts(tmpS[:, 0:16], tmpS[:, 0:16], 1.0, None, ALU.add, [b_tmpS], [b_tmpS])
        DVE(lambda e: e.reciprocal(tmpS[:, 0:16], tmpS[:, 0:16]), [b_tmpS], [b_tmpS])
        tt(scT[:].rearrange("p k v -> p (k v)"), tmpS[:, 0:16], cT[:].rearrange("p k v -> p (k v)"), ALU.mult, [b_tmpS, b_cT], [b_scT])
        for v in range(2):
            cp(screp[:, v, :, :], scT[:, :, v:v + 1].to_broadcast([128, 8, 128]), [b_scT], [b_screp])

        ring_state = {'n': 0}

        def load_block(l, pieces, wsrc):
            s = ring_state['n'] % NSLOT
            ring_state['n'] += 1
            wt, wb_, ws = Wr[s]
            c = 0
            for (c0, n) in pieces:
                S.dma('pool', ws, wt[:, :, c:c + n], wsrc[l, :, c0:c0 + n].rearrange("(kc p) n -> p kc n", p=128), writes=[wb_])
                c += n
            return wt, wb_

        def proj(ps, psb, tloc, wt, wb_, c0, n, hb):
            toks = slice(tloc * 128, (tloc + 1) * 128)
            for kc in range(8):
                mm(ps, hT[:, kc, toks], wt[:, kc, c0:c0 + n], kc == 0, kc == 7, [hb, wb_], [psb], inc=(kc == 7))

        ada_state = {'n': 0}

        ada_q = []

        def ada_prefetch(l, blk):
            sl = ada_state['n'] % 2
            ada_state['n'] += 1
            at, ab, asem = ada[sl]
            S.dma('pool', asem, at[:], wada_d[l, :, blk * 128:(blk + 1) * 128].rearrange("(kc p) n -> p kc n", p=128), writes=[ab])
            ada_q.append((l, blk, at, ab))

        def ada_consume(l, blk):
            l_, blk_, at, ab = ada_q.pop(0)
            assert (l_, blk_) == (l, blk), (l_, blk_, l, blk)
            if blk < 16:
                for kc in range(8):
                    mm(PM[:, 0:2], at[:, kc, :], scT[:, kc, :], kc == 0, kc == 7, [ab, b_scT], [b_PM], inc=(kc == 7))
                cp(modraw[:, blk, :], PM[:, 0:2], [b_PM], [b_modraw])
            else:
                cols = slice((blk - 16) * 128, (blk - 15) * 128)
                for v in range(2):
                    for kc in range(8):
                        mm(PO1[:, v * 128:(v + 1) * 128], screp[:, v, kc, :], at[:, kc, :], kc == 0, kc == 7, [ab, b_screp], [b_PO1], inc=(kc == 7))
                tt(gate_bc[:, :, cols], PO1[:, 0:256].rearrange("p (v n) -> p v n", v=2), bg_bc[:, cols].unsqueeze(1).to_broadcast([128, 2, 128]), ALU.add,
                   [b_PO1, b_bg], [b_gate])

        def mod_ss_finish(l):
            par = l % 2
            gsT, shT = gsT2[:, par], shT2[:, par]
            tt(shT, modraw[:, 0:8, :], bshT[:, l, :].unsqueeze(2).to_broadcast([128, 8, 2]), ALU.add, [b_modraw, b_bshT], [b_shTp[par]])
            tt(gsT, modraw[:, 8:16, :], bscT[:, l, :].unsqueeze(2).to_broadcast([128, 8, 2]), ALU.add, [b_modraw, b_bscT], [b_gsTp[par]])
            ts(gsT, gsT, 1.0, None, ALU.add, [b_gsTp[par]], [b_gsTp[par]])
            tt(gsT, gsT, ngT[:, l, :].unsqueeze(2).to_broadcast([128, 8, 2]), ALU.mult, [b_gsTp[par], b_ngT], [b_gsTp[par]])

        def mod_stream(l, blks):
            blks = list(blks)
            state = {'i': 0}
            for b_ in blks[0:2]:
                ada_prefetch(l, b_)

            def step():
                i = state['i']
                if i >= len(blks):
                    return
                ada_consume(l, blks[i])
                if i + 2 < len(blks):
                    ada_prefetch(l, blks[i + 2])
                state['i'] = i + 1
            return step

        def load_bg(l):
            S.dma('sp', d_misc, bg_bc[:], bada_d[l, 2 * D:3 * D].partition_broadcast(128), writes=[b_bg])

        def emit_ret_statics(l):
            lg = rl_bc[:, l * 8:(l + 1) * 8]
            act(L1r[:], lg, AF.Exp, [b_rl], [b_L1r], scale=-1.0)
            act(L1r[:], L1r[:], AF.Ln, [b_L1r], [b_L1r], bias=1.0)
            ts(nL1r[:], L1r[:], -1.0, None, ALU.mult, [b_L1r], [b_nL1r])
            act(wend[:, 0, :], L1r[:, 0:4], AF.Exp, [b_L1r, b_cst], [b_wend], scale=cst[:, C_COLA:C_COLA + 1])
            act(wend[:, 1, :], L1r[:, 4:8], AF.Exp, [b_L1r, b_cst], [b_wend], scale=cst[:, C_COLB:C_COLB + 1])
            ts(wend[:].rearrange("p d h -> p (d h)"), wend[:].rearrange("p d h -> p (d h)"), 0.125, None, ALU.mult, [b_wend], [b_wend])
            act(gC[:], L1r[:], AF.Exp, [b_L1r], [b_gC], scale=-128.0)
            for d in range(2):
                for blk in range(2):
                    for hh in range(2):
                        pr = slice(hh * 64, (hh + 1) * 64)
                        h = blk * 2 + hh
                        cp(gblk[pr, d, blk:blk + 1], gC[pr, d * 4 + h:d * 4 + h + 1], [b_gC], [b_gblk])
                        cp(nLblk[pr, d, blk:blk + 1], nL1r[pr, d * 4 + h:d * 4 + h + 1], [b_nL1r], [b_nLblk])
            for d in range(2):
                for blk in range(2):
                    act(Wq[:, d, blk, :], cI(C_POSF if d == 0 else C_POSB), AF.Exp, [b_cst, b_nLblk], [b_Wq], scale=nLblk[:, d, blk:blk + 1])
            for h in range(4):
                act(tm1, cI(C_RIJ), AF.Exp, [b_cst, b_nL1r], [b_tm1], scale=nL1r[:, h:h + 1])
                act(tm2, cI(C_RJI), AF.Exp, [b_cst, b_nL1r], [b_tm2], scale=nL1r[:, 4 + h:5 + h])
                tt(tm1, tm1, cI(C_UINC), ALU.mult, [b_tm1, b_cst], [b_tm1])
                tt(tm2, tm2, cI(C_LINC), ALU.mult, [b_tm2, b_cst], [b_tm2])
                tt(Mret[:, h, :], tm1, tm2, ALU.add, [b_tm1, b_tm2], [b_Mret])
                ts(Mret[:, h, :], Mret[:, h, :], 0.125, None, ALU.mult, [b_Mret], [b_Mret])

        def emit_gla_statics(l):
            S.dma('sp', d_misc, w2s[0:16, 0:128], w2_d[l, 0], writes=[b_w2s])
            S.dma('sp', d_misc, w2s[16:32, 128:256], w2_d[l, 1], writes=[b_w2s])
            S.dma('sp', d_misc, w2s[32:33, :], b2_d[l:l + 1, :], writes=[b_w2s])
            cp(w2x[:], w2s[:], [b_w2s], [b_w2x])

        def emit_norm(l, half):
            v = half
            gsT, shT = gsT2[:, l % 2], shT2[:, l % 2]
            b_gsT, b_shT = b_gsTp[l % 2], b_shTp[l % 2]
            for t in range(8):
                gt = half * 8 + t
                act(hT[:, t, :], x_sb[:, gt, :], AF.Square, [bx[gt]], [b_hT[0], b_hT[1], b_ss], accum=ss[:, t:t + 1])
            rsq(rstd[:, 0:8], ss[:, 0:8], float(D), [b_ss], [b_rstd], ss[:, 8:16])
            for g in range(2):
                for tl in range(4):
                    t = g * 4 + tl
                    gt = half * 8 + t
                    xp, xb_ = xn_parts[tl // 2]
                    ts(xp[:, tl % 2, :], x_sb[:, gt, :], rstd[:, t:t + 1], None, ALU.mult, [bx[gt], b_rstd], [xb_])
                for kc in range(8):
                    pt, ptb = nextPT()
                    for tl in range(4):
                        xp, xb_ = xn_parts[tl // 2]
                        tr(pt[:, tl * 128:(tl + 1) * 128], xp[:, tl % 2, kc * 128:(kc + 1) * 128], idb[:], [xb_, b_idb], [ptb], inc=(tl == 3))
                    dst = hT[:, kc, g * 512:(g + 1) * 512]
                    if kc % 2 == 0:
                        act(dst, pt[:, 0:512], AF.Identity, [ptb, b_gsT, b_shT], [b_hT[g]], scale=gsT[:, kc, v:v + 1], bias=shT[:, kc, v:v + 1])
                    else:
                        ts(dst, pt[:, 0:512], gsT[:, kc, v:v + 1], shT[:, kc, v:v + 1], ALU.mult, [ptb, b_gsT, b_shT], [b_hT[g]], op1=ALU.add)

        hb_of = lambda t: b_hT[t // 4]

        def sig_inplace(par, n, lo=0):
            bb = b_Esb[par][lo // 256:(lo + n + 255) // 256]
            act(Esb[:, par, lo:lo + n], Esb[:, par, lo:lo + n], AF.Ln, bb, bb, bias=1.0)
            act(Esb[:, par, lo:lo + n], Esb[:, par, lo:lo + n], AF.Exp, bb, bb, scale=-1.0)

        def finish_y(l, t, par, src, nh, dh, kc0, zoff):
            n = nh * dh
            for h in range(nh):
                act(junk[:, h * dh:(h + 1) * dh], src[:, h * dh:(h + 1) * dh], AF.Square, [b_ya[(h * dh) // 128]], [b_junk, b_ssy[h]], accum=ssy[:, h:h + 1])
            rsq(ssy[:, 0:nh], ssy[:, 0:nh], float(dh), b_ssy[0:nh], b_ssy[0:nh], ssy[:, 4:4 + nh])
            tt(zsb[:, par, 0:n], zsb[:, par, 0:n], Esb[:, par, zoff:zoff + n], ALU.mult, [b_zsb[par], b_Esb[par][zoff // 256]], [b_zsb[par]])
            tt(src, src, zsb[:, par, 0:n], ALU.mult, [b_ya, b_zsb[par]], [b_ya])
            tt(ybf[:, 0:n].rearrange("p (h e) -> p h e", h=nh), src.rearrange("p (h e) -> p h e", h=nh),
               ssy[:, 0:nh].unsqueeze(2).to_broadcast([128, nh, dh]), ALU.mult, [b_ya, b_ssy[0:nh]], [b_ybf])

        def y_transpose(l, t, kc0):
            pt, ptb = nextPT()
            for j in range(2):
                tr(pt[:, j * 128:(j + 1) * 128], ybf[:, j * 128:(j + 1) * 128], idb[:], [b_ybf, b_idb], [ptb], inc=(j == 1))
            for j in range(2):
                act(yT[:, kc0 + j, t * 128:(t + 1) * 128], pt[:, j * 128:(j + 1) * 128], AF.Identity, [ptb, b_hngT], [b_yT[t]],
                    scale=hngT[:, l, kc0 + j:kc0 + j + 1])

        seqs_of = lambda half: [(0, 2), (2, 4), (4, 6), (6, 8)] if half == 0 else [(0, 8)]

        pending_tail = []

        def flush_tail():
            while pending_tail:
                pending_tail.pop(0)()

        def run_pass2(half, Fp, Feq, Fer, M1, M2, B1, B2, Yt, seq_begin, seq_end):
            starts = {t0: si for si, (t0, t1) in enumerate(seqs_of(half))}
            ends = {t1 - 1: si for si, (t0, t1) in enumerate(seqs_of(half))}
            Fp(0)
            Feq(0)
            M1(0)
            Fer(0)
            Fp(1)
            for c in range(8):
                if c in starts:
                    seq_begin(starts[c])
                M2(c)
                if c in ends:
                    seq_end(ends[c])
                if c + 1 < 8:
                    Feq(c + 1)
                    M1(c + 1)
                B1(c)
                if c + 1 < 8:
                    Fer(c + 1)
                if c + 2 < 8:
                    Fp(c + 2)
                B2(c)
                if c < 7:
                    Yt(c)
                else:
                    pending_tail.append(lambda: Yt(7))

        def emit_gates(l, half):
            for d in range(2):
                act(L1g[:, d, :, :], Gs[:, :, d * 8 + 4:d * 8 + 8], AF.Exp, [b_Gs], [b_L1g], scale=-1.0)
                yield
            l1flat = L1g[:].rearrange("p d t h -> p (d t h)")
            act(l1flat, l1flat, AF.Ln, [b_L1g], [b_L1g], bias=1.0)
            yield
            mm(PM[:, 0:32], cI(C_UINC), l1flat[:, 0:32], True, True, [b_cst, b_L1g], [b_PM], inc=False)
            yield
            mm(PM[:, 32:64], cI(C_LINC), l1flat[:, 32:64], True, True, [b_cst, b_L1g], [b_PM], inc=False)
            yield
            mm(PM[:, 64:128], ones, l1flat, True, True, [b_cst, b_L1g], [b_PM])
            yield
            for d in range(2):
                cp(Fp[:, :, d * 4:(d + 1) * 4], PM[:, d * 32:(d + 1) * 32].rearrange("p (t h) -> p t h", t=8), [b_PM], [b_Fp])
                yield
                cp(Fp[:, :, 8 + d * 4:12 + d * 4], PM[:, 64 + d * 32:96 + d * 32].rearrange("p (t h) -> p t h", t=8), [b_PM], [b_Fp])
                yield
            for d in range(2):
                tt(ug[:, :, d * 4:(d + 1) * 4], Fp[:, :, d * 4:(d + 1) * 4], Gs[:, :, d * 8:d * 8 + 4], ALU.add, [b_Fp, b_Gs], [b_ug])
                yield
            PE(lambda e: e.transpose(PM[0:64, 0:128], ug[:].rearrange("p t g -> p (t g)"), idf), [b_ug, b_cst], [b_PM])
            yield
            DVE(lambda e: e.reduce_max(umax[:], PM[0:64, 0:128], axis=AX.X), [b_PM], [b_umax])
            yield
            ts(udiag, cst[0:64, C_IDF:C_IDF + 64], umax[:, 0:1], None, ALU.mult, [b_cst, b_umax], [b_udiag])
            yield
            mm(PM[:, 128:192], cst[0:64, C_ONES:C_ONES + 128], udiag, True, True, [b_cst, b_udiag], [b_PM])
            yield
            cp(Abc[:].rearrange("p t g -> p (t g)"), PM[:, 128:192], [b_PM], [b_Abc])
            yield
            if half == 0:
                v4 = lambda X: X[:].rearrange("p (s k) g -> p s k g", k=2)
                mst = Mst[:].rearrange("p (s r) -> p s r", s=4)
                for d in range(2):
                    sl = slice(d * 4, (d + 1) * 4)
                    tl = slice(8 + d * 4, 12 + d * 4)
                    k0, k1 = (0, 1) if d == 0 else (1, 0)
                    DVE(lambda e: e.memset(v4(MP)[:, :, k0, sl], 0.0), [], [b_MP])
                    yield
                    tt(v4(Sg)[:, :, k0, sl], v4(MP)[:, :, k0, sl], v4(Abc)[:, :, k0, sl], ALU.max, [b_MP, b_Abc], [b_Sg])
                    yield
                    tt(v4(MP)[:, :, k1, sl], v4(Sg)[:, :, k0, sl], v4(Fp)[:, :, k0, tl], ALU.subtract, [b_Sg, b_Fp], [b_MP])
                    yield
                    tt(v4(Sg)[:, :, k1, sl], v4(MP)[:, :, k1, sl], v4(Abc)[:, :, k1, sl], ALU.max, [b_MP, b_Abc], [b_Sg])
                    yield
                    tt(mst[:, :, l * 8 + d * 4:l * 8 + d * 4 + 4], v4(Sg)[:, :, k1, sl], v4(Fp)[:, :, k1, tl], ALU.subtract, [b_Sg, b_Fp], [b_Mst])
                    yield
            else:
                orders = [list(range(8)), list(range(7, -1, -1))]
                for d in range(2):
                    sl = slice(d * 4, (d + 1) * 4)
                    cp(MP[:, orders[d][0], sl], m0_bc[:, l * 8 + d * 4:l * 8 + d * 4 + 4], [b_m0], [b_MP])
                    yield
                for i in range(8):
                    for d in range(2):
                        sl = slice(d * 4, (d + 1) * 4)
                        tl = slice(8 + d * 4, 12 + d * 4)
                        c = orders[d][i]
                        tt(Sg[:, c, sl], MP[:, c, sl], Abc[:, c, sl], ALU.max, [b_MP, b_Abc], [b_Sg])
                        yield
                        if i + 1 < 8:
                            tt(MP[:, orders[d][i + 1], sl], Sg[:, c, sl], Fp[:, c, tl], ALU.subtract, [b_Sg, b_Fp], [b_MP])
                            yield
            tt(tmpg[:], MP[:], Sg[:], ALU.subtract, [b_MP, b_Sg], [b_tmpg])
            yield
            act(rg[:], tmpg[:], AF.Exp, [b_tmpg], [b_rg])
            yield
            tt(tmpg[:], ug[:], Sg[:], ALU.subtract, [b_ug, b_Sg], [b_tmpg])
            yield
            act(wg[:], tmpg[:], AF.Exp, [b_tmpg], [b_wg])
            yield
            tt(tmpg[:], Fp[:, :, 0:8], Sg[:], ALU.subtract, [b_Fp, b_Sg], [b_tmpg])
            yield
            act(flo[:], tmpg[:], AF.Exp, [b_tmpg], [b_flo])
            yield

        def state_init_A(l, half, j, d, h, slot=None):
            st, sb, ssem = Cst[j * 2 + (d if slot is None else slot)]
            if half == 0:
                DVE(lambda e: e.memset(st[:, 0:130], 0.0), [], [sb])
            else:
                S.dma('sp', ssem, st[:, 0:128], sC_d[l, d, h], writes=[sb])
                S.dma('sp', ssem, st[:, 128:129], sn_d[l, d, h].rearrange("(p o) -> p o", o=1), writes=[sb], slow=True)

        def state_out_A(l, si, j, d, h, slot=None):
            st, sb, ssem = Cst[j * 2 + (d if slot is None else slot)]
            S.dma('sp', ssem, oC_d[si, l, d, h], st[:, 0:128], reads=[sb])
            col = ((si * DEPTH + l) * 2 + d) * 4 + h
            cp(Nst[:, col:col + 1], st[:, 128:129], [sb], [b_Nst])

        def emit_A_pair(l, half, p, wKV, wQO, wZ, after_p1=lambda: None):
            (wkv, bkv), (wqo, bqo), (wz, bz) = wKV, wQO, wZ
            h0 = 2 * p
            ksc = 128.0 ** -0.5
            for t in range(8):
                ps, psb = (PA, b_PA) if t % 2 == 0 else (PB, b_PB)
                proj(ps[:], psb, t, wkv, bkv, 0, 512, hb_of(t))
                if t % 2 == 0:
                    act(ktok[:, t, :], ps[:, 0:256], AF.Identity, [psb], [b_ktok[t]], scale=ksc)
                    for j in range(2):
                        h = h0 + j
                        act(vt_f[:, t, j, 0:128], ps[:, 256 + j * 128:384 + j * 128], AF.Identity, [psb, b_wg], [b_vtf[t]], scale=wg[:, t, h:h + 1])
                        act(vt_b[:, t, j, 0:128], ps[:, 256 + j * 128:384 + j * 128], AF.Identity, [psb, b_wg], [b_vtb[t]], scale=wg[:, t, 4 + h:5 + h])
                else:
                    ts(ktok[:, t, :], ps[:, 0:256], ksc, None, ALU.mult, [psb], [b_ktok[t]])
                    for j in range(2):
                        h = h0 + j
                        ts(vt_f[:, t, j, 0:128], ps[:, 256 + j * 128:384 + j * 128], wg[:, t, h:h + 1], None, ALU.mult, [psb, b_wg], [b_vtf[t]])
                        ts(vt_b[:, t, j, 0:128], ps[:, 256 + j * 128:384 + j * 128], wg[:, t, 4 + h:5 + h], None, ALU.mult, [psb, b_wg], [b_vtb[t]])
                if t == 1:
                    flush_tail()
            for j in range(2):
                cp(vt_f[:, :, j, 128:129], wg[:, :, h0 + j:h0 + j + 1], [b_wg], [b_vtf])
                cp(vt_b[:, :, j, 128:129], wg[:, :, 4 + h0 + j:5 + h0 + j], [b_wg], [b_vtb])
            after_p1()
            for si, (t0, t1) in enumerate(seqs_of(half)):
                slot = 1 - (si % 2)
                for j in range(2):
                    state_init_A(l, half, j, 1, h0 + j, slot)
                cur = [slot, slot]
                for c in range(t1 - 1, t0 - 1, -1):
                    for j in range(2):
                        h = h0 + j
                        st, sb, _ = Cst[j * 2 + cur[j]]
                        if half == 1:
                            cur[j] = 1 - cur[j]
                        so, sob, _ = Cst[j * 2 + cur[j]]
                        pu, pub = (PU, b_PU) if j == 0 else (PSc, b_PSc)
                        act(Sb16[:, c, j, :], st[:, 0:130], AF.Identity, [sb, b_rg], [b_Sb16], scale=rg[:, c, 4 + h:5 + h])
                        mm(pu[:, 0:129], ktok[:, c, j * 128:(j + 1) * 128], vt_b[:, c, j, 0:129], True, True, [b_ktok[c], b_vtb[c]], [pub])
                        stt(so[:, 0:129], st[:, 0:129], rg[:, c, 4 + h:5 + h], pu[:, 0:129], ALU.mult, ALU.add, [sb, b_rg, pub], [sob])
                if half == 0:
                    for j in range(2):
                        state_out_A(l, si, j, 1, h0 + j, slot)

            def Fp(c):
                proj(PA[:], b_PA, c, wqo, bqo, 0, 512, hb_of(c))
                proj(PB[:, 0:256], b_PB, c, wz, bz, 0, 256, hb_of(c))

            def Feq(c):
                act(qtok[:], PA[:, 0:256], AF.Identity, [b_PA], [b_qtok])

            def Fer(c):
                par = c % 2
                act(Esb[:, par, 0:256], PA[:, 256:512], AF.Exp, [b_PA], [b_Esb[par][0]], scale=-1.0)
                act(Esb[:, par, 256:512], PB[:, 0:256], AF.Exp, [b_PB], [b_Esb[par][1]], scale=-1.0)
                act(zsb[:, par, :], PB[:, 0:256], AF.Identity, [b_PB], [b_zsb[par]])
                sig_inplace(par, 512)

            def M1(c):
                pt, ptb = nextPT()
                for j in range(2):
                    tr(pt[:, j * 128:(j + 1) * 128], qtok[:, j * 128:(j + 1) * 128], idb[:], [b_qtok, b_idb], [ptb], inc=False)
                    tr(pt[:, (2 + j) * 128:(3 + j) * 128], ktok[:, c, j * 128:(j + 1) * 128], idb[:], [b_ktok[c], b_idb], [ptb], inc=(j == 1))
                cp(qkT[:].rearrange("p a b -> p (a b)"), pt[:, 0:512], [ptb], [b_qkT])
                for j in range(2):
                    mm(PSc[:, j * 128:(j + 1) * 128], qkT[:, 2 + j, :], qkT[:, j, :], True, True, [b_qkT], [b_PSc], inc=(j == 1))

            def M2(c):
                psv = PSc[:, 0:256].rearrange("p (j i) -> p j i", j=2)
                tt(Pf[:, 0:2, :], psv, uincb[:].unsqueeze(1).to_broadcast([128, 2, 128]), ALU.mult, [b_PSc, b_uincb], [b_Pf])
                tt(Pb[:, 0:2, :], psv, lincb[:].unsqueeze(1).to_broadcast([128, 2, 128]), ALU.mult, [b_PSc, b_lincb], [b_Pb])
                for j in range(2):
                    h = h0 + j
                    st, sb, _ = Cst[j * 2 + 0]
                    act(C16[:, j, 0:130], st[:, 0:130], AF.Identity, [sb, b_rg], [b_C16], scale=rg[:, c, h:h + 1])
                for j in range(2):
                    mm(PO1[:, j * 256:j * 256 + 129], Pf[:, j, :], vt_f[:, c, j, 0:129], True, False, [b_Pf, b_vtf[c]], [b_PO1], inc=False)
                    mm(PO1[:, j * 256:j * 256 + 129], qkT[:, j, :], C16[:, j, 0:129], False, True, [b_qkT, b_C16], [b_PO1], inc=(j == 1))
                for j in range(2):
                    mm(PO2[:, j * 256:j * 256 + 129], Pb[:, j, :], vt_b[:, c, j, 0:129], True, False, [b_Pb, b_vtb[c]], [b_PO2], inc=False)
                    mm(PO2[:, j * 256:j * 256 + 129], qkT[:, j, :], Sb16[:, c, j, 0:129], False, True, [b_qkT, b_Sb16], [b_PO2], inc=(j == 1))
                for j in range(2):
                    st, sb, _ = Cst[j * 2 + 0]
                    mm(PU[:, 0:129], ktok[:, c, j * 128:(j + 1) * 128], vt_f[:, c, j, 0:129], True, True, [b_ktok[c], b_vtf[c]], [b_PU])
                    stt(st[:, 0:129], st[:, 0:129], rg[:, c, h0 + j:h0 + j + 1], PU[:, 0:129], ALU.mult, ALU.add, [sb, b_rg, b_PU], [sb])

            def B1(c):
                po1 = PO1[:].rearrange("p (j n) -> p j n", j=2)
                po2 = PO2[:].rearrange("p (j n) -> p j n", j=2)
                act(dn[:, 0:2].unsqueeze(2), po1[:, :, 128:129], AF.Abs, [b_PO1], [b_dn[0]])
                act(dn[:, 2:4].unsqueeze(2), po2[:, :, 128:129], AF.Abs, [b_PO2], [b_dn[1]])
                tt(dn[:, 0:2], dn[:, 0:2], flo[:, c, h0:h0 + 2], ALU.max, [b_dn[0], b_flo], [b_dn[0]])
                tt(dn[:, 2:4], dn[:, 2:4], flo[:, c, 4 + h0:6 + h0], ALU.max, [b_dn[1], b_flo], [b_dn[1]])
                DVE(lambda e: e.reciprocal(dn[:, 0:4], dn[:, 0:4]), [b_dn], [b_dn])
                tt(ya[:].rearrange("p (j e) -> p j e", j=2), po1[:, :, 0:128], dn[:, 0:2].unsqueeze(2).to_broadcast([128, 2, 128]), ALU.mult,
                   [b_PO1, b_dn], [b_ya])
                for j in range(2):
                    stt(ya[:, j * 128:(j + 1) * 128], po2[:, j, 0:128], dn[:, 2 + j:3 + j], ya[:, j * 128:(j + 1) * 128], ALU.mult, ALU.add,
                        [b_PO2, b_dn, b_ya[j]], [b_ya[j]])

            def B2(c):
                par = c % 2
                tt(ya[:], ya[:], Esb[:, par, 0:256], ALU.mult, [b_ya, b_Esb[par][0]], [b_ya])
                finish_y(l, c, par, ya[:], 2, 128, h0, 256)

            def Yt(c):
                y_transpose(l, c, h0)

            def seq_begin(si):
                for j in range(2):
                    state_init_A(l, half, j, 0, h0 + j)

            def seq_end(si):
                if half == 0:
                    for j in range(2):
                        state_out_A(l, si, j, 0, h0 + j)

            run_pass2(half, Fp, Feq, Fer, M1, M2, B1, B2, Yt, seq_begin, seq_end)

        def rotary(dst, ps, t, R, W):
            cs = cst[:, C_COS + t * 32:C_COS + (t + 1) * 32].unsqueeze(1).to_broadcast([128, 4, 32])
            sn = cst[:, C_SIN + t * 32:C_SIN + (t + 1) * 32].unsqueeze(1).to_broadcast([128, 4, 32])
            pv = ps.rearrange("p (h e) -> p h e", h=4)
            dv = dst.rearrange("p (h e) -> p h e", h=4)
            t1, t2 = pv[:, :, 0:32], pv[:, :, 32:64]
            tt(rot1[:], t1, cs, ALU.mult, R + [b_cst], [b_rot1])
            tt(rot2[:], t2, sn, ALU.mult, R + [b_cst], [b_rot2])
            tt(dv[:, :, 0:32], rot1[:], rot2[:], ALU.subtract, [b_rot1, b_rot2], W)
            tt(rot1[:], t1, sn, ALU.mult, R + [b_cst], [b_rot1])
            tt(rot2[:], t2, cs, ALU.mult, R + [b_cst], [b_rot2])
            tt(dv[:, :, 32:64], rot1[:], rot2[:], ALU.add, [b_rot1, b_rot2], W)

        def emit_B(l, half, wKV, wQZ):
            (wkv, bkv), (wqz, bqz) = wKV, wQZ
            vhf = vt_f[:].rearrange("p t j n -> p t (j n)")
            vhb = vt_b[:].rearrange("p t j n -> p t (j n)")
            sb16 = Sb16[:].rearrange("p t j n -> p t (j n)")
            for t in range(8):
                ps, psb = (PA, b_PA) if t % 2 == 0 else (PB, b_PB)
                proj(ps[:], psb, t, wkv, bkv, 0, 512, hb_of(t))
                pav = ps[:, 256:512].rearrange("p (h e) -> p h e", h=4)
                if half == 0:
                    act(ktok[:, t, :], ps[:, 0:256], AF.Identity, [psb], [b_ktok[t]])
                else:
                    rotary(ktok[:, t, :], ps[:, 0:256], t, [psb], [b_ktok[t]])
                act(vtok[:, t, :], ps[:, 256:512], AF.Identity, [psb], [b_vtok[t]])
                tt(vhf[:, t, 0:256].rearrange("p (h e) -> p h e", h=4), pav, wend[:, 0, :].unsqueeze(2).to_broadcast([128, 4, 64]), ALU.mult, [psb, b_wend], [b_vtf[t]])
                tt(vhb[:, t, 0:256].rearrange("p (h e) -> p h e", h=4), pav, wend[:, 1, :].unsqueeze(2).to_broadcast([128, 4, 64]), ALU.mult, [psb, b_wend], [b_vtb[t]])
                if t == 1:
                    flush_tail()

            if MKX == 21:
                return

            def st_init(blk, d, slot=None):
                st, sb, ssem = Cst[blk * 2 + (d if slot is None else slot)]
                if half == 0:
                    DVE(lambda e: e.memset(st[:, 0:128], 0.0), [], [sb])
                else:
                    for hh in range(2):
                        S.dma('sp', ssem, st[hh * 64:(hh + 1) * 64, hh * 64:(hh + 1) * 64], sret_d[l, d, blk * 2 + hh], writes=[sb])

            def st_out(si, blk, d, slot=None):
                st, sb, ssem = Cst[blk * 2 + (d if slot is None else slot)]
                for hh in range(2):
                    S.dma('sp', ssem, oret_d[si, l, d, blk * 2 + hh], st[hh * 64:(hh + 1) * 64, hh * 64:(hh + 1) * 64], reads=[sb])

            for si, (t0, t1) in enumerate(seqs_of(half)):
                slot = 1 - (si % 2)
                for blk in range(2):
                    st_init(blk, 1, slot)
                for c in range(t1 - 1, t0 - 1, -1):
                    for blk in range(2):
                        st, sb, _ = Cst[blk * 2 + slot]
                        bs = slice(blk * 128, (blk + 1) * 128)
                        pu, pub = (PU, b_PU) if blk == 0 else (PSc, b_PSc)
                        tt(sb16[:, c, bs], st[:, 0:128], cI(C_BD2), ALU.mult, [sb, b_cst], [b_Sb16])
                        mm(pu[:, 0:128], ktok[:, c, bs], vhb[:, c, bs], True, True, [b_ktok[c], b_vtb[c]], [pub])
                        stt(st[:, 0:128], st[:, 0:128], gblk[:, 1, blk:blk + 1], pu[:, 0:128], ALU.mult, ALU.add, [sb, b_gblk, pub], [sb])
                if half == 0:
                    for blk in range(2):
                        st_out(si, blk, 1, slot)
            if MKX == 22:
                return

            def Fp(c):
                proj(PA[:], b_PA, c, wqz, bqz, 0, 512, hb_of(c))

            def Feq(c):
                if half == 0:
                    act(qtok[:], PA[:, 0:256], AF.Identity, [b_PA], [b_qtok])
                else:
                    rotary(qtok[:], PA[:, 0:256], c, [b_PA], [b_qtok])

            def Fer(c):
                par = c % 2
                act(Esb[:, par, 256:512], PA[:, 256:512], AF.Exp, [b_PA], [b_Esb[par][1]], scale=-1.0)
                act(zsb[:, par, :], PA[:, 256:512], AF.Identity, [b_PA], [b_zsb[par]])
                sig_inplace(par, 256, 256)

            def M1(c):
                pt, ptb = nextPT()
                for blk in range(2):
                    tr(pt[:, blk * 128:(blk + 1) * 128], qtok[:, blk * 128:(blk + 1) * 128], idb[:], [b_qtok, b_idb], [ptb], inc=False)
                    tr(pt[:, (2 + blk) * 128:(3 + blk) * 128], ktok[:, c, blk * 128:(blk + 1) * 128], idb[:], [b_ktok[c], b_idb], [ptb], inc=(blk == 1))
                cp(qkT[:].rearrange("p a b -> p (a b)"), pt[:, 0:512], [ptb], [b_qkT])
                for d in range(2):
                    tt(qhT[:, d, :, :], qkT[:, 0:2, :], Wq[:, d, :, :], ALU.mult, [b_qkT, b_Wq], [b_qhT])
                qbd = Pb[:].rearrange("p a b -> p (a b)").rearrange("p (k a i) -> p k a i", k=2, a=2)
                for blk in range(2):
                    tt(qbd[:, blk, :, :], qkT[:, blk, :].unsqueeze(1).to_broadcast([128, 2, 128]),
                       cst[:, C_MC2:C_MC2 + 2].unsqueeze(2).to_broadcast([128, 2, 128]), ALU.mult, [b_qkT, b_cst], [b_Pb])
                for blk in range(2):
                    mm(PSc[:, blk * 256:(blk + 1) * 256], qkT[:, 2 + blk, :], qbd[:, blk, :, :].rearrange("p a i -> p (a i)"), True, True,
                       [b_qkT, b_Pb], [b_PSc], inc=(blk == 1))

            def M2(c):
                tt(Pf[:].rearrange("p a b -> p (a b)"), PSc[:], Mret[:].rearrange("p a b -> p (a b)"), ALU.mult, [b_PSc, b_Mret], [b_Pf])
                for blk in range(2):
                    st, sb, _ = Cst[blk * 2 + 0]
                    tt(C16[:, blk, 0:128], st[:, 0:128], cI(C_BD2), ALU.mult, [sb, b_cst], [b_C16])
                for blk in range(2):
                    bs = slice(blk * 128, (blk + 1) * 128)
                    mm(PO2[:, bs], qhT[:, 0, blk, :], C16[:, blk, 0:128], True, False, [b_qhT, b_C16], [b_PO2], inc=False)
                    mm(PO2[:, bs], qhT[:, 1, blk, :], sb16[:, c, bs], False, False, [b_qhT, b_Sb16], [b_PO2], inc=False)
                    for hh in range(2):
                        h = blk * 2 + hh
                        oc = slice(h * 64, (h + 1) * 64)
                        mm(PO2[:, oc], Pf[:, h, :], vtok[:, c, oc], False, hh == 1, [b_Pf, b_vtok[c]], [b_PO2], inc=(blk == 1 and hh == 1))
                for blk in range(2):
                    st, sb, _ = Cst[blk * 2 + 0]
                    bs = slice(blk * 128, (blk + 1) * 128)
                    mm(PU[:, 0:128], ktok[:, c, bs], vhf[:, c, bs], True, True, [b_ktok[c], b_vtf[c]], [b_PU])
                    stt(st[:, 0:128], st[:, 0:128], gblk[:, 0, blk:blk + 1], PU[:, 0:128], ALU.mult, ALU.add, [sb, b_gblk, b_PU], [sb])

            def B1(c):
                act(ya[:], PO2[:, 0:256], AF.Identity, [b_PO2], [b_ya])

            def B2(c):
                finish_y(l, c, c % 2, ya[:], 4, 64, 4, 256)

            def Yt(c):
                y_transpose(l, c, 4)

            def seq_begin(si):
                for blk in range(2):
                    st_init(blk, 0)

            def seq_end(si):
                if half == 0:
                    for blk in range(2):
                        st_out(si, blk, 0)

            run_pass2(half, Fp, Feq, Fer, M1, M2, B1, B2, Yt, seq_begin, seq_end)

        def emit_C(l, half, wKV, wQZ, after_p1, after_gates, per_tile=lambda: None):
            (wkv, bkv), (wqz, bqz) = wKV, wQZ
            sb16 = Sb16[:].rearrange("p t j n -> p t (j n)")
            for t in range(8):
                toks = slice(t * 128, (t + 1) * 128)
                for kc in range(8):
                    mm(PB[0:32, 0:128], wkv[:, kc, 400:432], hT[:, kc, toks], kc == 0, kc == 7, [bkv, hb_of(t)], [b_PB], inc=(kc == 7))
                act(clrT[0:32, :], PB[0:32, 0:128], AF.Identity, [b_PB], [b_clrT])
                mm(PB[:, 256:512], clrT[:], w2x[:], True, True, [b_clrT, b_w2x], [b_PB])
                act(L1c, PB[:, 256:512], AF.Exp, [b_PB], [b_L1c], scale=-1.0)
                act(L1c, L1c, AF.Ln, [b_L1c], [b_L1c], bias=1.0)
                proj(PA[:, 0:400], b_PA, t, wkv, bkv, 0, 400, hb_of(t))
                mm(PB[:, 0:128], cI(C_UINC), L1c[:, 0:128], True, True, [b_cst, b_L1c], [b_PB], inc=False)
                mm(PB[:, 128:256], cI(C_LINC), L1c[:, 128:256], True, True, [b_cst, b_L1c], [b_PB])
                for d in range(2):
                    mm(PM[:, 200 + d:201 + d], L1c[:, d * 128:(d + 1) * 128], cst[:, C_ONES:C_ONES + 1], True, True, [b_L1c, b_cst], [b_PM], inc=(d == 1))
                act(Gc[:, t, :], PM[:, 200:202], AF.Exp, [b_PM], [b_Gc], scale=-1.0 / 16)
                act(EBn[:], PB[:, 0:256], AF.Exp, [b_PB], [b_EBn], scale=1.0 / 16)
                for d in range(2):
                    act(EBd[d][0][:, t, :], PB[:, d * 128:(d + 1) * 128], AF.Exp, [b_PB], [EBd[d][1]], scale=-1.0 / 16)
                act(vtok[:, t, :], PA[:, 128:384], AF.Identity, [b_PA], [b_vtok[t]])
                stt(ktok[:, t, :].rearrange("p (d n) -> p d n", d=2), EBn[:].rearrange("p (d n) -> p d n", d=2), 32.0 ** -0.5,
                    PA[:, 0:128].unsqueeze(1).to_broadcast([128, 2, 128]), ALU.mult, ALU.mult, [b_EBn, b_PA], [b_ktok])
                tt(Gs[:, t, :], PA[:, 384:400], gb_bc[:, l * 16:(l + 1) * 16], ALU.add, [b_PA, b_gb], [b_Gs])
                per_tile()
                if t == 1:
                    flush_tail()
            after_p1()
            gates_gen = emit_gates(l, half)

            def gstep(n=1):
                for _ in range(n):
                    next(gates_gen, None)
            after_gates()

            def st_init(d, slot=None):
                st, sb, ssem = Cst[d if slot is None else slot]
                DVE(lambda e: e.memset(st[:, 0:256], 0.0), [], [sb])
                if half == 1:
                    for h in range(4):
                        S.dma('sp', ssem, st[h * 32:(h + 1) * 32, h * 64:(h + 1) * 64], sgla_d[l, d, h], writes=[sb])

            def st_out(si, d, slot=None):
                st, sb, ssem = Cst[d if slot is None else slot]
                for h in range(4):
                    S.dma('sp', ssem, ogla_d[si, l, d, h], st[h * 32:(h + 1) * 32, h * 64:(h + 1) * 64], reads=[sb])

            def st_update(d, c, slot=None):
                st, sb, _ = Cst[d if slot is None else slot]
                mm(PU[:, 0:256], ktok[:, c, d * 128:(d + 1) * 128], vtok[:, c, :], True, True, [b_ktok[c], b_vtok[c]], [b_PU])
                tt(st[:, 0:256], st[:, 0:256], PU[:, 0:256], ALU.add, [sb, b_PU], [sb])
                ts(st[:, 0:256], st[:, 0:256], Gc[:, c, d:d + 1], None, ALU.mult, [sb, b_Gc], [sb])

            for si, (t0, t1) in enumerate(seqs_of(half)):
                slot = 1 + (si % 2)
                st, sb, _ = Cst[slot]
                st_init(1, slot)
                for c in range(t1 - 1, t0 - 1, -1):
                    tt(sb16[:, c, 0:256], st[:, 0:256], cI(C_BD4, 256), ALU.mult, [sb, b_cst], [b_Sb16])
                    gstep(2)
                    st_update(1, c, slot)
                    gstep(2)
                if half == 0:
                    st_out(si, 1, slot)

            qt = qtok[:].rearrange("p (d n) -> p d n", d=2)

            def Fp(c):
                proj(PA[:, 0:384], b_PA, c, wqz, bqz, 0, 384, hb_of(c))

            def Feq(c):
                for d in range(2):
                    tt(qt[:, d, :], EBd[d][0][:, c, :], PA[:, 0:128], ALU.mult, [EBd[d][1], b_PA], [b_qtok])

            def Fer(c):
                par = c % 2
                act(Esb[:, par, 256:512], PA[:, 128:384], AF.Exp, [b_PA], [b_Esb[par][1]], scale=-1.0)
                act(zsb[:, par, :], PA[:, 128:384], AF.Identity, [b_PA], [b_zsb[par]])
                sig_inplace(par, 256, 256)

            def M1(c):
                pt, ptb = nextPT()
                for d in range(2):
                    tr(pt[:, d * 128:(d + 1) * 128], qt[:, d, :], idb[:], [b_qtok, b_idb], [ptb], inc=False)
                    tr(pt[:, (2 + d) * 128:(3 + d) * 128], ktok[:, c, d * 128:(d + 1) * 128], idb[:], [b_ktok[c], b_idb], [ptb], inc=(d == 1))
                cp(qkT[:].rearrange("p a b -> p (a b)"), pt[:, 0:512], [ptb], [b_qkT])
                mc4 = cst[:, C_MC4:C_MC4 + 4].unsqueeze(2).to_broadcast([128, 4, 128])
                tt(Pf[:], qkT[:, 0, :].unsqueeze(1).to_broadcast([128, 4, 128]), mc4, ALU.mult, [b_qkT, b_cst], [b_Pf])
                tt(Pb[:], qkT[:, 1, :].unsqueeze(1).to_broadcast([128, 4, 128]), mc4, ALU.mult, [b_qkT, b_cst], [b_Pb])
                mm(PSc[:], qkT[:, 2, :], Pf[:].rearrange("p a b -> p (a b)"), True, True, [b_qkT, b_Pf], [b_PSc])
                mm(PO1[:], qkT[:, 3, :], Pb[:].rearrange("p a b -> p (a b)"), True, True, [b_qkT, b_Pb], [b_PO1])

            def M2(c):
                st, sb, _ = Cst[0]
                tt(Pf[:], PSc[:].rearrange("p (h i) -> p h i", h=4), uincb[:].unsqueeze(1).to_broadcast([128, 4, 128]), ALU.mult, [b_PSc, b_uincb], [b_Pf])
                tt(Pb[:], PO1[:].rearrange("p (h i) -> p h i", h=4), lincb[:].unsqueeze(1).to_broadcast([128, 4, 128]), ALU.mult, [b_PO1, b_lincb], [b_Pb])
                tt(C16[:, 0, 0:256], st[:, 0:256], cI(C_BD4, 256), ALU.mult, [sb, b_cst], [b_C16])
                mm(PO2[:, 0:256], qkT[:, 0, :], C16[:, 0, 0:256], True, False, [b_qkT, b_C16], [b_PO2], inc=False)
                mm(PO2[:, 0:256], qkT[:, 1, :], sb16[:, c, 0:256], False, False, [b_qkT, b_Sb16], [b_PO2], inc=False)
                for h in range(4):
                    oc = slice(h * 64, (h + 1) * 64)
                    mm(PO2[:, oc], Pf[:, h, :], vtok[:, c, oc], False, False, [b_Pf, b_vtok[c]], [b_PO2], inc=False)
                    mm(PO2[:, oc], Pb[:, h, :], vtok[:, c, oc], False, h == 3, [b_Pb, b_vtok[c]], [b_PO2], inc=(h == 3))
                st_update(0, c)

            def B1(c):
                act(ya[:], PO2[:, 0:256], AF.Identity, [b_PO2], [b_ya])
                gstep(2)

            def B2(c):
                finish_y(l, c, c % 2, ya[:], 4, 64, 6, 256)
                gstep(2)

            def Yt(c):
                y_transpose(l, c, 6)

            def seq_begin(si):
                st_init(0)

            def seq_end(si):
                if half == 0:
                    st_out(si, 0)

            run_pass2(half, Fp, Feq, Fer, M1, M2, B1, B2, Yt, seq_begin, seq_end)
            for _ in gates_gen:
                pass

        def emit_out(l, half, wo0, wo1, per_group=lambda: None):
            v = half
            flush_tail()
            for t in range(8):
                gt = half * 8 + t
                toks = slice(t * 128, (t + 1) * 128)
                for fb, (wt, wb_) in enumerate((wo0, wo1)):
                    ps, psb = (PA, b_PA) if fb == 0 else (PB, b_PB)
                    for kc in range(8):
                        mm(ps[:], yT[:, kc, toks], wt[:, kc, 0:512], kc == 0, kc == 7, [b_yT[t], wb_], [psb], inc=(kc == 7))
                    fs = slice(fb * 512, (fb + 1) * 512)
                    tt(Esb[:, fb, :], ps[:], gate_bc[:, v, fs], ALU.mult, [psb, b_gate], [b_Esb[fb]])
                    tt(x_sb[:, gt, fs], x_sb[:, gt, fs], Esb[:, fb, :], ALU.add, [bx[gt], b_Esb[fb]], [bx[gt]])
                    per_group()

        fin_state = {'loaded': False, 'early': False}

        def final_part(lo, hi):
            for gt in range(lo, hi):
                act(hT[:, gt % 8, :], x_sb[:, gt, :], AF.Square, [bx[gt]], [b_hT[0], b_hT[1], b_ss], accum=ss[:, gt:gt + 1])
            act(rstd[:, lo:hi], ss[:, lo:hi], AF.Ln, [b_ss], [b_rstd], scale=1.0 / D, bias=EPS)
            act(rstd[:, lo:hi], rstd[:, lo:hi], AF.Exp, [b_rstd], [b_rstd], scale=-0.5)
            if not fin_state['loaded']:
                fin_state['loaded'] = True
                S.dma('sp', d_misc, fin_bc[:], fing_d.partition_broadcast(128), writes=[b_fin])
            for gt in range(lo, hi):
                stt(x_sb[:, gt, :], x_sb[:, gt, :], rstd[:, gt:gt + 1], fin_bc[:], ALU.mult, ALU.mult, [bx[gt], b_rstd, b_fin], [bx[gt]])
                S.dma('sp', d_out, y_d[gt * 128:(gt + 1) * 128, :], x_sb[:, gt, :], reads=[bx[gt]])

        PHASES.clear()

        def mark(name):
            PHASES.append((name, {k: v for k, v in S.cnt.items() if not k.startswith('dma_')}))
        for l in range(depth):
            mark("L%d mod" % l)
            load_bg(l)
            if l == 0:
                for blk in range(16):
                    wt, wb_, ws = Wr[blk // 4]
                    cs = slice((blk % 4) * 128, (blk % 4 + 1) * 128)
                    S.dma('pool', ws, wt[:, :, cs], wada_d[0, :, blk * 128:(blk + 1) * 128].rearrange("(kc p) n -> p kc n", p=128), writes=[wb_])
                for blk in range(16):
                    wt, wb_, ws = Wr[blk // 4]
                    cs = slice((blk % 4) * 128, (blk % 4 + 1) * 128)
                    for kc in range(8):
                        mm(PM[:, (blk % 8) * 2:(blk % 8) * 2 + 2], wt[:, kc, cs], scT[:, kc, :], kc == 0, kc == 7, [wb_, b_scT], [b_PM], inc=(kc == 7))
                    cp(modraw[:, blk, :], PM[:, (blk % 8) * 2:(blk % 8) * 2 + 2], [b_PM], [b_modraw])
                mod_ss_finish(0)
            if STAGE < 2:
                break
            emit_ret_statics(l)
            emit_gla_statics(l)
            if STAGE < 3:
                break
            for half in range(2):
                mark("L%d h%d norm" % (l, half))
                emit_norm(l, half)
                if STAGE < 4:
                    continue
                blocks = {}
                blocks['CKV'] = load_block(l, [(CK, 128), (CV, 256), (OG, 16), (CL, 32)], win_d)
                blocks['CQZ'] = load_block(l, [(CQ, 128), (CZ, 256)], win_d)
                blocks['KV0'] = load_block(l, [(OK_, 256), (OV, 256)], win_d)
                blocks['QO0'] = load_block(l, [(OQ, 256), (OO, 256)], win_d)

                def c_after_p1():
                    blocks['Z0'] = load_block(l, [(OZ, 256)], win_d)

                def c_after_gates():
                    pass
                mark("L%d h%d C" % (l, half))
                gate_step = mod_stream(l, range(16, 24)) if half == 0 else (lambda: None)
                emit_C(l, half, blocks['CKV'], blocks['CQZ'], c_after_p1, c_after_gates, gate_step)
                blocks['KV1'] = load_block(l, [(OK_ + 256, 256), (OV + 256, 256)], win_d)

                def a0_after_p1():
                    blocks['QO1'] = load_block(l, [(OQ + 256, 256), (OO + 256, 256)], win_d)
                mark("L%d h%d A0" % (l, half))
                emit_A_pair(l, half, 0, blocks['KV0'], blocks['QO0'], blocks['Z0'], a0_after_p1)
                blocks['Z1'] = load_block(l, [(OZ + 256, 256)], win_d)
                blocks['BKV'] = load_block(l, [(BK, 256), (BV, 256)], win_d)

                def a1_after_p1():
                    blocks['BQZ'] = load_block(l, [(BQ, 256), (BZ, 256)], win_d)
                mark("L%d h%d A1" % (l, half))
                emit_A_pair(l, half, 1, blocks['KV1'], blocks['QO1'], blocks['Z1'], a1_after_p1)
                blocks['WO0'] = load_block(l, [(0, 512)], wout_d)
                blocks['WO1'] = load_block(l, [(512, 512)], wout_d)
                mark("L%d h%d B" % (l, half))
                emit_B(l, half, blocks['BKV'], blocks['BQZ'])
                mark("L%d h%d out" % (l, half))
                if half == 0 and l + 1 < depth:
                    ss_step = mod_stream(l + 1, range(16))
                    emit_out(l, half, blocks['WO0'], blocks['WO1'], ss_step)
                    mod_ss_finish(l + 1)
                else:
                    emit_out(l, half, blocks['WO0'], blocks['WO1'])
                if l == depth - 1 and half == 0 and STAGE >= 99:
                    fin_state['early'] = True
                    final_part(0, 8)

        mark("final")
        if not fin_state['early']:
            final_part(0, 8)
        final_part(8, 16)
        mm(PA[:, 0:128], Nst[:], idf, True, True, [b_Nst, b_cst], [b_PA])
        cp(tmpS[:, 0:128], PA[:, 0:128], [b_PA], [b_tmpS])
        S.dma('sp', d_out, on_d, tmpS[:, 0:128], reads=[b_tmpS])
        S.dma('sp', d_out, om_d, Mst[0:1, :], reads=[b_Mst])
        S.final_wait('sp')
        print("instructions", S.nins, "waits", S.nwait, {k: v for k, v in S.cnt.items() if not k.startswith('dma_')})
    return nc


_NC_CACHE = {}


def kernel(x_prompt, x_sample, state_mlstm_C, state_mlstm_n, state_mlstm_m, state_ret, state_gla, c, c_ctx,
           norm_g, w_ada, b_ada, w_in, mlstm_gate_b, ret_decay_logit, gla_w2, gla_b2, headnorm_g, w_out, final_g,
           _depth=DEPTH):
    f = lambda a: np.ascontiguousarray(np.asarray(a, dtype=np.float32))
    x_prompt, x_sample = f(x_prompt), f(x_sample)
    nc = build(_depth)
    cstv = make_consts()
    shared = {
        "norm_g": f(norm_g), "w_ada": f(w_ada), "b_ada": f(b_ada), "w_in": f(w_in),
        "gate_b": f(mlstm_gate_b).reshape(-1), "ret_logit": f(ret_decay_logit).reshape(-1),
        "gla_w2": f(gla_w2), "gla_b2": f(gla_b2).reshape(DEPTH, 256), "hn_g": f(headnorm_g),
        "w_out": f(w_out), "final_g": f(final_g), "cst": cstv,
    }
    in_maps = []
    for i in range(NCORES):
        xx = np.concatenate([x_prompt[4 * i:4 * i + 4].reshape(1024, D), x_sample[i]], axis=0)
        m = dict(shared)
        m.update({
            "x": np.ascontiguousarray(xx),
            "cvec": np.ascontiguousarray(np.stack([f(c_ctx), f(c)[i]], axis=0)),
            "sC": f(state_mlstm_C)[i], "sn": f(state_mlstm_n)[i], "sm": f(state_mlstm_m)[i].reshape(-1),
            "sret": f(state_ret)[i], "sgla": f(state_gla)[i],
        })
        in_maps.append(m)
    res = run_bass_kernel_spmd(nc, in_maps, core_ids=list(range(NCORES)))
    R = res.results
    y_prompt = np.stack([R[i]["y"][0:1024].reshape(4, 256, D) for i in range(NCORES)], 0).reshape(32, 256, D)
    y_sample = np.stack([R[i]["y"][1024:2048] for i in range(NCORES)], 0)
    oC = np.concatenate([R[i]["oC"] for i in range(NCORES)], 0)
    on = np.concatenate([R[i]["on"].reshape(4, DEPTH, 2, 4, 128) for i in range(NCORES)], 0)
    om = np.concatenate([R[i]["om"].reshape(4, DEPTH, 2, 4) for i in range(NCORES)], 0)
    oret = np.concatenate([R[i]["oret"] for i in range(NCORES)], 0)
    ogla = np.concatenate([R[i]["ogla"] for i in range(NCORES)], 0)
    return (y_prompt.astype(np.float32), y_sample.astype(np.float32), oC.astype(np.float32), on.astype(np.float32),
            om.astype(np.float32), oret.astype(np.float32), ogla.astype(np.float32))
```
